# Optimizing a Trainium2 kernel written in Bass

```python
import math
import jax
import jax.numpy as jnp
from jax import lax
import numpy as np

D_MODEL = 2048
BATCH = 2
SEQ = 16384
DEPTH = 2

HEAD_DIM = 64
ATTN_BLOCK = 128
NORM_EPS = 1e-5

SWA_Q_HEADS = D_MODEL // 128
SWA_KV_HEADS = max(1, SWA_Q_HEADS // 8)
SWA_GROUP = SWA_Q_HEADS // SWA_KV_HEADS
SWA_WINDOW = 128
SWA_Q_W = SWA_Q_HEADS * HEAD_DIM
SWA_KV_W = SWA_KV_HEADS * HEAD_DIM

SSM_HEAD_DIM = 64
SSM_HEADS = D_MODEL // SSM_HEAD_DIM
SSM_D_INNER = SSM_HEADS * SSM_HEAD_DIM
SSM_GROUPS = 4
SSM_STATE = 128
SSM_BC_W = SSM_GROUPS * SSM_STATE
SSM_CONV = 4
SSM_CONV_DIM = SSM_D_INNER + 2 * SSM_BC_W
SSM_CHUNK = 128
DT_MIN = 1e-3
DT_MAX = 1e-1

AB_SPLITS = (SWA_Q_W,
             SWA_Q_W + SWA_KV_W,
             SWA_Q_W + 2 * SWA_KV_W,
             SWA_Q_W + 2 * SWA_KV_W + SSM_D_INNER,
             SWA_Q_W + 2 * SWA_KV_W + SSM_D_INNER + SSM_CONV_DIM)
AB_IN_W = AB_SPLITS[-1] + SSM_HEADS
AB_MIX_W = SWA_Q_W + SSM_D_INNER

DIL_HEADS = D_MODEL // HEAD_DIM
DIL_W = DIL_HEADS * HEAD_DIM
DIL_BRANCHES = ((128, 1), (512, 4), (2048, 16))
DIL_PAD = max(r for _, r in DIL_BRANCHES) * ATTN_BLOCK

D_FF = 4 * D_MODEL

N_EVEN = (DEPTH + 1) // 2
N_ODD = DEPTH // 2

kernel_name = 'hybrid_swa_ssd_dilated_block'


def rms_norm(x, gain):
    xf = x.astype(jnp.float32)
    y = xf * lax.rsqrt(jnp.mean(xf * xf, axis=-1, keepdims=True) + NORM_EPS)
    return (y * gain.astype(jnp.float32)).astype(x.dtype)


def banded_attention(q, k, v, max_dist, sink=None, return_lse=False):
    bsz, n, kh, g, hd = q.shape
    nb = n // ATTN_BLOCK
    qb = q.reshape(bsz, nb, ATTN_BLOCK, kh, g, hd)

    def with_prev(t):
        tb = t.reshape(bsz, nb, ATTN_BLOCK, kh, hd)
        prev = jnp.pad(tb, ((0, 0), (1, 0), (0, 0), (0, 0), (0, 0)))[:, :-1]
        return jnp.concatenate([prev, tb], axis=2)

    kw = with_prev(k)
    vw = with_prev(v)
    s = jnp.einsum('bnqhgd,bnkhd->bnhgqk', qb, kw,
                   preferred_element_type=jnp.float32) * (hd ** -0.5)
    qi = jnp.arange(ATTN_BLOCK)[:, None]
    kj = jnp.arange(2 * ATTN_BLOCK)[None, :]
    dist = ATTN_BLOCK + qi - kj
    in_band = (dist >= 0) & (dist <= max_dist)
    before_start = (jnp.arange(nb)[:, None, None] == 0) & (kj < ATTN_BLOCK)[None]
    valid = in_band[None] & ~before_start
    s = jnp.where(valid[None, :, None, None], s, -jnp.inf)
    m = jnp.max(s, axis=-1, keepdims=True)
    if sink is not None:
        sk = sink.astype(jnp.float32)[None, None, :, :, None, None]
        m = jnp.maximum(m, sk)
    p = jnp.exp(s - m)
    denom = jnp.sum(p, axis=-1, keepdims=True)
    if sink is not None:
        denom = denom + jnp.exp(sk - m)
    o = jnp.einsum('bnhgqk,bnkhd->bnqhgd', (p / denom).astype(v.dtype), vw)
    o = o.reshape(bsz, n, kh, g, hd)
    if not return_lse:
        return o
    lse = (m + jnp.log(denom))[..., 0]
    lse = jnp.moveaxis(lse, -1, 2).reshape(bsz, n, kh, g)
    return o, lse


def causal_depthwise_conv(x, w, b):
    c = x.shape[-1]
    y = lax.conv_general_dilated(x, w[:, None, :].astype(x.dtype), window_strides=(1,),
                                 padding=[(w.shape[0] - 1, 0)],
                                 dimension_numbers=('NWC', 'WIO', 'NWC'),
                                 feature_group_count=c)
    return y + b.astype(x.dtype)


def ssd_chunked(x, dt, a, bmat, cmat):
    bsz, s, h, p = x.shape
    g, nst = bmat.shape[2], bmat.shape[3]
    r = h // g
    nc = s // SSM_CHUNK
    f32 = jnp.float32
    xdt = (x.astype(f32) * dt[..., None]).reshape(bsz, nc, SSM_CHUNK, g, r, p)
    bc = bmat.astype(f32).reshape(bsz, nc, SSM_CHUNK, g, nst)
    cc = cmat.astype(f32).reshape(bsz, nc, SSM_CHUNK, g, nst)
    cum = jnp.cumsum((dt * a).reshape(bsz, nc, SSM_CHUNK, g, r), axis=2)
    cum = jnp.moveaxis(cum, 2, -1)
    causal = jnp.tril(jnp.ones((SSM_CHUNK, SSM_CHUNK), dtype=bool))
    seg = jnp.exp(jnp.where(causal, cum[..., :, None] - cum[..., None, :], -jnp.inf))
    cb = jnp.einsum('bctgn,bcsgn->bcgts', cc, bc)
    y_diag = jnp.einsum('bcgts,bcgrts,bcsgrp->bctgrp', cb, seg, xdt)
    states = jnp.einsum('bcsgn,bcgrs,bcsgrp->bcgrpn', bc, jnp.exp(cum[..., -1:] - cum), xdt)

    def carry_state(hst, inp):
        st, dec = inp
        return hst * dec[..., None, None] + st, hst

    h0 = jnp.zeros((bsz, g, r, p, nst), f32)
    _, h_in = lax.scan(carry_state, h0,
                       (jnp.moveaxis(states, 1, 0), jnp.moveaxis(jnp.exp(cum[..., -1]), 1, 0)))
    h_in = jnp.moveaxis(h_in, 0, 1)
    y_off = jnp.einsum('bctgn,bcgrt,bcgrpn->bctgrp', cc, jnp.exp(cum), h_in)
    return (y_diag + y_off).reshape(bsz, s, h, p)


def gated_group_rmsnorm(y, z, gain):
    gy = y.astype(jnp.float32) * jax.nn.silu(z.astype(jnp.float32))
    shp = gy.shape
    gy = gy.reshape(shp[:-1] + (SSM_GROUPS, shp[-1] // SSM_GROUPS))
    gy = gy * lax.rsqrt(jnp.mean(gy * gy, axis=-1, keepdims=True) + NORM_EPS)
    return gy.reshape(shp) * gain.astype(jnp.float32)


def swa_ssd_mixer(h, w_in, q_norm, k_norm, sinks, conv_w, conv_b, dt_bias, a_log,
                  d_skip, gate_norm, w_out):
    bsz, s, _ = h.shape
    proj = jnp.einsum('bsd,de->bse', h, w_in)
    q, k, v, z, xbc, dt_raw = jnp.split(proj, AB_SPLITS, axis=-1)
    q = rms_norm(q.reshape(bsz, s, SWA_KV_HEADS, SWA_GROUP, HEAD_DIM), q_norm)
    k = rms_norm(k.reshape(bsz, s, SWA_KV_HEADS, HEAD_DIM), k_norm)
    v = v.reshape(bsz, s, SWA_KV_HEADS, HEAD_DIM)
    attn = banded_attention(q, k, v, SWA_WINDOW - 1,
                            sink=sinks.reshape(SWA_KV_HEADS, SWA_GROUP))
    attn = attn.reshape(bsz, s, SWA_Q_W)
    xbc = jax.nn.silu(causal_depthwise_conv(xbc, conv_w, conv_b))
    xs, bm, cm = jnp.split(xbc, (SSM_D_INNER, SSM_D_INNER + SSM_BC_W), axis=-1)
    dt = jax.nn.softplus(dt_raw.astype(jnp.float32) + dt_bias.astype(jnp.float32))
    a = -jnp.exp(a_log.astype(jnp.float32))
    xh = xs.reshape(bsz, s, SSM_HEADS, SSM_HEAD_DIM)
    y = ssd_chunked(xh, dt, a,
                    bm.reshape(bsz, s, SSM_GROUPS, SSM_STATE),
                    cm.reshape(bsz, s, SSM_GROUPS, SSM_STATE))
    y = y + d_skip.astype(jnp.float32)[:, None] * xh.astype(jnp.float32)
    y = gated_group_rmsnorm(y.reshape(bsz, s, SSM_D_INNER), z, gate_norm).astype(h.dtype)
    mixed = jnp.concatenate([attn, y], axis=-1)
    return jnp.einsum('bse,ed->bsd', mixed, w_out)


def to_strided(t, dil):
    bsz, s, h, d = t.shape
    t = t.reshape(bsz, s // dil, dil, h, d).transpose(0, 2, 1, 3, 4)
    return t.reshape(bsz * dil, s // dil, h, d)


def from_strided(t, bsz, dil):
    n = t.shape[1]
    rest = t.shape[2:]
    t = jnp.swapaxes(t.reshape((bsz, dil, n) + rest), 1, 2)
    return t.reshape((bsz, n * dil) + rest)


def dilated_mixer(h, w_qkv, q_norm, k_norm, w_o):
    bsz, s, _ = h.shape
    q, k, v = jnp.split(jnp.einsum('bsd,de->bse', h, w_qkv), 3, axis=-1)
    q = rms_norm(q.reshape(bsz, s, DIL_HEADS, HEAD_DIM), q_norm)
    k = rms_norm(k.reshape(bsz, s, DIL_HEADS, HEAD_DIM), k_norm)
    v = v.reshape(bsz, s, DIL_HEADS, HEAD_DIM)
    s_pad = -(-s // DIL_PAD) * DIL_PAD
    pad = ((0, 0), (0, s_pad - s), (0, 0), (0, 0))
    q, k, v = jnp.pad(q, pad), jnp.pad(k, pad), jnp.pad(v, pad)
    outs = []
    lses = []
    for window, dil in DIL_BRANCHES:
        o, lse = banded_attention(to_strided(q, dil)[:, :, :, None, :], to_strided(k, dil),
                                  to_strided(v, dil), window // dil, return_lse=True)
        outs.append(from_strided(o[:, :, :, 0], bsz, dil))
        lses.append(from_strided(lse[..., 0], bsz, dil))
    weights = jax.nn.softmax(jnp.stack(lses, axis=0), axis=0)
    o = jnp.einsum('xbsh,xbshd->bshd', weights, jnp.stack(outs, axis=0).astype(jnp.float32))
    o = o[:, :s].reshape(bsz, s, DIL_W).astype(h.dtype)
    return jnp.einsum('bse,ed->bsd', o, w_o)


def squared_relu_mlp(h, w_up, w_down):
    u = jnp.einsum('bsd,df->bsf', h, w_up)
    return jnp.einsum('bsf,fd->bsd', jnp.square(jax.nn.relu(u)), w_down)


def setup_inputs(seed: int = 0) -> dict:
    key = jax.random.key(seed)
    ks = jax.random.split(key, 20)
    f32 = jnp.float32

    def dense(k, shape):
        return jax.random.normal(k, shape, f32) * (shape[-2] ** -0.5)

    def gain(k, shape):
        return 1.0 + 0.02 * jax.random.normal(k, shape, f32)

    dt = jnp.exp(jax.random.uniform(ks[11], (N_EVEN, SSM_HEADS), f32,
                                    math.log(DT_MIN), math.log(DT_MAX)))
    dt_bias = dt + jnp.log(-jnp.expm1(-dt))
    return {
        'x': jax.random.normal(ks[0], (BATCH, SEQ, D_MODEL), f32),
        'norm_mix': gain(ks[1], (DEPTH, D_MODEL)),
        'norm_ffn': gain(ks[2], (DEPTH, D_MODEL)),
        'w_up': dense(ks[3], (DEPTH, D_MODEL, D_FF)),
        'w_down': dense(ks[4], (DEPTH, D_FF, D_MODEL)),
        'ab_w_in': dense(ks[5], (N_EVEN, D_MODEL, AB_IN_W)),
        'ab_q_norm': gain(ks[6], (N_EVEN, HEAD_DIM)),
        'ab_k_norm': gain(ks[7], (N_EVEN, HEAD_DIM)),
        'ab_sinks': jax.random.normal(ks[8], (N_EVEN, SWA_Q_HEADS), f32),
        'ab_conv_w': jax.random.uniform(ks[9], (N_EVEN, SSM_CONV, SSM_CONV_DIM), f32, -0.5, 0.5),
        'ab_conv_b': jax.random.uniform(ks[10], (N_EVEN, SSM_CONV_DIM), f32, -0.1, 0.1),
        'ab_dt_bias': dt_bias,
        'ab_a_log': jnp.log(jax.random.uniform(ks[12], (N_EVEN, SSM_HEADS), f32, 1.0, 16.0)),
        'ab_d_skip': 1.0 + 0.1 * jax.random.normal(ks[13], (N_EVEN, SSM_HEADS), f32),
        'ab_gate_norm': gain(ks[14], (N_EVEN, SSM_D_INNER)),
        'ab_w_out': dense(ks[15], (N_EVEN, AB_MIX_W, D_MODEL)),
        'c_w_qkv': dense(ks[16], (N_ODD, D_MODEL, 3 * DIL_W)),
        'c_q_norm': gain(ks[17], (N_ODD, HEAD_DIM)),
        'c_k_norm': gain(ks[18], (N_ODD, HEAD_DIM)),
        'c_w_o': dense(ks[19], (N_ODD, DIL_W, D_MODEL)),
    }


def reference(x, norm_mix, norm_ffn, w_up, w_down, ab_w_in, ab_q_norm, ab_k_norm, ab_sinks,
              ab_conv_w, ab_conv_b, ab_dt_bias, ab_a_log, ab_d_skip, ab_gate_norm, ab_w_out,
              c_w_qkv, c_q_norm, c_k_norm, c_w_o):
    for layer in range(DEPTH):
        i = layer // 2
        h = rms_norm(x, norm_mix[layer])
        if layer % 2 == 0:
            x = x + swa_ssd_mixer(h, ab_w_in[i], ab_q_norm[i], ab_k_norm[i], ab_sinks[i],
                                  ab_conv_w[i], ab_conv_b[i], ab_dt_bias[i], ab_a_log[i],
                                  ab_d_skip[i], ab_gate_norm[i], ab_w_out[i])
        else:
            x = x + dilated_mixer(h, c_w_qkv[i], c_q_norm[i], c_k_norm[i], c_w_o[i])
        h = rms_norm(x, norm_ffn[layer])
        x = x + squared_relu_mlp(h, w_up[layer], w_down[layer])
    return x
```

```python
import bisect
import contextlib
import numpy as np
import ml_dtypes
import concourse.bass as bass
import concourse.mybir as mybir
from concourse.bass_utils import run_bass_kernel_spmd

F32 = mybir.dt.float32
BF16 = mybir.dt.bfloat16
AF = mybir.ActivationFunctionType
ALU = mybir.AluOpType
AX = mybir.AxisListType

D_MODEL = 2048
DC = D_MODEL // 128
D_FF = 8192
FC = D_FF // 128
NFB = D_FF // 512
EPS = 1e-5
SAME_ENGINE_SYNC = True


class Prog:
    ENGS = ("pe", "act", "dve", "pool")

    def __init__(self, nc):
        self.nc = nc
        self.ops = []
        self.top = contextlib.ExitStack()
        self.pstack = None
        self.last_w = {}
        self.readers = {}
        self.n_sb = 0
        self.cnt = {e: 0 for e in self.ENGS}
        self.dma_cum = {}
        self.dma_lists = {}
        self.sems = {e: self.top.enter_context(nc.semaphore("s_" + e)) for e in self.ENGS}
        self.bar = self.top.enter_context(nc.semaphore("s_bar"))
        self.dsems = {}
        self.phase = 0
        self.phase_start = 0
        self.regs = {}

    def _stk(self):
        return self.pstack if self.pstack is not None else self.top

    def sbuf(self, shape, dt, name=None):
        self.n_sb += 1
        return self._stk().enter_context(
            self.nc.sbuf_tensor(name or f"sb{self.n_sb}", list(shape), dt))

    def psum(self, shape, dt, name=None):
        self.n_sb += 1
        return self._stk().enter_context(
            self.nc.psum_tensor(name or f"ps{self.n_sb}", list(shape), dt))

    def op(self, eng, fn, reads=(), writes=(), dma_key=None, inc=16):
        raw, war = set(), set()
        for b in reads:
            if b in self.last_w:
                raw.add(self.last_w[b])
        for b in writes:
            if b in self.last_w:
                war.add(self.last_w[b])
            war.update(self.readers.get(b, ()))
        idx = len(self.ops)
        war -= raw
        self.ops.append(dict(eng=eng, fn=fn, raw=raw, war=war, dma_key=dma_key,
                             needed=False, cnt=None, inc=inc))
        for b in reads:
            self.readers.setdefault(b, []).append(idx)
        for b in writes:
            self.last_w[b] = idx
            self.readers[b] = []
        return idx

    def dma(self, q, out, in_, reads=(), writes=(), key=None, **kw):
        assert key is not None
        return self.op(q, lambda e: e.dma_start(out=out, in_=in_, **kw),
                       reads=reads, writes=writes, dma_key=key)

    def cc(self, kind, in_ap, out_ap, groups, reads=(), writes=(), key=None):
        return self.op('pool', lambda e: e.collective_compute(kind, ALU.bypass, replica_groups=groups,
                                                              ins=[in_ap], outs=[out_ap]),
                       reads=reads, writes=writes, dma_key=key, inc=1)

    def _synced(self, x, y, kind):
        if x["eng"] != y["eng"]:
            return True
        if x["eng"] in ("pe", "sp"):
            return False
        return SAME_ENGINE_SYNC and kind == "raw"

    def begin_phase(self):
        self.pstack = contextlib.ExitStack()
        self.regs = {}

    def end_phase(self):
        nc = self.nc
        ops = self.ops
        ps = self.phase_start
        last = {}
        for i in range(ps, len(ops)):
            y = ops[i]
            if y["dma_key"] is None:
                last[y["eng"]] = i
            for kind in ("raw", "war"):
                for xi in y[kind]:
                    if xi < ps:
                        continue
                    x = ops[xi]
                    if x["dma_key"] is None and self._synced(x, y, kind):
                        x["needed"] = True
        for e, i in last.items():
            if e in self.ENGS:
                ops[i]["needed"] = True
        for i in range(ps, len(ops)):
            x = ops[i]
            if x["dma_key"] is not None:
                k = x["dma_key"]
                self.dma_cum[k] = self.dma_cum.get(k, 0) + x["inc"]
                self.dma_lists.setdefault(k, ([], []))
                self.dma_lists[k][0].append(i)
                self.dma_lists[k][1].append(self.dma_cum[k])
                if k not in self.dsems:
                    self.dsems[k] = self.top.enter_context(nc.semaphore("d_" + str(k)))
            elif x["needed"]:
                self.cnt[x["eng"]] += 1
                x["cnt"] = self.cnt[x["eng"]]
        dma_lists, dsems, sems = self.dma_lists, self.dsems, self.sems
        phase = self.phase

        def waits_for(yi):
            y = ops[yi]
            w = {}
            for kind in ("raw", "war"):
                for xi in y[kind]:
                    if xi < ps:
                        continue
                    x = ops[xi]
                    if x["dma_key"] is not None:
                        k = x["dma_key"]
                        idxs, cums = dma_lists[k]
                        pos = bisect.bisect_left(idxs, yi)
                        v = cums[pos - 1]
                        key = ("d", k)
                    else:
                        if not self._synced(x, y, kind):
                            continue
                        v = x["cnt"]
                        key = ("c", x["eng"])
                    if w.get(key, 0) < v:
                        w[key] = v
            return w

        def emit_engine(ename, e):
            seen = {}
            if phase > 0:
                e.wait_ge(self.bar, phase)
            for yi in range(ps, len(ops)):
                y = ops[yi]
                if y["eng"] != ename:
                    continue
                for key, v in waits_for(yi).items():
                    if seen.get(key, 0) >= v:
                        continue
                    seen[key] = v
                    sm = dsems[key[1]] if key[0] == "d" else sems[key[1]]
                    e.wait_ge(sm, v)
                ins = y["fn"](e)
                if y["dma_key"] is not None:
                    ins.then_inc(dsems[y["dma_key"]], y["inc"])
                elif y["needed"]:
                    ins.then_inc(sems[ename], 1)
            if ename == "sp":
                for k, v in self.dma_cum.items():
                    e.wait_ge(dsems[k], v)
                for en in self.ENGS:
                    if self.cnt[en] > 0:
                        e.wait_ge(sems[en], self.cnt[en])
                e.sem_inc(self.bar, 1)

        with nc.Block() as block:
            @block.sync
            def _(e):
                emit_engine("sp", e)

            @block.tensor
            def _(e):
                emit_engine("pe", e)

            @block.scalar
            def _(e):
                emit_engine("act", e)

            @block.vector
            def _(e):
                emit_engine("dve", e)

            @block.gpsimd
            def _(e):
                emit_engine("pool", e)
        if self.pstack is not None:
            self.pstack.close()
            self.pstack = None
        self.phase += 1
        self.phase_start = len(ops)

    def emit(self, final_wait_keys=()):
        self.end_phase()
        self.top.close()

    def close(self):
        self.top.close()


class Ctx:
    def __init__(self, p):
        self.p = p
        self.ps = [p.psum([128, 512], F32, name=f"psb{i}") for i in range(7)]
        self.psbf = p.psum([128, 1024], BF16, name="psbf")
        self.ps_rr = 0
        self.ones_bf = p.sbuf([128, 128], BF16, name="ones_bf")
        self.eps = p.sbuf([128, 1], F32, name="eps_t")
        p.op('pool', lambda e: e.memset(self.ones_bf[:], 1.0), writes=['ones_bf'])
        p.op('pool', lambda e: e.memset(self.eps[:], EPS), writes=['eps_t'])
        self.uid = 0

    def bank(self, lo=0, hi=7):
        i = lo + self.ps_rr % (hi - lo)
        self.ps_rr += 1
        return i

    def u(self):
        self.uid += 1
        return self.uid


def emit_rmsnorm_fm(cx, xt, xtok, nchunk, width, gain, out_bf, out_tok, sq_ring, rstd, dim, tmp_sqrt):
    p = cx.p
    b = cx.bank()
    ps = cx.ps[b]
    for c in range(nchunk):
        sq = sq_ring[c % len(sq_ring)]
        sqt = ('sq', id(sq_ring), c % len(sq_ring))
        p.op('act', lambda e, c=c, sq=sq: e.activation(out=sq[:, :width], in_=xt[:, c, :width], func=AF.Square),
             reads=[xtok(c)], writes=[sqt])
        p.op('pe', lambda e, c=c, sq=sq: e.matmul(ps[:, :width], lhsT=cx.ones_bf[:], rhs=sq[:, :width],
                                                   start=(c == 0), stop=(c == nchunk - 1)),
             reads=[sqt, 'ones_bf'], writes=[('ps', b)])
    p.op('act', lambda e: e.activation(out=tmp_sqrt[:, :width], in_=ps[:, :width], func=AF.Ln,
                                       bias=cx.eps[:, 0:1], scale=1.0 / dim),
         reads=[('ps', b), 'eps_t'], writes=[('tmp_sqrt', id(tmp_sqrt))])
    p.op('act', lambda e: e.activation(out=rstd[:, :width], in_=tmp_sqrt[:, :width], func=AF.Exp, scale=-0.5),
         reads=[('tmp_sqrt', id(tmp_sqrt))], writes=[('rstd', id(rstd))])
    for c in range(nchunk):
        p.op('dve', lambda e, c=c: e.scalar_tensor_tensor(out=out_bf[:, c, :width], in0=xt[:, c, :width],
                                                          scalar=gain[:, c:c + 1], in1=rstd[:, :width],
                                                          op0=ALU.mult, op1=ALU.mult),
             reads=[xtok(c), ('rstd', id(rstd)), 'gains'], writes=[out_tok(c)])


def emit_rmsnorm_stream(cx, src, nchunk, width, gain, out_bf, out_tok, xring, sq_ring, rstd, tmpl, dim, q, key):
    p = cx.p
    b = cx.bank()
    ps = cx.ps[b]
    for c in range(nchunk):
        xr, xrt = xring.next()
        p.dma(q, xr[:, :width], src(c), writes=[xrt], key=key)
        sq = sq_ring[c % len(sq_ring)]
        sqt = ('sq', id(sq_ring), c % len(sq_ring))
        p.op('act', lambda e, xr=xr, sq=sq: e.activation(out=sq[:, :width], in_=xr[:, :width], func=AF.Square),
             reads=[xrt], writes=[sqt])
        p.op('pe', lambda e, c=c, sq=sq: e.matmul(ps[:, :width], lhsT=cx.ones_bf[:], rhs=sq[:, :width],
                                                   start=(c == 0), stop=(c == nchunk - 1)),
             reads=[sqt, 'ones_bf'], writes=[('ps', b)])
    p.op('act', lambda e: e.activation(out=tmpl[:, :width], in_=ps[:, :width], func=AF.Ln,
                                       bias=cx.eps[:, 0:1], scale=1.0 / dim),
         reads=[('ps', b), 'eps_t'], writes=[('tmpl', id(tmpl))])
    p.op('act', lambda e: e.activation(out=rstd[:, :width], in_=tmpl[:, :width], func=AF.Exp, scale=-0.5),
         reads=[('tmpl', id(tmpl))], writes=[('rstd', id(rstd))])
    for c in range(nchunk):
        xr, xrt = xring.next()
        p.dma(q, xr[:, :width], src(c), writes=[xrt], key=key)
        p.op('dve', lambda e, c=c, xr=xr: e.scalar_tensor_tensor(out=out_bf[:, c, :width], in0=xr[:, :width],
                                                                 scalar=gain[:, c:c + 1], in1=rstd[:, :width],
                                                                 op0=ALU.mult, op1=ALU.mult),
             reads=[xrt, ('rstd', id(rstd)), 'gains'], writes=[out_tok(c)])


def cast_dense_weight_ops(p, w_out, w_up, w_dn, s_out, s_up, s_dn, EC, tag):
    ops = []
    for db in range(4):
        ops.append(lambda db=db: p.dma('pool', s_out[db].rearrange("p (c j) -> p c j", j=512),
                                       w_out[:, db * 512:(db + 1) * 512].rearrange("(c p) j -> p c j", p=128),
                                       writes=[(tag, 's_out', db)], key='cast'))
    for fb in range(NFB):
        ops.append(lambda fb=fb: p.dma('pool', s_up[fb].rearrange("p (c j) -> p c j", j=512),
                                       w_up[:, fb * 512:(fb + 1) * 512].rearrange("(c p) j -> p c j", p=128),
                                       writes=[(tag, 's_up', fb)], key='cast'))
    rows = min(2048, D_FF)
    for db in range(16):
        for q in range(D_FF // rows):
            ops.append(lambda db=db, q=q: p.dma(
                'pool', s_dn[db][:, q * rows:(q + 1) * rows].rearrange("p (c j) -> p c j", j=128),
                w_dn[q * rows:(q + 1) * rows, db * 128:(db + 1) * 128].rearrange("(c p) j -> p c j", p=128),
                writes=[(tag, 's_dn', db, q)], key='cast'))
    return ops


def emit_cast_dense_weights(p, w_out, w_up, w_dn, s_out, s_up, s_dn, EC, tag):
    for f in cast_dense_weight_ops(p, w_out, w_up, w_dn, s_out, s_up, s_dn, EC, tag):
        f()


def emit_dense(cx, T, EC, xT, mixT, s_out, s_up, s_dn, gain_ffn, outT, tag,
               gain_next=None, hnT=None, TT=512, mix_load=None, hn_store=None, aq='pool'):
    p = cx.p
    nt = T // TT
    xt = p.sbuf([128, DC, TT], F32, name=tag + "x")
    mt = p.sbuf([128, EC, TT], BF16, name=tag + "mix")
    h2 = p.sbuf([128, DC, TT], BF16, name=tag + "h2")
    at = p.sbuf([128, FC, TT], BF16, name=tag + "a")
    WR = 24 * 512
    wring = [p.sbuf([128, WR], BF16, name=tag + f"w{i}") for i in range(2)]
    stage = [p.sbuf([128, TT], F32, name=tag + f"st{i}") for i in range(2)]
    sq_ring = [p.sbuf([128, TT], BF16, name=tag + f"sq{i}") for i in range(2)]
    sqf = [p.sbuf([128, TT], F32, name=tag + f"sqf{i}") for i in range(2)]
    rstd = p.sbuf([128, TT], F32, name=tag + "rstd")
    tmp_sqrt = p.sbuf([128, TT], F32, name=tag + "tsq")
    g1 = p.sbuf([128, DC], F32, name=tag + "g1")
    p.dma('sp', g1[:], gain_ffn.rearrange("(c p) -> p c", p=128), writes=['gains'], key=tag + 'g',
          allow_slow_non_contiguous=True)
    if gain_next is not None:
        g2 = p.sbuf([128, DC], F32, name=tag + "g2")
        p.dma('sp', g2[:], gain_next.rearrange("(c p) -> p c", p=128), writes=['gains'], key=tag + 'g',
              allow_slow_non_contiguous=True)

    blocks = []
    for db in range(4):
        blocks.append(('out', db, s_out[db], EC * 512, [(tag, 's_out', db)]))
    for fb in range(NFB):
        blocks.append(('up', fb, s_up[fb], 16 * 512, [(tag, 's_up', fb)]))
    for db in range(16):
        blocks.append(('dn', db, s_dn[db], FC * 128, [(tag, 's_dn', db, q) for q in range(D_FF // min(2048, D_FF))]))
    nblk = len(blocks)
    wcount = [0]

    def load_block(gi):
        kind, bi, src, n, tok = blocks[gi % nblk]
        slot = gi % 2
        p.dma('sp', wring[slot][:, :n], src, reads=tok, writes=[(tag, 'wr', slot)], key=tag + f'wr{slot}')

    xtok = lambda c: (tag, 'x', c)
    total_blocks = nt * nblk
    load_block(0)
    gi = 0
    for t in range(nt):
        ts = slice(t * TT, (t + 1) * TT)
        for c in range(DC):
            p.dma(aq, xt[:, c, :], xT[c * 128:(c + 1) * 128, ts], writes=[xtok(c)], key=tag + 'xin')
        if mix_load is None:
            p.dma(aq, mt[:], mixT[:, ts].rearrange("(c p) t -> p c t", p=128),
                  writes=[(tag, 'mix')], key=tag + 'min')
        else:
            mix_load(t, mt, (tag, 'mix'))
        for db in range(4):
            if gi + 1 < total_blocks:
                load_block(gi + 1)
            slot = gi % 2
            w = wring[slot]
            for dj in range(4):
                dc = db * 4 + dj
                b = cx.bank()
                for ec in range(EC):
                    p.op('pe', lambda e, b=b, w=w, ec=ec, dj=dj: e.matmul(
                        cx.ps[b][:, :TT], lhsT=w[:, ec * 512 + dj * 128: ec * 512 + (dj + 1) * 128],
                        rhs=mt[:, ec, :], start=(ec == 0), stop=(ec == EC - 1)),
                        reads=[(tag, 'wr', slot), (tag, 'mix')], writes=[('ps', b)])
                p.op('dve', lambda e, b=b, dc=dc: e.tensor_tensor(out=xt[:, dc, :], in0=xt[:, dc, :],
                                                                   in1=cx.ps[b][:, :TT], op=ALU.add),
                     reads=[('ps', b), xtok(dc)], writes=[xtok(dc)])
            gi += 1
        emit_rmsnorm_fm(cx, xt, xtok, DC, TT, g1, h2, lambda c: (tag, 'h2', c), sq_ring, rstd, D_MODEL, tmp_sqrt)
        for fb in range(NFB):
            if gi + 1 < total_blocks:
                load_block(gi + 1)
            slot = gi % 2
            w = wring[slot]
            for fj in range(4):
                fc = fb * 4 + fj
                b = cx.bank()
                for dc in range(DC):
                    p.op('pe', lambda e, b=b, w=w, dc=dc, fj=fj: e.matmul(
                        cx.ps[b][:, :TT], lhsT=w[:, dc * 512 + fj * 128: dc * 512 + (fj + 1) * 128],
                        rhs=h2[:, dc, :], start=(dc == 0), stop=(dc == DC - 1)),
                        reads=[(tag, 'wr', slot), (tag, 'h2', dc)], writes=[('ps', b)])
                sf = sqf[fc % 2]
                p.op('act', lambda e, b=b, sf=sf: e.activation(out=sf[:], in_=cx.ps[b][:, :TT], func=AF.Square),
                     reads=[('ps', b)], writes=[(tag, 'sqf', fc % 2)])
                p.op('dve', lambda e, b=b, sf=sf, fc=fc: e.scalar_tensor_tensor(
                    out=at[:, fc, :], in0=cx.ps[b][:, :TT], scalar=0.0, in1=sf[:], op0=ALU.is_gt, op1=ALU.mult),
                    reads=[('ps', b), (tag, 'sqf', fc % 2)], writes=[(tag, 'a', fc)])
            gi += 1
        for db in range(16):
            if gi + 1 < total_blocks:
                load_block(gi + 1)
            slot = gi % 2
            w = wring[slot]
            b = cx.bank()
            for fc in range(FC):
                p.op('pe', lambda e, b=b, w=w, fc=fc: e.matmul(
                    cx.ps[b][:, :TT], lhsT=w[:, fc * 128:(fc + 1) * 128], rhs=at[:, fc, :],
                    start=(fc == 0), stop=(fc == FC - 1)),
                    reads=[(tag, 'wr', slot), (tag, 'a', fc)], writes=[('ps', b)])
            if hnT is None and hn_store is None:
                st = stage[db % 2]
                p.op('dve', lambda e, b=b, db=db, st=st: e.tensor_tensor(out=st[:], in0=xt[:, db, :],
                                                                         in1=cx.ps[b][:, :TT], op=ALU.add),
                     reads=[('ps', b), xtok(db)], writes=[(tag, 'st', db % 2)])
                p.dma(aq, outT[db * 128:(db + 1) * 128, ts], st[:], reads=[(tag, 'st', db % 2)],
                      writes=[(tag, 'outT')], key=tag + 'out')
            else:
                p.op('dve', lambda e, b=b, db=db: e.tensor_tensor(out=xt[:, db, :], in0=xt[:, db, :],
                                                                   in1=cx.ps[b][:, :TT], op=ALU.add),
                     reads=[('ps', b), xtok(db)], writes=[xtok(db)])
                p.dma(aq, outT[db * 128:(db + 1) * 128, ts], xt[:, db, :], reads=[xtok(db)],
                      writes=[(tag, 'outT')], key=tag + 'out')
            gi += 1
        if hnT is not None or hn_store is not None:
            emit_rmsnorm_fm(cx, xt, xtok, DC, TT, g2, h2, lambda c: (tag, 'h2', c), sq_ring, rstd, D_MODEL, tmp_sqrt)
            if hn_store is None:
                p.dma(aq, hnT[:, ts].rearrange("(c p) t -> p c t", p=128), h2[:],
                      reads=[(tag, 'h2', c) for c in range(DC)], writes=[(tag, 'hnT')], key=tag + 'hout')
            else:
                hn_store(t, h2, [(tag, 'h2', c) for c in range(DC)])


def build_dense_program(T, EC, with_hn, do_cast=True):
    nc = bass.Bass("TRN2", target_bir_lowering=False)
    E = EC * 128
    xT = nc.dram_tensor("xT", [D_MODEL, T], F32, kind="ExternalInput").ap()
    mixT = nc.dram_tensor("mixT", [E, T], BF16, kind="ExternalInput").ap()
    w_out = nc.dram_tensor("w_out", [E, D_MODEL], F32, kind="ExternalInput").ap()
    w_up = nc.dram_tensor("w_up", [D_MODEL, D_FF], F32, kind="ExternalInput").ap()
    w_dn = nc.dram_tensor("w_dn", [D_FF, D_MODEL], F32, kind="ExternalInput").ap()
    g_ffn = nc.dram_tensor("g_ffn", [D_MODEL], F32, kind="ExternalInput").ap()
    outT = nc.dram_tensor("outT", [D_MODEL, T], F32, kind="ExternalOutput").ap()
    g_next = hnT = None
    if with_hn:
        g_next = nc.dram_tensor("g_next", [D_MODEL], F32, kind="ExternalInput").ap()
        hnT = nc.dram_tensor("hnT", [D_MODEL, T], BF16, kind="ExternalOutput").ap()
    s_out = nc.dram_tensor("s_out", [4, 128, EC * 512], BF16, kind="Internal").ap()
    s_up = nc.dram_tensor("s_up", [NFB, 128, 16 * 512], BF16, kind="Internal").ap()
    s_dn = nc.dram_tensor("s_dn", [16, 128, FC * 128], BF16, kind="Internal").ap()
    p = Prog(nc)
    cx = Ctx(p)
    emit_cast_dense_weights(p, w_out, w_up, w_dn, s_out, s_up, s_dn, EC, 'L')
    emit_dense(cx, T, EC, xT, mixT, s_out, s_up, s_dn, g_ffn, outT, 'L', gain_next=g_next, hnT=hnT)
    p.emit(final_wait_keys=['Lout'] + (['Lhout'] if with_hn else []))
    return nc


M0_NCOL = 1736
QO, KO, XO, BO, CO, VO, DTO, ZO = 0, 256, 384, 896, 1024, 1152, 1216, 1224
NEG = -30000.0


class Ring:
    def __init__(self, p, n, shape, dt, name):
        self.tiles = [p.sbuf(shape, dt, name=f"{name}{i}") for i in range(n)]
        self.name = name
        self.i = 0

    def next(self):
        k = self.i % len(self.tiles)
        self.i += 1
        return self.tiles[k], (self.name, k)


def host_consts():
    bf = ml_dtypes.bfloat16
    i = np.arange(128)
    U = (i[:, None] <= i[None, :]).astype(np.float32)
    c = {}
    c['ident_f'] = np.eye(128, dtype=np.float32)
    c['ident_b'] = np.eye(128, dtype=np.float32).astype(bf)
    c['U'] = U
    c['negU'] = -U
    c['ones_f'] = np.ones((128, 128), np.float32)
    nm = np.where(i[None, :] < i[:, None], NEG, 0.0).astype(np.float32)
    c['negmask4'] = np.tile(nm, (1, 4)).astype(bf)
    prev_swa = (i[:, None] > i[None, :]).astype(np.float32)
    prev_dil = (i[:, None] >= i[None, :]).astype(np.float32)
    own = (i[:, None] <= i[None, :]).astype(np.float32)
    c['swamask'] = np.tile(np.concatenate([prev_swa, own], 1), (1, 2)).astype(bf)
    c['dilmask'] = np.tile(np.concatenate([prev_dil, own], 1), (1, 2)).astype(bf)
    bo = np.zeros((128, 128), np.float32)
    bo[:64, :64] = 1
    bo[64:, 64:] = 1
    c['blockones'] = bo.astype(bf)
    op = np.zeros((128, 256), np.float32)
    op[:, 0:64] = 1
    op[:, 128 + 64:256] = 1
    c['onespad'] = op.astype(bf)
    return c


CONST_SPECS = dict(ident_f=([128, 128], F32), ident_b=([128, 128], BF16), U=([128, 128], F32),
                   negU=([128, 128], F32), ones_f=([128, 128], F32), negmask4=([128, 512], BF16),
                   swamask=([128, 512], BF16), dilmask=([128, 512], BF16), blockones=([128, 128], BF16),
                   onespad=([128, 256], BF16))


def load_consts(cx, nc, names):
    p = cx.p
    cx.c = {}
    for n in names:
        shp, dt = CONST_SPECS[n]
        src = nc.dram_tensor("c_" + n, shp, dt, kind="ExternalInput").ap()
        t = p.sbuf(shp, dt, name="k_" + n)
        p.dma('sp', t[:], src, writes=[('const', n)], key='consts')
        cx.c[n] = t


def emit_mixer0(cx, S, d, tag='m0', out_tile=None, per_tile=None):
    p = cx.p
    C = cx.c
    TT = 512
    nt = S // TT
    CT = lambda n: ('const', n)
    W = p.sbuf([128, DC, M0_NCOL], BF16, name='m0W')
    for c4 in range(4):
        p.dma('pool', W[:, c4 * 4:(c4 + 1) * 4, :],
              d['w_in'][c4 * 512:(c4 + 1) * 512, :].rearrange("(c p) n -> p c n", p=128),
              writes=[('m0W', c4)], key='m0w')
    Wtok = [('m0W', c4) for c4 in range(4)]
    gmix = p.sbuf([128, DC], F32, name='m0gmix')
    p.dma('sp', gmix[:], d['g_mix'].rearrange("(c p) -> p c", p=128), writes=['gains'], key='m0s',
          allow_slow_non_contiguous=True)
    small = {}
    for n, shp in (('qk_gain', [128, 2]), ('sink2', [128, 2]), ('conv_w', [128, 6, 4]), ('conv_b', [128, 6]),
                   ('dtb', [128, 8]), ('alog', [128, 8]), ('dskip', [128, 8]), ('gate', [128, 512])):
        t = p.sbuf(shp, F32, name='m0_' + n)
        p.dma('sp', t[:], d[n], writes=[('m0s', n)], key='m0s')
        small[n] = t
    esink = p.sbuf([128, 2], F32, name='m0esink')
    p.op('act', lambda e: e.activation(out=esink[:], in_=small['sink2'][:], func=AF.Exp),
         reads=[('m0s', 'sink2')], writes=['esink'])
    a_bc = p.sbuf([128, 8], F32, name='m0a')
    p.op('act', lambda e: e.activation(out=a_bc[:], in_=small['alog'][:], func=AF.Exp),
         reads=[('m0s', 'alog')], writes=['a_bc0'])
    na_bc = p.sbuf([128, 8], F32, name='m0na')
    p.op('dve', lambda e: e.tensor_scalar(out=na_bc[:], in0=a_bc[:], scalar1=-1.0, scalar2=None, op0=ALU.mult),
         reads=['a_bc0'], writes=['a_bc'])

    xring = Ring(p, 4, [128, TT], F32, 'm0xr')
    hT = p.sbuf([128, DC, TT], BF16, name='m0h')
    sq_ring = [p.sbuf([128, TT], BF16, name=f'm0sq{i}') for i in range(2)]
    qT = p.sbuf([128, 2, TT], BF16, name='m0qT')
    kT = p.sbuf([128, 2, TT], BF16, name='m0kT')
    qsb = Ring(p, 1, [128, TT], F32, 'm0qsb')
    qsq = Ring(p, 2, [128, TT], BF16, 'm0qsq')
    qln = Ring(p, 1, [128, TT], F32, 'm0qln')
    qrs = Ring(p, 1, [128, TT], F32, 'm0qrs')
    rstd = qrs.tiles[0]
    tmpl = qln.tiles[0]
    cinr = Ring(p, 2, [128, TT + 3], F32, 'm0cin')
    chist = p.sbuf([128, 6, 3], F32, name='m0chist')
    cacc = Ring(p, 2, [128, TT], F32, 'm0cacc')
    ctmp = Ring(p, 1, [128, TT], F32, 'm0ctmp')
    xsT = p.sbuf([128, 4, TT], F32, name='m0xsT')
    BT = p.sbuf([128, TT], BF16, name='m0BT')
    CTt = p.sbuf([128, TT], BF16, name='m0CT')
    Vpad = [p.sbuf([128, 8, 128], BF16, name=f'm0V{i}') for i in range(2)]
    praw = Ring(p, 2, [128, 512], BF16, 'm0praw')
    pT = [Ring(p, 2, [128, 512], BF16, f'm0pT{i}') for i in range(2)]
    tden = Ring(p, 1, [128, TT], F32, 'm0tden')
    tden2 = Ring(p, 1, [128, TT], F32, 'm0tden2')
    mixst = [p.sbuf([128, 6, TT], BF16, name='m0mix0')] * 2
    H = p.sbuf([128, 512], F32, name='m0H')
    Hbf = p.sbuf([128, 512], BF16, name='m0Hbf')
    t8 = Ring(p, 4, [128, 8], F32, 'm0t8')
    e8 = Ring(p, 2, [128, 8], F32, 'm0e8')
    dt8 = Ring(p, 2, [128, 8], F32, 'm0dt8')
    dA8 = Ring(p, 2, [128, 8], F32, 'm0dA8')
    dAU = Ring(p, 1, [128, 8, 128], F32, 'm0dAU')
    cs16 = Ring(p, 2, [128, 16], F32, 'm0cs16')
    d8 = Ring(p, 2, [128, 8], F32, 'm0d8')
    ed8 = Ring(p, 2, [128, 8], F32, 'm0ed8')
    w28 = Ring(p, 2, [128, 8], F32, 'm0w28')
    ecum8 = Ring(p, 2, [128, 8], F32, 'm0ecum8')
    dect8 = Ring(p, 2, [128, 8], F32, 'm0dect8')
    Esb = Ring(p, 2, [128, 512], F32, 'm0E')
    MT = Ring(p, 2, [128, 8, 128], BF16, 'm0MT')
    xtok = Ring(p, 1, [128, 512], F32, 'm0xtok')
    xdt = Ring(p, 2, [128, 8, 64], BF16, 'm0xdt')
    xdec = Ring(p, 2, [128, 512], BF16, 'm0xdec')
    xD = Ring(p, 1, [128, 512], F32, 'm0xD')
    Btok = Ring(p, 2, [128, 128], BF16, 'm0Btok')
    y1 = Ring(p, 2, [128, 512], F32, 'm0y1')
    sz = Ring(p, 1, [128, 512], F32, 'm0sz')
    junk = Ring(p, 1, [128, 512], BF16, 'm0junk')
    ss1 = Ring(p, 2, [128, 1], F32, 'm0ss1')
    ln1 = Ring(p, 2, [128, 1], F32, 'm0ln1')
    rs1 = Ring(p, 2, [128, 1], F32, 'm0rs1')
    ybf = Ring(p, 2, [128, 512], BF16, 'm0ybf')
    psbf = cx.psbf

    p.op('pool', lambda e: e.memset(chist[:], 0.0), writes=[('cinh', c6) for c6 in range(6)])
    p.op('pool', lambda e: e.memset(kT[:], 0.0), writes=[('kT', 0), ('kT', 1)])
    for i in range(2):
        p.op('pool', lambda e, i=i: e.memset(Vpad[i][:], 0.0), writes=[('Vpad', i, s) for s in range(8)])
    p.op('pool', lambda e: e.memset(H[:], 0.0), writes=['H'])
    p.op('pool', lambda e: e.memset(Hbf[:], 0.0), writes=['Hbf'])

    for t in range(nt):
        ts = slice(t * TT, (t + 1) * TT)
        sl = t % 2
        emit_rmsnorm_stream(cx, lambda c: d['xT'][c * 128:(c + 1) * 128, ts], DC, TT, gmix, hT, lambda c: ('m0h', c),
                            xring, sq_ring, rstd, tmpl, D_MODEL, 'sp', 'm0x')
        hreads = [('m0h', c) for c in range(DC)]
        mst = mixst[sl]
        mtok = lambda c: ('mixst', 0, c)

        def proj_fm(col0):
            b = cx.bank()
            for dc in range(DC):
                p.op('pe', lambda e, b=b, dc=dc: e.matmul(cx.ps[b][:, :TT], lhsT=W[:, dc, col0:col0 + 128],
                                                          rhs=hT[:, dc, :], start=(dc == 0), stop=(dc == DC - 1)),
                     reads=[('m0h', dc), Wtok[dc // 4]], writes=[('ps', b)])
            return b

        for qi in range(3):
            b = proj_fm(QO + qi * 128)
            qs, qst = qsb.next()
            sq, sqt = qsq.next()
            p.op('act', lambda e, b=b, qs=qs: e.activation(out=qs[:], in_=cx.ps[b][:, :TT], func=AF.Copy),
                 reads=[('ps', b)], writes=[qst])
            p.op('act', lambda e, b=b, sq=sq: e.activation(out=sq[:], in_=cx.ps[b][:, :TT], func=AF.Square),
                 reads=[('ps', b)], writes=[sqt])
            b2 = cx.bank()
            p.op('pe', lambda e, b2=b2, sq=sq: e.matmul(cx.ps[b2][:, :TT], lhsT=C['blockones'][:], rhs=sq[:],
                                                         start=True, stop=True),
                 reads=[sqt, CT('blockones')], writes=[('ps', b2)])
            ln, lnt = qln.next()
            rs, rst = qrs.next()
            p.op('act', lambda e, b2=b2, ln=ln: e.activation(out=ln[:], in_=cx.ps[b2][:, :TT], func=AF.Ln,
                                                              bias=cx.eps[:, 0:1], scale=1.0 / 64),
                 reads=[('ps', b2), 'eps_t'], writes=[lnt])
            p.op('act', lambda e, ln=ln, rs=rs: e.activation(out=rs[:], in_=ln[:], func=AF.Exp, scale=-0.5),
                 reads=[lnt], writes=[rst])
            if qi < 2:
                dst, dtok, gcol = qT[:, qi, :], ('qT', qi), 0
            else:
                dst, dtok, gcol = kT[:, sl, :], ('kT', sl), 1
            p.op('dve', lambda e, qs=qs, rs=rs, dst=dst, gcol=gcol: e.scalar_tensor_tensor(
                out=dst, in0=qs[:], scalar=small['qk_gain'][:, gcol:gcol + 1], in1=rs[:], op0=ALU.mult, op1=ALU.mult),
                reads=[qst, rst, ('m0s', 'qk_gain')], writes=[dtok])

        for c6 in range(6):
            b = proj_fm(XO + c6 * 128)
            cin, cint = cinr.next()
            p.op('act', lambda e, b=b, cin=cin: e.activation(out=cin[:, 3:3 + TT], in_=cx.ps[b][:, :TT], func=AF.Copy),
                 reads=[('ps', b)], writes=[(cint, 'd')])
            p.op('pool', lambda e, cin=cin, c6=c6: e.tensor_copy(out=cin[:, 0:3], in_=chist[:, c6, :]),
                 reads=[('cinh', c6)], writes=[(cint, 'h')])
            acc, acct = cacc.next()
            cw = small['conv_w']
            p.op('act', lambda e, acc=acc, cin=cin, c6=c6: e.activation(out=acc[:], in_=cin[:, 0:TT], func=AF.Copy, scale=cw[:, c6, 0:1]),
                 reads=[(cint, 'd'), (cint, 'h'), ('m0s', 'conv_w')], writes=[acct])
            for j in range(1, 4):
                p.op('dve', lambda e, acc=acc, cin=cin, c6=c6, j=j: e.scalar_tensor_tensor(
                    out=acc[:], in0=cin[:, j:j + TT], scalar=cw[:, c6, j:j + 1], in1=acc[:], op0=ALU.mult, op1=ALU.add),
                    reads=[(cint, 'd'), (cint, 'h'), acct], writes=[acct])
            p.op('pool', lambda e, cin=cin, c6=c6: e.tensor_copy(out=chist[:, c6, :], in_=cin[:, TT:TT + 3]),
                 reads=[(cint, 'd')], writes=[('cinh', c6)])
            if c6 < 4:
                dst, dtok = xsT[:, c6, :], ('xsT', c6)
            elif c6 == 4:
                dst, dtok = BT[:], 'BT'
            else:
                dst, dtok = CTt[:], 'CT'
            p.op('act', lambda e, acc=acc, dst=dst, c6=c6: e.activation(out=dst, in_=acc[:], func=AF.Silu,
                                                                         bias=small['conv_b'][:, c6:c6 + 1]),
                 reads=[acct, ('m0s', 'conv_b')], writes=[dtok])

        pv = []
        pz = []
        for ci in range(4):
            cs = slice(ci * 128, (ci + 1) * 128)
            bv = cx.bank()
            for dc in range(DC):
                p.op('pe', lambda e, bv=bv, dc=dc, cs=cs: e.matmul(cx.ps[bv][:, 0:72], lhsT=hT[:, dc, cs],
                                                                   rhs=W[:, dc, VO:VO + 72], start=(dc == 0), stop=(dc == DC - 1)),
                     reads=[('m0h', dc), Wtok[dc // 4]], writes=[('ps', bv)])
            G = t * 4 + ci
            vs = G % 8
            p.op('act', lambda e, bv=bv, vs=vs: e.activation(out=Vpad[0][:, vs, 0:64], in_=cx.ps[bv][:, 0:64], func=AF.Copy),
                 reads=[('ps', bv)], writes=[('Vpad', 0, vs)])
            p.op('act', lambda e, bv=bv, vs=vs: e.activation(out=Vpad[1][:, vs, 64:128], in_=cx.ps[bv][:, 0:64], func=AF.Copy),
                 reads=[('ps', bv)], writes=[('Vpad', 1, vs)])
            tt8, tt8t = t8.next()
            p.op('dve', lambda e, bv=bv, tt8=tt8: e.tensor_tensor(out=tt8[:], in0=cx.ps[bv][:, 64:72], in1=small['dtb'][:], op=ALU.add),
                 reads=[('ps', bv), ('m0s', 'dtb')], writes=[tt8t])
            pv.append((tt8, tt8t))

        for pr in range(2):
            nb = cx.bank()
            db = cx.bank()
            for cp in range(2):
                pts = []
                for hp in range(2):
                    rows = slice(64 * hp, 64 * hp + 64)
                    b = cx.bank()
                    for cj in range(2):
                        ci = cp * 2 + cj
                        qs_ = qT[rows, pr, ci * 128:(ci + 1) * 128]
                        if ci == 0:
                            kprev, kpt = kT[rows, 1 - sl, 384:512], ('kT', 1 - sl)
                        else:
                            kprev, kpt = kT[rows, sl, (ci - 1) * 128:ci * 128], ('kT', sl)
                        kown = kT[rows, sl, ci * 128:(ci + 1) * 128]
                        p.op('pe', lambda e, b=b, cj=cj, kprev=kprev, qs_=qs_: e.matmul(
                            cx.ps[b][:, (cj * 2) * 128:(cj * 2 + 1) * 128], lhsT=kprev, rhs=qs_, start=True, stop=True),
                            reads=[kpt, ('qT', pr)], writes=[('ps', b)])
                        p.op('pe', lambda e, b=b, cj=cj, kown=kown, qs_=qs_: e.matmul(
                            cx.ps[b][:, (cj * 2 + 1) * 128:(cj * 2 + 2) * 128], lhsT=kown, rhs=qs_, start=True, stop=True),
                            reads=[('kT', sl), ('qT', pr)], writes=[('ps', b)])
                    pr_, prt = praw.next()
                    p.op('act', lambda e, b=b, pr_=pr_: e.activation(out=pr_[:], in_=cx.ps[b][:, :], func=AF.Exp, scale=0.125),
                         reads=[('ps', b)], writes=[prt])
                    pt_, ptt = pT[hp].next()
                    p.op('pool', lambda e, pr_=pr_, pt_=pt_: e.tensor_tensor(out=pt_[:], in0=pr_[:], in1=C['swamask'][:], op=ALU.mult),
                         reads=[prt, CT('swamask')], writes=[ptt])
                    pts.append((pt_, ptt))
                for cj in range(2):
                    ci = cp * 2 + cj
                    G = t * 4 + ci
                    terms = []
                    for hp in range(2):
                        for kb in range(2):
                            if G == 0 and kb == 0:
                                continue
                            terms.append((hp, kb))
                    for which, bank, in ((0, nb), (1, db)):
                        for n_, (hp, kb) in enumerate(terms):
                            vslot = (G - 1 + kb) % 8
                            if which == 0:
                                lhsT, lt = Vpad[hp][:, vslot, :], ('Vpad', hp, vslot)
                            else:
                                lhsT, lt = C['onespad'][:, hp * 128:(hp + 1) * 128], CT('onespad')
                            rhs = pts[hp][0][:, (cj * 2 + kb) * 128:(cj * 2 + kb + 1) * 128]
                            p.op('pe', lambda e, bank=bank, ci=ci, lhsT=lhsT, rhs=rhs, n_=n_, nn=len(terms): e.matmul(
                                cx.ps[bank][:, ci * 128:(ci + 1) * 128], lhsT=lhsT, rhs=rhs, start=(n_ == 0), stop=(n_ == nn - 1)),
                                reads=[lt, pts[hp][1]], writes=[('ps', bank)])
            td, tdt = tden.next()
            td2, td2t = tden2.next()
            p.op('dve', lambda e, db=db, td=td, pr=pr: e.tensor_scalar(out=td[:], in0=cx.ps[db][:, :TT], scalar1=esink[:, pr:pr + 1],
                                                                        scalar2=None, op0=ALU.add),
                 reads=[('ps', db), 'esink'], writes=[tdt])
            p.op('dve', lambda e, td=td, td2=td2: e.reciprocal(out=td2[:], in_=td[:]), reads=[tdt], writes=[td2t])
            p.op('dve', lambda e, nb=nb, td2=td2, pr=pr: e.tensor_tensor(out=mst[:, pr, :], in0=cx.ps[nb][:, :TT], in1=td2[:], op=ALU.mult),
                 reads=[('ps', nb), td2t], writes=[mtok(pr)])

        for ci in range(4):
            cs = slice(ci * 128, (ci + 1) * 128)
            tt8, tt8t = pv[ci]
            ee, eet = e8.next()
            dt_, dtt = dt8.next()
            dA, dAt = dA8.next()
            p.op('act', lambda e, tt8=tt8, ee=ee: e.activation(out=ee[:], in_=tt8[:], func=AF.Exp), reads=[tt8t], writes=[eet])
            p.op('act', lambda e, ee=ee, dt_=dt_: e.activation(out=dt_[:], in_=ee[:], func=AF.Ln, bias=1.0), reads=[eet], writes=[dtt])
            p.op('dve', lambda e, dt_=dt_, dA=dA: e.tensor_tensor(out=dA[:], in0=dt_[:], in1=na_bc[:], op=ALU.mult),
                 reads=[dtt, 'a_bc'], writes=[dAt])
            dau, daut = dAU.next()
            p.op('pool', lambda e, dau=dau, dA=dA: e.tensor_tensor(out=dau[:], in0=C['U'][:].unsqueeze(1).to_broadcast([128, 8, 128]),
                                                                    in1=dA[:].unsqueeze(2).to_broadcast([128, 8, 128]), op=ALU.mult),
                 reads=[dAt, CT('U')], writes=[daut])
            bs = cx.bank()
            p.op('pe', lambda e, bs=bs, dA=dA: e.matmul(cx.ps[bs][:, 0:8], lhsT=C['U'][:], rhs=dA[:], start=True, stop=True),
                 reads=[dAt, CT('U')], writes=[('ps', bs)])
            p.op('pe', lambda e, bs=bs, dA=dA: e.matmul(cx.ps[bs][:, 8:16], lhsT=C['ones_f'][:], rhs=dA[:], start=True, stop=True),
                 reads=[dAt, CT('ones_f')], writes=[('ps', bs)])
            cs_, cst = cs16.next()
            p.op('act', lambda e, bs=bs, cs_=cs_: e.activation(out=cs_[:], in_=cx.ps[bs][:, 0:16], func=AF.Copy),
                 reads=[('ps', bs)], writes=[cst])
            dd, ddt = d8.next()
            p.op('dve', lambda e, cs_=cs_, dd=dd: e.tensor_tensor(out=dd[:], in0=cs_[:, 8:16], in1=cs_[:, 0:8], op=ALU.subtract),
                 reads=[cst], writes=[ddt])
            ed, edt = ed8.next()
            p.op('act', lambda e, dd=dd, ed=ed: e.activation(out=ed[:], in_=dd[:], func=AF.Exp), reads=[ddt], writes=[edt])
            w2, w2t = w28.next()
            p.op('dve', lambda e, ed=ed, dt_=dt_, w2=w2: e.tensor_tensor(out=w2[:], in0=ed[:], in1=dt_[:], op=ALU.mult),
                 reads=[edt, dtt], writes=[w2t])
            ec, ect = ecum8.next()
            p.op('act', lambda e, cs_=cs_, ec=ec: e.activation(out=ec[:], in_=cs_[:, 0:8], func=AF.Exp), reads=[cst], writes=[ect])
            dct, dctt = dect8.next()
            p.op('act', lambda e, cs_=cs_, dct=dct: e.activation(out=dct[:], in_=cs_[:, 8:16], func=AF.Exp), reads=[cst], writes=[dctt])
            bcb = cx.bank()
            p.op('pe', lambda e, bcb=bcb, cs=cs: e.matmul(cx.ps[bcb][:, 0:128], lhsT=BT[:, cs], rhs=CTt[:, cs], start=True, stop=True),
                 reads=['BT', 'CT'], writes=[('ps', bcb)])
            mt_, mtt = MT.next()
            for half in range(2):
                bsg = cx.bank()
                hs = slice(half * 4, half * 4 + 4)
                p.op('pe', lambda e, bsg=bsg, dau=dau, hs=hs: e.matmul(cx.ps[bsg][:, :], lhsT=C['ones_f'][:],
                                                                       rhs=dau[:, hs, :], start=True, stop=False),
                     reads=[daut, CT('ones_f')], writes=[('ps', bsg)])
                p.op('pe', lambda e, bsg=bsg, dA=dA, hs=hs: e.matmul(cx.ps[bsg][:, :], lhsT=C['negU'][:],
                                                                      rhs=dA[:, hs].unsqueeze(2).to_broadcast([128, 4, 128]),
                                                                      start=False, stop=False),
                     reads=[dAt, CT('negU')], writes=[('ps', bsg)])
                p.op('pe', lambda e, bsg=bsg: e.matmul(cx.ps[bsg][:, :], lhsT=C['ident_b'][:], rhs=C['negmask4'][:],
                                                       start=False, stop=True),
                     reads=[CT('ident_b'), CT('negmask4')], writes=[('ps', bsg)])
                E_, Et = Esb.next()
                p.op('act', lambda e, bsg=bsg, E_=E_: e.activation(out=E_[:], in_=cx.ps[bsg][:, :], func=AF.Exp),
                     reads=[('ps', bsg)], writes=[Et])
                p.op('dve', lambda e, E_=E_, mt_=mt_, hs=hs, bcb=bcb: e.tensor_tensor(
                    out=mt_[:, hs, :], in0=E_[:].rearrange("p (h t) -> p h t", h=4),
                    in1=cx.ps[bcb][:, 0:128].unsqueeze(1).to_broadcast([128, 4, 128]), op=ALU.mult),
                    reads=[Et, ('ps', bcb)], writes=[(mtt, half)])
            bx = cx.bank()
            for c in range(4):
                p.op('pe', lambda e, bx=bx, c=c, cs=cs: e.transpose(out=cx.ps[bx][:, c * 128:(c + 1) * 128], in_=xsT[:, c, cs],
                                                                    identity=C['ident_f'][:]),
                     reads=[('xsT', c), CT('ident_f')], writes=[('ps', bx)])
            xk, xkt = xtok.next()
            p.op('act', lambda e, bx=bx, xk=xk: e.activation(out=xk[:], in_=cx.ps[bx][:, :], func=AF.Copy),
                 reads=[('ps', bx)], writes=[xkt])
            xd, xdt_t = xdt.next()
            xk3 = xk[:].rearrange("p (h j) -> p h j", h=8)
            p.op('dve', lambda e, xk3=xk3, xd=xd, dt_=dt_: e.tensor_tensor(out=xd[:], in0=xk3,
                                                                           in1=dt_[:].unsqueeze(2).to_broadcast([128, 8, 64]), op=ALU.mult),
                 reads=[xkt, dtt], writes=[xdt_t])
            xc, xct = xdec.next()
            p.op('dve', lambda e, xk3=xk3, xc=xc, w2=w2: e.tensor_tensor(out=xc[:].rearrange("p (h j) -> p h j", h=8), in0=xk3,
                                                                         in1=w2[:].unsqueeze(2).to_broadcast([128, 8, 64]), op=ALU.mult),
                 reads=[xkt, w2t], writes=[xct])
            xD_, xDt = xD.next()
            p.op('pool', lambda e, xk3=xk3, xD_=xD_: e.tensor_tensor(out=xD_[:].rearrange("p (h j) -> p h j", h=8), in0=xk3,
                                                                     in1=small['dskip'][:].unsqueeze(2).to_broadcast([128, 8, 64]), op=ALU.mult),
                 reads=[xkt, ('m0s', 'dskip')], writes=[xDt])
            p.op('pe', lambda e, cs=cs: e.transpose(out=psbf[:, 0:128], in_=BT[:, cs], identity=C['ident_b'][:]),
                 reads=['BT', CT('ident_b')], writes=['psbf'])
            bt_, btt = Btok.next()
            p.op('act', lambda e, bt_=bt_: e.activation(out=bt_[:], in_=psbf[:, 0:128], func=AF.Copy), reads=['psbf'], writes=[btt])
            bst = cx.bank()
            p.op('pe', lambda e, bst=bst, bt_=bt_, xc=xc: e.matmul(cx.ps[bst][:, :], lhsT=bt_[:], rhs=xc[:], start=True, stop=True),
                 reads=[btt, xct], writes=[('ps', bst)])
            byo = cx.bank()
            p.op('pe', lambda e, byo=byo, cs=cs: e.matmul(cx.ps[byo][:, :], lhsT=CTt[:, cs], rhs=Hbf[:], start=True, stop=True),
                 reads=['CT', 'Hbf'], writes=[('ps', byo)])
            byd = cx.bank()
            for h in range(8):
                p.op('pe', lambda e, byd=byd, h=h, mt_=mt_, xd=xd: e.matmul(cx.ps[byd][:, h * 64:(h + 1) * 64], lhsT=mt_[:, h, :],
                                                                            rhs=xd[:, h, :], start=True, stop=True),
                     reads=[(mtt, h // 4), xdt_t], writes=[('ps', byd)])
            ya, yat = y1.next()
            p.op('dve', lambda e, byo=byo, ya=ya, ec=ec: e.tensor_tensor(out=ya[:].rearrange("p (h j) -> p h j", h=8),
                                                                         in0=cx.ps[byo][:, :].rearrange("p (h j) -> p h j", h=8),
                                                                         in1=ec[:].unsqueeze(2).to_broadcast([128, 8, 64]), op=ALU.mult),
                 reads=[('ps', byo), ect], writes=[yat])
            p.op('dve', lambda e, byd=byd, ya=ya: e.tensor_tensor(out=ya[:], in0=ya[:], in1=cx.ps[byd][:, :], op=ALU.add),
                 reads=[('ps', byd), yat], writes=[yat])
            p.op('pool', lambda e, ya=ya, xD_=xD_: e.tensor_tensor(out=ya[:], in0=ya[:], in1=xD_[:], op=ALU.add),
                 reads=[yat, xDt], writes=[yat])
            yc, yct = ya, yat
            p.op('dve', lambda e, dct=dct: e.tensor_tensor(out=H[:].rearrange("p (h j) -> p h j", h=8),
                                                           in0=H[:].rearrange("p (h j) -> p h j", h=8),
                                                           in1=dct[:].unsqueeze(2).to_broadcast([128, 8, 64]), op=ALU.mult),
                 reads=['H', dctt], writes=['H'])
            p.op('dve', lambda e, bst=bst: e.tensor_tensor(out=H[:], in0=H[:], in1=cx.ps[bst][:, :], op=ALU.add),
                 reads=['H', ('ps', bst)], writes=['H'])
            p.op('pool', lambda e: e.tensor_copy(out=Hbf[:], in_=H[:]), reads=['H'], writes=['Hbf'])
            bz = cx.bank()
            for dc in range(DC):
                p.op('pe', lambda e, bz=bz, dc=dc, cs=cs: e.matmul(cx.ps[bz][:, :], lhsT=hT[:, dc, cs], rhs=W[:, dc, ZO:ZO + 512],
                                                                   start=(dc == 0), stop=(dc == DC - 1)),
                     reads=[('m0h', dc), Wtok[dc // 4]], writes=[('ps', bz)])
            sz_, szt = sz.next()
            p.op('act', lambda e, bz=bz, sz_=sz_: e.activation(out=sz_[:], in_=cx.ps[bz][:, :], func=AF.Silu),
                 reads=[('ps', bz)], writes=[szt])
            gy_, gyt = yc, yct
            p.op('pool', lambda e, gy_=gy_, sz_=sz_: e.tensor_tensor(out=gy_[:], in0=gy_[:], in1=sz_[:], op=ALU.mult),
                 reads=[yct, szt], writes=[gyt])
            ss_, sst = ss1.next()
            jk, jkt = junk.next()
            p.op('pool', lambda e, ss_=ss_: e.memset(ss_[:], 0.0), writes=[sst])
            p.op('act', lambda e, gy_=gy_, jk=jk, ss_=ss_: e.activation(out=jk[:], in_=gy_[:], func=AF.Square, accum_out=ss_[:, 0:1]),
                 reads=[gyt, sst], writes=[jkt, sst])
            l1, l1t = ln1.next()
            r1, r1t = rs1.next()
            p.op('act', lambda e, ss_=ss_, l1=l1: e.activation(out=l1[:], in_=ss_[:], func=AF.Ln, bias=cx.eps[:, 0:1], scale=1.0 / 512),
                 reads=[sst, 'eps_t'], writes=[l1t])
            p.op('act', lambda e, l1=l1, r1=r1: e.activation(out=r1[:], in_=l1[:], func=AF.Exp, scale=-0.5), reads=[l1t], writes=[r1t])
            yf, yft = ybf.next()
            p.op('dve', lambda e, gy_=gy_, r1=r1, yf=yf: e.scalar_tensor_tensor(out=yf[:], in0=gy_[:], scalar=r1[:, 0:1], in1=small['gate'][:],
                                                                                op0=ALU.mult, op1=ALU.mult),
                 reads=[gyt, r1t, ('m0s', 'gate')], writes=[yft])
            for c in range(4):
                p.op('pe', lambda e, c=c, yf=yf: e.transpose(out=psbf[:, 128 + c * 128:128 + (c + 1) * 128], in_=yf[:, c * 128:(c + 1) * 128],
                                                              identity=C['ident_b'][:]),
                     reads=[yft, CT('ident_b')], writes=['psbf_y'])
            p.op('act', lambda e, cs=cs: e.activation(out=mst[:, 2:6, cs], in_=psbf[:, 128:640].rearrange("p (c t) -> p c t", c=4), func=AF.Copy),
                 reads=['psbf_y'], writes=[mtok(2 + ci)])
        if out_tile is None:
            p.dma('pool', d['mixT'][:, ts].rearrange("(c p) t -> p c t", p=128), mst[:],
                  reads=[mtok(c) for c in range(6)], writes=['mixT_out'], key='m0out')
        else:
            out_tile(t, mst, [mtok(c) for c in range(6)])
        if per_tile is not None:
            per_tile(t)


AB_Q, AB_K, AB_V, AB_Z, AB_X, AB_DT = 0, 1024, 1152, 1280, 3328, 6400


def m0_host_inputs(inp, b, g, S):
    w = inp['ab_w_in'][0]
    kv = g // 2
    cols = np.concatenate([
        np.arange(AB_Q + g * 256, AB_Q + (g + 1) * 256),
        np.arange(AB_K + kv * 64, AB_K + (kv + 1) * 64), np.arange(AB_K + kv * 64, AB_K + (kv + 1) * 64),
        np.arange(AB_X + g * 512, AB_X + (g + 1) * 512),
        np.arange(AB_X + 2048 + g * 128, AB_X + 2048 + (g + 1) * 128),
        np.arange(AB_X + 2560 + g * 128, AB_X + 2560 + (g + 1) * 128),
        np.arange(AB_V + kv * 64, AB_V + (kv + 1) * 64),
        np.arange(AB_DT + g * 8, AB_DT + (g + 1) * 8),
        np.arange(AB_Z + g * 512, AB_Z + (g + 1) * 512)])
    assert len(cols) == M0_NCOL
    chan = np.concatenate([np.arange(g * 512, (g + 1) * 512), 2048 + np.arange(g * 128, (g + 1) * 128),
                           2560 + np.arange(g * 128, (g + 1) * 128)])
    cw = inp['ab_conv_w'][0][:, chan]
    cb = inp['ab_conv_b'][0][chan]
    sk = inp['ab_sinks'][0][4 * g:4 * g + 4]
    rep = lambda v: np.ascontiguousarray(np.broadcast_to(np.asarray(v, np.float32)[None, :], (128, len(v))))
    d = dict(
        w_in=np.ascontiguousarray(w[:, cols]),
        g_mix=np.ascontiguousarray(inp['norm_mix'][0]),
        qk_gain=np.ascontiguousarray(np.stack([np.tile(inp['ab_q_norm'][0], 2), np.tile(inp['ab_k_norm'][0], 2)], 1)),
        sink2=np.ascontiguousarray(np.stack([np.repeat(sk[0:2], 64), np.repeat(sk[2:4], 64)], 1)),
        conv_w=np.ascontiguousarray(cw.reshape(4, 6, 128).transpose(2, 1, 0)),
        conv_b=np.ascontiguousarray(cb.reshape(6, 128).T),
        dtb=rep(inp['ab_dt_bias'][0][8 * g:8 * g + 8]),
        alog=rep(inp['ab_a_log'][0][8 * g:8 * g + 8]),
        dskip=rep(inp['ab_d_skip'][0][8 * g:8 * g + 8]),
        gate=rep(inp['ab_gate_norm'][0][512 * g:512 * (g + 1)]),
    )
    return {k: np.asarray(v, np.float32) for k, v in d.items()}


M0_SMALL = dict(qk_gain=[128, 2], sink2=[128, 2], conv_w=[128, 6, 4], conv_b=[128, 6], dtb=[128, 8],
                alog=[128, 8], dskip=[128, 8], gate=[128, 512])
M0_CONSTS = ['ident_f', 'ident_b', 'U', 'negU', 'ones_f', 'negmask4', 'swamask', 'blockones', 'onespad']


def const_inputs(names):
    hc = host_consts()
    return {"c_" + n: hc[n] for n in names}


def build_mixer0_program(S):
    nc = bass.Bass("TRN2", target_bir_lowering=False)
    d = {}
    d['xT'] = nc.dram_tensor("xT", [D_MODEL, S], F32, kind="ExternalInput").ap()
    d['w_in'] = nc.dram_tensor("w_in", [D_MODEL, M0_NCOL], F32, kind="ExternalInput").ap()
    d['g_mix'] = nc.dram_tensor("g_mix", [D_MODEL], F32, kind="ExternalInput").ap()
    for n, shp in M0_SMALL.items():
        d[n] = nc.dram_tensor(n, shp, F32, kind="ExternalInput").ap()
    d['mixT'] = nc.dram_tensor("mixT", [768, S], BF16, kind="ExternalOutput").ap()
    p = Prog(nc)
    cx = Ctx(p)
    load_consts(cx, nc, M0_CONSTS)
    emit_mixer0(cx, S, d)
    p.emit(final_wait_keys=['m0out'])
    return nc


def emit_mixer1(cx, S, d, tag='m1', h_src=None, out_pair=None):
    p = cx.p
    C = cx.c
    TT = 512
    ST = 2048
    nst = S // ST
    CT = lambda n: ('const', n)
    W = p.sbuf([128, DC, 1536], BF16, name='m1W')
    for c4 in range(4):
        p.dma('pool', W[:, c4 * 4:(c4 + 1) * 4, :],
              d['w_qkv'][c4 * 512:(c4 + 1) * 512, :].rearrange("(c p) n -> p c n", p=128),
              writes=[('m1W', c4)], key='m1w')
    Wtok = [('m1W', c4) for c4 in range(4)]
    qkg = p.sbuf([128, 2], F32, name='m1qkg')
    p.dma('sp', qkg[:], d['qk_gain'], writes=[('m1s', 'qk_gain')], key='m1s')
    hT = p.sbuf([128, DC, TT], BF16, name='m1h')
    qT = p.sbuf([128, 4, ST], BF16, name='m1qT')
    kT = p.sbuf([128, 4, 2, ST], BF16, name='m1kT')
    vT = p.sbuf([128, 4, 2, ST], BF16, name='m1vT')
    accN = p.sbuf([128, ST], F32, name='m1accN')
    accD = p.sbuf([128, ST], F32, name='m1accD')
    oT = p.sbuf([128, ST], BF16, name='m1oT')
    qsb = Ring(p, 1, [128, TT], F32, 'm1qsb')
    qsq = Ring(p, 2, [128, TT], BF16, 'm1qsq')
    qln = Ring(p, 1, [128, TT], F32, 'm1qln')
    qrs = Ring(p, 1, [128, TT], F32, 'm1qrs')
    praw = Ring(p, 2, [128, 512], BF16, 'm1praw')
    pT = [Ring(p, 2, [128, 512], BF16, f'm1pT{i}') for i in range(2)]
    VE = Ring(p, 8, [128, 128], BF16, 'm1VE')
    VO = Ring(p, 8, [128, 128], BF16, 'm1VO')
    psbf = cx.psbf
    for r_ in (VE, VO):
        for i, t_ in enumerate(r_.tiles):
            p.op('pool', lambda e, t_=t_: e.memset(t_[:], 0.0), writes=[(r_.name, i)])

    for M in range(nst):
        sl = M % 2
        for tt in range(4):
            t = M * 4 + tt
            ts = slice(t * TT, (t + 1) * TT)
            lc = slice(tt * TT, (tt + 1) * TT)
            for c4 in range(4):
                p.dma('sp', hT[:, c4 * 4:(c4 + 1) * 4, :],
                      (d['hT'][c4 * 512:(c4 + 1) * 512, ts] if h_src is None else h_src(t, c4)).rearrange("(c p) t -> p c t", p=128),
                      writes=[('m1h', c4 * 4 + j) for j in range(4)], key=f'm1h{c4}')
            for fc in range(12):
                b = cx.bank()
                for dc in range(DC):
                    p.op('pe', lambda e, b=b, dc=dc, fc=fc: e.matmul(cx.ps[b][:, :TT], lhsT=W[:, dc, fc * 128:(fc + 1) * 128],
                                                                     rhs=hT[:, dc, :], start=(dc == 0), stop=(dc == DC - 1)),
                         reads=[('m1h', dc), Wtok[dc // 4]], writes=[('ps', b)])
                kind, c = fc // 4, fc % 4
                if kind == 2:
                    vdst = vT[:, c, sl, lc]
                    p.op('act', lambda e, b=b, vdst=vdst: e.activation(out=vdst, in_=cx.ps[b][:, :TT], func=AF.Copy),
                         reads=[('ps', b)], writes=[('vT', c, sl, tt)])
                    continue
                qs, qst = qsb.next()
                sq, sqt = qsq.next()
                p.op('act', lambda e, b=b, qs=qs: e.activation(out=qs[:], in_=cx.ps[b][:, :TT], func=AF.Copy),
                     reads=[('ps', b)], writes=[qst])
                p.op('act', lambda e, b=b, sq=sq: e.activation(out=sq[:], in_=cx.ps[b][:, :TT], func=AF.Square),
                     reads=[('ps', b)], writes=[sqt])
                b2 = cx.bank()
                p.op('pe', lambda e, b2=b2, sq=sq: e.matmul(cx.ps[b2][:, :TT], lhsT=C['blockones'][:], rhs=sq[:], start=True, stop=True),
                     reads=[sqt, CT('blockones')], writes=[('ps', b2)])
                ln, lnt = qln.next()
                rs, rst = qrs.next()
                p.op('act', lambda e, b2=b2, ln=ln: e.activation(out=ln[:], in_=cx.ps[b2][:, :TT], func=AF.Ln,
                                                                  bias=cx.eps[:, 0:1], scale=1.0 / 64),
                     reads=[('ps', b2), 'eps_t'], writes=[lnt])
                p.op('act', lambda e, ln=ln, rs=rs: e.activation(out=rs[:], in_=ln[:], func=AF.Exp, scale=-0.5),
                     reads=[lnt], writes=[rst])
                if kind == 0:
                    dst, dtok = qT[:, c, lc], ('qT1', c, tt)
                else:
                    dst, dtok = kT[:, c, sl, lc], ('kT1', c, sl, tt)
                p.op('dve', lambda e, qs=qs, rs=rs, dst=dst, kind=kind: e.scalar_tensor_tensor(
                    out=dst, in0=qs[:], scalar=qkg[:, kind:kind + 1], in1=rs[:], op0=ALU.mult, op1=ALU.mult),
                    reads=[qst, rst, ('m1s', 'qk_gain')], writes=[dtok])

        def colslice(start, r):
            return slice(start, start + 127 * r + 1, r)

        def toks(name, c, slot, start, r):
            lo = start // TT
            hi = (start + 127 * r) // TT
            return [(name, c, slot, j) for j in range(lo, hi + 1)]

        for pr in range(4):
            for r in (1, 4, 16):
                nblk = ST // (128 * r)
                if r == 1:
                    quads = [[(0, m) for m in range(q4 * 4, q4 * 4 + 4)] for q4 in range(4)]
                elif r == 4:
                    quads = [[(c, m) for c in range(4)] for m in range(4)]
                else:
                    quads = [[(c, 0) for c in range(q4 * 4, q4 * 4 + 4)] for q4 in range(4)]
                for qi, quad in enumerate(quads):
                    nb = cx.bank()
                    db = cx.bank()
                    for half in range(2):
                        units = quad[half * 2:half * 2 + 2]
                        info = []
                        for (c, m) in units:
                            own_start = r * 128 * m + c
                            if m > 0:
                                prev = (sl, r * 128 * (m - 1) + c)
                            elif M > 0:
                                prev = (1 - sl, ST - 128 * r + c)
                            else:
                                prev = None
                            info.append((own_start, prev))
                        pts = []
                        for hp in range(2):
                            rows = slice(64 * hp, 64 * hp + 64)
                            b = cx.bank()
                            for uj, (own_start, prev) in enumerate(info):
                                qcols = colslice(own_start, r)
                                qs_ = qT[rows, pr, qcols]
                                qtk = [('qT1', pr, j) for j in range(own_start // TT, (own_start + 127 * r) // TT + 1)]
                                if prev is not None:
                                    kp = kT[rows, pr, prev[0], colslice(prev[1], r)]
                                    p.op('pe', lambda e, b=b, uj=uj, kp=kp, qs_=qs_: e.matmul(
                                        cx.ps[b][:, (uj * 2) * 128:(uj * 2 + 1) * 128], lhsT=kp, rhs=qs_, start=True, stop=True),
                                        reads=toks('kT1', pr, prev[0], prev[1], r) + qtk, writes=[('ps', b)])
                                ko = kT[rows, pr, sl, qcols]
                                p.op('pe', lambda e, b=b, uj=uj, ko=ko, qs_=qs_: e.matmul(
                                    cx.ps[b][:, (uj * 2 + 1) * 128:(uj * 2 + 2) * 128], lhsT=ko, rhs=qs_, start=True, stop=True),
                                    reads=toks('kT1', pr, sl, own_start, r) + qtk, writes=[('ps', b)])
                            pr_, prt = praw.next()
                            p.op('act', lambda e, b=b, pr_=pr_: e.activation(out=pr_[:], in_=cx.ps[b][:, :], func=AF.Exp, scale=0.125),
                                 reads=[('ps', b)], writes=[prt])
                            pt_, ptt = pT[hp].next()
                            p.op('pool', lambda e, pr_=pr_, pt_=pt_: e.tensor_tensor(out=pt_[:], in0=pr_[:], in1=C['dilmask'][:], op=ALU.mult),
                                 reads=[prt, CT('dilmask')], writes=[ptt])
                            pts.append((pt_, ptt))
                        for uj, (own_start, prev) in enumerate(info):
                            u = half * 2 + uj
                            vtiles = {}
                            for kb, src in ((0, prev), (1, (sl, own_start))):
                                if src is None:
                                    continue
                                boff = kb * 128
                                vsrc = vT[:, pr, src[0], colslice(src[1], r)]
                                p.op('pe', lambda e, vsrc=vsrc, boff=boff: e.transpose(out=psbf[:, boff:boff + 128],
                                                                                        in_=vsrc, identity=C['ident_b'][:]),
                                     reads=toks('vT', pr, src[0], src[1], r) + [CT('ident_b')], writes=[('psbf', kb)])
                                ve, vet = VE.next()
                                vo, vot = VO.next()
                                p.op('act', lambda e, ve=ve, boff=boff: e.activation(out=ve[:, 0:64], in_=psbf[:, boff:boff + 64], func=AF.Copy),
                                     reads=[('psbf', kb)], writes=[vet])
                                p.op('act', lambda e, vo=vo, boff=boff: e.activation(out=vo[:, 64:128], in_=psbf[:, boff + 64:boff + 128], func=AF.Copy),
                                     reads=[('psbf', kb)], writes=[vot])
                                vtiles[kb] = ((ve, vet), (vo, vot))
                            terms = [(hp, kb) for hp in range(2) for kb in range(2) if kb in vtiles]
                            for which, bank in ((0, nb), (1, db)):
                                for n_, (hp, kb) in enumerate(terms):
                                    if which == 0:
                                        vt_, vtt_ = vtiles[kb][hp]
                                        lhsT, lt = vt_[:], vtt_
                                    else:
                                        lhsT, lt = C['onespad'][:, hp * 128:(hp + 1) * 128], CT('onespad')
                                    rhs = pts[hp][0][:, (uj * 2 + kb) * 128:(uj * 2 + kb + 1) * 128]
                                    p.op('pe', lambda e, bank=bank, u=u, lhsT=lhsT, rhs=rhs, n_=n_, nn=len(terms): e.matmul(
                                        cx.ps[bank][:, u * 128:(u + 1) * 128], lhsT=lhsT, rhs=rhs, start=(n_ == 0), stop=(n_ == nn - 1)),
                                        reads=[lt, pts[hp][1]], writes=[('ps', bank)])
                    for acc, bank, an in ((accN, nb, 'accN'), (accD, db, 'accD')):
                        if r == 1:
                            va = acc[:, qi * 512:(qi + 1) * 512]
                            pv = cx.ps[bank][:, :]
                        elif r == 4:
                            va = acc[:, qi * 512:(qi + 1) * 512].rearrange("p (i c) -> p c i", c=4)
                            pv = cx.ps[bank][:, :].rearrange("p (c i) -> p c i", c=4)
                        else:
                            va = acc[:, :].rearrange("p (i c) -> p c i", c=16)[:, qi * 4:(qi + 1) * 4, :]
                            pv = cx.ps[bank][:, :].rearrange("p (c i) -> p c i", c=4)
                        if r == 1:
                            p.op('dve', lambda e, va=va, pv=pv: e.tensor_copy(out=va, in_=pv),
                                 reads=[('ps', bank)], writes=[an])
                        else:
                            p.op('dve', lambda e, va=va, pv=pv: e.tensor_tensor(out=va, in0=va, in1=pv, op=ALU.add),
                                 reads=[('ps', bank), an], writes=[an])
            p.op('dve', lambda e: e.reciprocal(out=accD[:], in_=accD[:]), reads=['accD'], writes=['accD'])
            p.op('dve', lambda e: e.tensor_tensor(out=oT[:], in0=accN[:], in1=accD[:], op=ALU.mult),
                 reads=['accN', 'accD'], writes=['oT1'])
            if out_pair is None:
                p.dma('pool', d['oT'][pr * 128:(pr + 1) * 128, M * ST:(M + 1) * ST], oT[:], reads=['oT1'], writes=['oT_out'], key='m1out')
            else:
                out_pair(M, pr, oT, ['oT1'])


def m1_host_inputs(inp, g):
    w = inp['c_w_qkv'][0]
    cols = np.concatenate([np.arange(k * 2048 + g * 512, k * 2048 + (g + 1) * 512) for k in range(3)])
    return dict(w_qkv=np.ascontiguousarray(w[:, cols]),
                qk_gain=np.ascontiguousarray(np.stack([np.tile(inp['c_q_norm'][0], 2), np.tile(inp['c_k_norm'][0], 2)], 1)).astype(np.float32))


M1_CONSTS = ['ident_b', 'dilmask', 'blockones', 'onespad']


def build_mixer1_program(S):
    nc = bass.Bass("TRN2", target_bir_lowering=False)
    d = {}
    d['hT'] = nc.dram_tensor("hT", [D_MODEL, S], BF16, kind="ExternalInput").ap()
    d['w_qkv'] = nc.dram_tensor("w_qkv", [D_MODEL, 1536], F32, kind="ExternalInput").ap()
    d['qk_gain'] = nc.dram_tensor("qk_gain", [128, 2], F32, kind="ExternalInput").ap()
    d['oT'] = nc.dram_tensor("oT", [512, S], BF16, kind="ExternalOutput").ap()
    p = Prog(nc)
    cx = Ctx(p)
    load_consts(cx, nc, M1_CONSTS)
    emit_mixer1(cx, S, d)
    p.emit(final_wait_keys=['m1out'])
    return nc


I32 = mybir.dt.int32
GROUPS = [[0, 1, 2, 3], [4, 5, 6, 7]]
ALL_CONSTS = ['ident_f', 'ident_b', 'U', 'negU', 'ones_f', 'negmask4', 'swamask', 'dilmask', 'blockones', 'onespad']


def build_fused_program(S, debug=False):
    TSEG = S // 4
    nt_all = S // 512
    ntseg = TSEG // 512
    nst = S // 2048
    nc = bass.Bass("TRN2", target_bir_lowering=False)
    ext = lambda n, shp, dt=F32: nc.dram_tensor(n, list(shp), dt, kind="ExternalInput").ap()
    itn = lambda n, shp, dt=BF16: nc.dram_tensor(n, list(shp), dt, kind="Internal").ap()
    d = {}
    d['xT'] = ext("xT", [D_MODEL, S])
    xTs = ext("xTs", [D_MODEL, TSEG])
    d['w_in'] = ext("w_in", [D_MODEL, M0_NCOL])
    d['g_mix'] = ext("g_mix", [D_MODEL])
    for n, shp in M0_SMALL.items():
        d[n] = ext(n, shp)
    rank = ext("rank", [1, 1], I32)
    w_out0 = ext("w_out0", [3072, D_MODEL])
    w_up0 = ext("w_up0", [D_MODEL, D_FF])
    w_dn0 = ext("w_dn0", [D_FF, D_MODEL])
    g_ffn0 = ext("g_ffn0", [D_MODEL])
    g_mix1 = ext("g_mix1", [D_MODEL])
    d1 = {}
    d1['w_qkv'] = ext("w_qkv", [D_MODEL, 1536])
    d1['qk_gain'] = ext("qk_gain1", [128, 2])
    w_out1 = ext("w_out1", [2048, D_MODEL])
    w_up1 = ext("w_up1", [D_MODEL, D_FF])
    w_dn1 = ext("w_dn1", [D_FF, D_MODEL])
    g_ffn1 = ext("g_ffn1", [D_MODEL])
    outT = nc.dram_tensor("outT", [D_MODEL, TSEG], F32, kind="ExternalOutput").ap()
    sc = []
    for l, EC in ((0, 24), (1, 16)):
        sc.append((itn(f"s_out{l}", [4, 128, EC * 512]), itn(f"s_up{l}", [NFB, 128, 16 * 512]),
                   itn(f"s_dn{l}", [16, 128, FC * 128])))
    mix0_loc = itn("mix0_loc", [nt_all, 768, 512])
    G0 = itn("G0", [nt_all, 4 * 768, 512])
    x1T = itn("x1T", [D_MODEL, TSEG], F32)
    h1_loc = itn("h1_loc", [ntseg, 2, 1024, 512])
    G1 = itn("G1", [ntseg, 2, 4 * 1024, 512])
    o_loc = itn("o_loc", [nst, 4, 128, 2048])
    G2 = itn("G2", [nst, 4, 4 * 128, 2048])

    if debug:
        dbg_G0 = nc.dram_tensor("dbg_G0", [4 * 768, 512], BF16, kind="ExternalOutput").ap()
        dbg_x1 = nc.dram_tensor("dbg_x1", [D_MODEL, TSEG], F32, kind="ExternalOutput").ap()
        dbg_G1 = nc.dram_tensor("dbg_G1", [4 * 1024, 512], BF16, kind="ExternalOutput").ap()
        dbg_G2 = nc.dram_tensor("dbg_G2", [4 * 128, 2048], BF16, kind="ExternalOutput").ap()
        dbg_mix = nc.dram_tensor("dbg_mix", [768, 512], BF16, kind="ExternalOutput").ap()
    p = Prog(nc)
    cx = Ctx(p)
    load_consts(cx, nc, ALL_CONSTS)

    def rankval(e, scale):
        k = (id(e), 'rank')
        if k not in p.regs:
            r = e.alloc_register(f"rk{p.phase}")
            e.reg_load(r, rank[0:1, 0:1])
            p.regs[k] = e.snap(e.snap(r, min_val=0, max_val=3) * scale, min_val=0, max_val=3 * scale)
        return p.regs[k]

    p.begin_phase()
    casts = (cast_dense_weight_ops(p, w_out0, w_up0, w_dn0, *sc[0], 24, 'L0') +
             cast_dense_weight_ops(p, w_out1, w_up1, w_dn1, *sc[1], 16, 'L1'))
    per = -(-len(casts) // max(1, nt_all - 2))

    def per_tile(t):
        for _ in range(per):
            if casts:
                casts.pop(0)()

    def out_tile0(t, mst, reads):
        p.dma('act', mix0_loc[t].rearrange("(c p) t -> p c t", p=128), mst[:], reads=reads,
              writes=[('mix0_loc', t)], key='m0out')
        p.cc("AllGather", mix0_loc[t], G0[t], GROUPS, reads=[('mix0_loc', t)], writes=[('G0', t)], key='cc0')

    emit_mixer0(cx, S, d, out_tile=out_tile0, per_tile=per_tile)
    while casts:
        casts.pop(0)()
    p.end_phase()

    p.begin_phase()
    if debug:
        p.dma('sp', dbg_G0, G0[0], writes=['dbg0'], key='dbg')
        p.dma('sp', dbg_mix, mix0_loc[0], writes=['dbg0m'], key='dbg')

    stg0 = itn("stg0", [2, 4 * 768, 512])

    def mix_load0(t, mt, tok):
        sslot = t % 2

        def fn(e):
            src = G0[t:][bass.ds(rankval(e, ntseg), 1)]
            return e.dma_start(out=stg0[sslot].rearrange("(a b) t -> a b t", a=128),
                               in_=src.rearrange("o (a b) t -> a (o b) t", a=128))
        p.op('sp', fn, writes=[('stg0', sslot)], dma_key='L0stg')
        for r in range(4):
            p.dma('sp', mt[:, r * 2:(r + 1) * 2, :],
                  stg0[sslot, r * 768:r * 768 + 256, :].rearrange("(c p) t -> p c t", p=128),
                  reads=[('stg0', sslot)], writes=[(tok, 'a', r)], key='L0min')
            p.dma('sp', mt[:, 8 + r * 4:8 + (r + 1) * 4, :],
                  stg0[sslot, r * 768 + 256:(r + 1) * 768, :].rearrange("(c p) t -> p c t", p=128),
                  reads=[('stg0', sslot)], writes=[tok] if r == 3 else [(tok, 'y', r)], key='L0min')

    def hn_store0(t, h2, reads):
        for hf in range(2):
            p.dma('act', h1_loc[t, hf].rearrange("(c p) t -> p c t", p=128), h2[:, hf * 8:(hf + 1) * 8, :], reads=reads,
                  writes=[('h1_loc', t, hf)], key='L0hout')
            p.cc("AllGather", h1_loc[t, hf], G1[t, hf], GROUPS, reads=[('h1_loc', t, hf)], writes=[('G1', t, hf)], key='cc1')

    emit_dense(cx, TSEG, 24, xTs, None, *sc[0], g_ffn0, x1T, 'L0', gain_next=g_mix1, mix_load=mix_load0,
               hn_store=hn_store0, aq='act')
    p.end_phase()

    p.begin_phase()
    if debug:
        p.dma('sp', dbg_x1, x1T, writes=['dbg1'], key='dbg')
        p.dma('sp', dbg_G1, G1[0, 0], writes=['dbg2'], key='dbg')

    def h_src(t, c4):
        r, lt = divmod(t, ntseg)
        return G1[lt, c4 // 2, r * 1024 + (c4 % 2) * 512: r * 1024 + (c4 % 2) * 512 + 512, :]

    def out_pair1(M, pr, oT, reads):
        p.dma('act', o_loc[M, pr], oT[:], reads=reads, writes=[('o_loc', M, pr)], key='m1out')
        p.cc("AllGather", o_loc[M, pr], G2[M, pr], GROUPS, reads=[('o_loc', M, pr)], writes=[('G2', M, pr)], key='cc2')

    emit_mixer1(cx, S, d1, h_src=h_src, out_pair=out_pair1)
    p.end_phase()

    p.begin_phase()
    if debug:
        p.dma('sp', dbg_G2, G2[0, 0], writes=['dbg3'], key='dbg')
    stseg = TSEG // 2048

    stg2 = itn("stg2", [2, 4, 512, 2048])

    def mix_load1(t, mt, tok):
        j = t // 4
        sslot = j % 2
        if t % 4 == 0:
            def fn(e):
                src = G2[j:][bass.ds(rankval(e, stseg), 1)]
                return e.dma_start(out=stg2[sslot].rearrange("q (a b) t -> a q (b t)", a=128),
                                   in_=src.rearrange("o q (a b) t -> a (o q) (b t)", a=128))
            p.op('sp', fn, writes=[('stg2', sslot)], dma_key='L1stg')
        for r in range(4):
            p.dma('sp', mt[:, r * 4:(r + 1) * 4, :],
                  stg2[sslot, :, r * 128:(r + 1) * 128, (t % 4) * 512:(t % 4 + 1) * 512].rearrange("q p t -> p q t"),
                  reads=[('stg2', sslot)], writes=[tok] if r == 3 else [(tok, 'q', r)], key='L1min')

    emit_dense(cx, TSEG, 16, x1T, None, *sc[1], g_ffn1, outT, 'L1', mix_load=mix_load1, aq='act')
    p.end_phase()
    p.close()
    return nc


BATCH = 2
SEQ = 16384
NCORES = 8
_PROGS = {}


def _prog(name, fn, *a):
    key = (name,) + tuple(a)
    if key not in _PROGS:
        _PROGS[key] = fn(*a)
    return _PROGS[key]


def fused_in_maps(inp, S):
    TSEG = S // 4
    x = inp['x']
    xT = [np.ascontiguousarray(x[b].T) for b in range(x.shape[0])]
    cst = const_inputs(ALL_CONSTS)
    maps = []
    for c in range(NCORES):
        b, g = divmod(c, 4)
        m = m0_host_inputs(inp, b, g, S)
        m['xT'] = xT[b]
        m['xTs'] = np.ascontiguousarray(xT[b][:, g * TSEG:(g + 1) * TSEG])
        m['rank'] = np.array([[g]], np.int32)
        m1 = m1_host_inputs(inp, g)
        m['w_qkv'] = m1['w_qkv']
        m['qk_gain1'] = m1['qk_gain']
        m.update(w_out0=inp['ab_w_out'][0], w_up0=inp['w_up'][0], w_dn0=inp['w_down'][0], g_ffn0=inp['norm_ffn'][0],
                 g_mix1=inp['norm_mix'][1], w_out1=inp['c_w_o'][0], w_up1=inp['w_up'][1], w_dn1=inp['w_down'][1],
                 g_ffn1=inp['norm_ffn'][1])
        m.update(cst)
        maps.append(m)
    return maps


def kernel(**inp):
    inp = {k: np.asarray(v) for k, v in inp.items()}
    S = inp['x'].shape[1]
    TSEG = S // 4
    nc = _prog('fused', build_fused_program, S)
    res = run_bass_kernel_spmd(nc, fused_in_maps(inp, S), core_ids=list(range(NCORES))).results
    out = np.empty((inp['x'].shape[0], S, D_MODEL), np.float32)
    for c in range(NCORES):
        b, g = divmod(c, 4)
        out[b, g * TSEG:(g + 1) * TSEG, :] = np.asarray(res[c]['outT']).T
    return out
```

```python
import bisect
import contextlib
import numpy as np
import ml_dtypes
import concourse.bass as bass
import concourse.mybir as mybir
from concourse.bass_utils import run_bass_kernel_spmd

F32 = mybir.dt.float32
BF16 = mybir.dt.bfloat16
AF = mybir.ActivationFunctionType
ALU = mybir.AluOpType
AX = mybir.AxisListType

D_MODEL = 2048
DC = D_MODEL // 128
D_FF = 8192
FC = D_FF // 128
NFB = D_FF // 512
EPS = 1e-5
SAME_ENGINE_SYNC = True
SCHEDULE = True


class Prog:
    ENGS = ("pe", "act", "dve", "pool")

    def __init__(self, nc):
        self.nc = nc
        self.ops = []
        self.top = contextlib.ExitStack()
        self.pstack = None
        self.last_w = {}
        self.readers = {}
        self.n_sb = 0
        self.cnt = {e: 0 for e in self.ENGS}
        self.dma_cum = {}
        self.dma_lists = {}
        self.sems = {e: self.top.enter_context(nc.semaphore("s_" + e)) for e in self.ENGS}
        self.bar = self.top.enter_context(nc.semaphore("s_bar"))
        self.dsems = {}
        self.phase = 0
        self.phase_start = 0
        self.regs = {}

    def _stk(self):
        return self.pstack if self.pstack is not None else self.top

    def sbuf(self, shape, dt, name=None):
        self.n_sb += 1
        return self._stk().enter_context(
            self.nc.sbuf_tensor(name or f"sb{self.n_sb}", list(shape), dt))

    def psum(self, shape, dt, name=None):
        self.n_sb += 1
        return self._stk().enter_context(
            self.nc.psum_tensor(name or f"ps{self.n_sb}", list(shape), dt))

    @staticmethod
    def _is_psum(t):
        return (isinstance(t, tuple) and t and t[0] in ('ps', 'psbf')) or (isinstance(t, str) and t.startswith('psbf'))

    def op(self, eng, fn, reads=(), writes=(), dma_key=None, inc=16):
        extra = [('psacc', t) for t in reads if self._is_psum(t)]
        if extra:
            writes = list(writes) + extra
        raw, war = set(), set()
        for b in reads:
            if b in self.last_w:
                raw.add(self.last_w[b])
        for b in writes:
            if b in self.last_w:
                war.add(self.last_w[b])
            war.update(self.readers.get(b, ()))
        idx = len(self.ops)
        war -= raw
        self.ops.append(dict(eng=eng, fn=fn, raw=raw, war=war, dma_key=dma_key,
                             needed=False, cnt=None, inc=inc))
        for b in reads:
            self.readers.setdefault(b, []).append(idx)
        for b in writes:
            self.last_w[b] = idx
            self.readers[b] = []
        return idx

    def dma(self, q, out, in_, reads=(), writes=(), key=None, **kw):
        assert key is not None
        return self.op(q, lambda e: e.dma_start(out=out, in_=in_, **kw),
                       reads=reads, writes=writes, dma_key=key)

    def cc(self, kind, in_ap, out_ap, groups, reads=(), writes=(), key=None):
        return self.op('pool', lambda e: e.collective_compute(kind, ALU.bypass, replica_groups=groups,
                                                              ins=[in_ap], outs=[out_ap]),
                       reads=reads, writes=writes, dma_key=key, inc=1)

    def _synced(self, x, y, kind):
        if x["eng"] != y["eng"]:
            return True
        if x["eng"] in ("pe", "sp"):
            return False
        return SAME_ENGINE_SYNC and kind == "raw"

    def begin_phase(self):
        self.pstack = contextlib.ExitStack()
        self.regs = {}

    COST = dict(pe=0.16, act=0.45, dve=0.4, pool=0.7, sp=0.15)

    def _schedule(self):
        import heapq
        ops = self.ops
        ps = self.phase_start
        n = len(ops) - ps
        if n == 0 or not SCHEDULE:
            return
        preds = [None] * n
        succs = [[] for _ in range(n)]
        npred = [0] * n
        for i in range(n):
            y = ops[ps + i]
            pr = set(x - ps for x in y["raw"] if x >= ps) | set(x - ps for x in y["war"] if x >= ps)
            preds[i] = pr
            npred[i] = len(pr)
            for x in pr:
                succs[x].append(i)
        engs = ("pe", "act", "dve", "pool", "sp")
        efree = {e: 0.0 for e in engs}
        byready = {e: [] for e in engs}
        byidx = {e: [] for e in engs}
        finish = [0.0] * n
        ready = [0.0] * n
        for i in range(n):
            if npred[i] == 0:
                heapq.heappush(byready[ops[ps + i]["eng"]], (0.0, i))
        order = []
        lo = 0
        done = [False] * n
        while len(order) < n:
            best = None
            for e in engs:
                hr, hi = byready[e], byidx[e]
                while hr and hr[0][0] <= efree[e]:
                    heapq.heappush(hi, heapq.heappop(hr)[1])
                if hi:
                    cand = (efree[e], hi[0], e, True)
                elif hr:
                    cand = (hr[0][0], hr[0][1], e, False)
                else:
                    continue
                if best is None or cand[:2] < best[:2]:
                    best = cand
            st, i, e, fromidx = best
            if fromidx:
                heapq.heappop(byidx[e])
            else:
                heapq.heappop(byready[e])
            y = ops[ps + i]
            cost = y.get("cost") or self.COST[e]
            if y["dma_key"] is not None:
                efree[e] = st + 0.12
                fin = st + (y.get("cost") or (80.0 if y["inc"] == 1 else 3.0))
            else:
                efree[e] = st + cost
                fin = st + cost
            finish[i] = fin
            done[i] = True
            order.append(i)
            for sidx in succs[i]:
                lat = 0.15 if (ops[ps + sidx]["eng"] == e and y["dma_key"] is None) else 1.2
                r_ = fin + lat
                if r_ > ready[sidx]:
                    ready[sidx] = r_
                npred[sidx] -= 1
                if npred[sidx] == 0:
                    heapq.heappush(byready[ops[ps + sidx]["eng"]], (ready[sidx], sidx))
        remap = {ps + old: ps + new for new, old in enumerate(order)}
        newops = [ops[ps + old] for old in order]
        for y in newops:
            y["raw"] = set(remap.get(x, x) for x in y["raw"])
            y["war"] = set(remap.get(x, x) for x in y["war"])
        ops[ps:] = newops
        self.last_w = {k: remap.get(v, v) for k, v in self.last_w.items()}
        self.readers = {k: [remap.get(v, v) for v in vs] for k, vs in self.readers.items()}

    def end_phase(self):
        nc = self.nc
        ops = self.ops
        ps = self.phase_start
        self._schedule()
        last = {}
        for i in range(ps, len(ops)):
            y = ops[i]
            if y["dma_key"] is None:
                last[y["eng"]] = i
            for kind in ("raw", "war"):
                for xi in y[kind]:
                    if xi < ps:
                        continue
                    x = ops[xi]
                    if x["dma_key"] is None and self._synced(x, y, kind):
                        x["needed"] = True
        for e, i in last.items():
            if e in self.ENGS:
                ops[i]["needed"] = True
        for i in range(ps, len(ops)):
            x = ops[i]
            if x["dma_key"] is not None:
                k = x["dma_key"]
                self.dma_cum[k] = self.dma_cum.get(k, 0) + x["inc"]
                self.dma_lists.setdefault(k, ([], []))
                self.dma_lists[k][0].append(i)
                self.dma_lists[k][1].append(self.dma_cum[k])
                if k not in self.dsems:
                    self.dsems[k] = self.top.enter_context(nc.semaphore("d_" + str(k)))
            elif x["needed"]:
                self.cnt[x["eng"]] += 1
                x["cnt"] = self.cnt[x["eng"]]
        dma_lists, dsems, sems = self.dma_lists, self.dsems, self.sems
        phase = self.phase

        def waits_for(yi):
            y = ops[yi]
            w = {}
            for kind in ("raw", "war"):
                for xi in y[kind]:
                    if xi < ps:
                        continue
                    x = ops[xi]
                    if x["dma_key"] is not None:
                        k = x["dma_key"]
                        idxs, cums = dma_lists[k]
                        pos = bisect.bisect_left(idxs, yi)
                        v = cums[pos - 1]
                        key = ("d", k)
                    else:
                        if not self._synced(x, y, kind):
                            continue
                        v = x["cnt"]
                        key = ("c", x["eng"])
                    if w.get(key, 0) < v:
                        w[key] = v
            return w

        def emit_engine(ename, e):
            seen = {}
            if phase > 0:
                e.wait_ge(self.bar, phase)
            for yi in range(ps, len(ops)):
                y = ops[yi]
                if y["eng"] != ename:
                    continue
                for key, v in waits_for(yi).items():
                    if seen.get(key, 0) >= v:
                        continue
                    seen[key] = v
                    sm = dsems[key[1]] if key[0] == "d" else sems[key[1]]
                    e.wait_ge(sm, v)
                ins = y["fn"](e)
                if y["dma_key"] is not None:
                    ins.then_inc(dsems[y["dma_key"]], y["inc"])
                elif y["needed"]:
                    ins.then_inc(sems[ename], 1)
            if ename == "sp":
                for k, v in self.dma_cum.items():
                    e.wait_ge(dsems[k], v)
                for en in self.ENGS:
                    if self.cnt[en] > 0:
                        e.wait_ge(sems[en], self.cnt[en])
                e.sem_inc(self.bar, 1)

        with nc.Block() as block:
            @block.sync
            def _(e):
                emit_engine("sp", e)

            @block.tensor
            def _(e):
                emit_engine("pe", e)

            @block.scalar
            def _(e):
                emit_engine("act", e)

            @block.vector
            def _(e):
                emit_engine("dve", e)

            @block.gpsimd
            def _(e):
                emit_engine("pool", e)
        if self.pstack is not None:
            self.pstack.close()
            self.pstack = None
        self.phase += 1
        self.phase_start = len(ops)

    def emit(self, final_wait_keys=()):
        self.end_phase()
        self.top.close()

    def close(self):
        self.top.close()


class Ctx:
    def __init__(self, p):
        self.p = p
        self.ps = [p.psum([128, 512], F32, name=f"psb{i}") for i in range(6)]
        self.psbf2 = [p.psum([128, 1024], BF16, name=f"psbf{i}") for i in range(2)]
        self.ps_rr = 0
        self.ones_bf = p.sbuf([128, 128], BF16, name="ones_bf")
        self.eps = p.sbuf([128, 1], F32, name="eps_t")
        p.op('pool', lambda e: e.memset(self.ones_bf[:], 1.0), writes=['ones_bf'])
        p.op('pool', lambda e: e.memset(self.eps[:], EPS), writes=['eps_t'])
        self.uid = 0

    def bank(self, lo=0, hi=6):
        i = lo + self.ps_rr % (hi - lo)
        self.ps_rr += 1
        return i

    def u(self):
        self.uid += 1
        return self.uid


def emit_rmsnorm_fm(cx, xt, xtok, nchunk, width, gain, out_bf, out_tok, sq_ring, rstd, dim, tmp_sqrt):
    p = cx.p
    b = cx.bank()
    ps = cx.ps[b]
    for c in range(nchunk):
        sq = sq_ring[c % len(sq_ring)]
        sqt = ('sq', id(sq_ring), c % len(sq_ring))
        p.op('act', lambda e, c=c, sq=sq: e.activation(out=sq[:, :width], in_=xt[:, c, :width], func=AF.Square),
             reads=[xtok(c)], writes=[sqt])
        p.op('pe', lambda e, c=c, sq=sq: e.matmul(ps[:, :width], lhsT=cx.ones_bf[:], rhs=sq[:, :width],
                                                   start=(c == 0), stop=(c == nchunk - 1)),
             reads=[sqt, 'ones_bf'], writes=[('ps', b)])
    p.op('act', lambda e: e.activation(out=tmp_sqrt[:, :width], in_=ps[:, :width], func=AF.Ln,
                                       bias=cx.eps[:, 0:1], scale=1.0 / dim),
         reads=[('ps', b), 'eps_t'], writes=[('tmp_sqrt', id(tmp_sqrt))])
    p.op('act', lambda e: e.activation(out=rstd[:, :width], in_=tmp_sqrt[:, :width], func=AF.Exp, scale=-0.5),
         reads=[('tmp_sqrt', id(tmp_sqrt))], writes=[('rstd', id(rstd))])
    for c in range(nchunk):
        p.op('dve', lambda e, c=c: e.scalar_tensor_tensor(out=out_bf[:, c, :width], in0=xt[:, c, :width],
                                                          scalar=gain[:, c:c + 1], in1=rstd[:, :width],
                                                          op0=ALU.mult, op1=ALU.mult),
             reads=[xtok(c), ('rstd', id(rstd)), 'gains'], writes=[out_tok(c)])


def emit_rmsnorm_stream(cx, src, nchunk, width, gain, out_bf, out_tok, xring, sq_ring, rstd, tmpl, dim, q, key):
    p = cx.p
    b = cx.bank()
    ps = cx.ps[b]
    for c in range(nchunk):
        xr, xrt = xring.next()
        p.dma(q, xr[:, :width], src(c), writes=[xrt], key=key)
        sq = sq_ring[c % len(sq_ring)]
        sqt = ('sq', id(sq_ring), c % len(sq_ring))
        p.op('act', lambda e, xr=xr, sq=sq: e.activation(out=sq[:, :width], in_=xr[:, :width], func=AF.Square),
             reads=[xrt], writes=[sqt])
        p.op('pe', lambda e, c=c, sq=sq: e.matmul(ps[:, :width], lhsT=cx.ones_bf[:], rhs=sq[:, :width],
                                                   start=(c == 0), stop=(c == nchunk - 1)),
             reads=[sqt, 'ones_bf'], writes=[('ps', b)])
    p.op('act', lambda e: e.activation(out=tmpl[:, :width], in_=ps[:, :width], func=AF.Ln,
                                       bias=cx.eps[:, 0:1], scale=1.0 / dim),
         reads=[('ps', b), 'eps_t'], writes=[('tmpl', id(tmpl))])
    p.op('act', lambda e: e.activation(out=rstd[:, :width], in_=tmpl[:, :width], func=AF.Exp, scale=-0.5),
         reads=[('tmpl', id(tmpl))], writes=[('rstd', id(rstd))])
    for c in range(nchunk):
        xr, xrt = xring.next()
        p.dma(q, xr[:, :width], src(c), writes=[xrt], key=key)
        p.op('dve', lambda e, c=c, xr=xr: e.scalar_tensor_tensor(out=out_bf[:, c, :width], in0=xr[:, :width],
                                                                 scalar=gain[:, c:c + 1], in1=rstd[:, :width],
                                                                 op0=ALU.mult, op1=ALU.mult),
             reads=[xrt, ('rstd', id(rstd)), 'gains'], writes=[out_tok(c)])


def cast_dense_weight_ops(p, w_out, w_up, w_dn, s_out, s_up, s_dn, EC, tag):
    ops = []
    for db in range(4):
        ops.append(lambda db=db: p.dma('pool', s_out[db].rearrange("p (c j) -> p c j", j=512),
                                       w_out[:, db * 512:(db + 1) * 512].rearrange("(c p) j -> p c j", p=128),
                                       writes=[(tag, 's_out', db)], key='cast'))
    for fb in range(NFB):
        ops.append(lambda fb=fb: p.dma('pool', s_up[fb].rearrange("p (c j) -> p c j", j=512),
                                       w_up[:, fb * 512:(fb + 1) * 512].rearrange("(c p) j -> p c j", p=128),
                                       writes=[(tag, 's_up', fb)], key='cast'))
    rows = min(2048, D_FF)
    for db in range(16):
        for q in range(D_FF // rows):
            ops.append(lambda db=db, q=q: p.dma(
                'pool', s_dn[db][:, q * rows:(q + 1) * rows].rearrange("p (c j) -> p c j", j=128),
                w_dn[q * rows:(q + 1) * rows, db * 128:(db + 1) * 128].rearrange("(c p) j -> p c j", p=128),
                writes=[(tag, 's_dn', db, q)], key='cast'))
    return ops


def emit_cast_dense_weights(p, w_out, w_up, w_dn, s_out, s_up, s_dn, EC, tag):
    for f in cast_dense_weight_ops(p, w_out, w_up, w_dn, s_out, s_up, s_dn, EC, tag):
        f()


def emit_dense(cx, T, EC, xT, mixT, s_out, s_up, s_dn, gain_ffn, outT, tag,
               gain_next=None, hnT=None, TT=512, mix_load=None, hn_store=None, aq='pool'):
    p = cx.p
    nt = T // TT
    xt = p.sbuf([128, DC, TT], F32, name=tag + "x")
    mt = p.sbuf([128, EC, TT], BF16, name=tag + "mix")
    h2 = p.sbuf([128, DC, TT], BF16, name=tag + "h2")
    at = p.sbuf([128, FC, TT], BF16, name=tag + "a")
    WR = 24 * 512
    wring = [p.sbuf([128, WR], BF16, name=tag + f"w{i}") for i in range(2)]
    stage = [p.sbuf([128, TT], F32, name=tag + f"st{i}") for i in range(2)]
    sq_ring = [p.sbuf([128, TT], BF16, name=tag + f"sq{i}") for i in range(2)]
    sqf = [p.sbuf([128, TT], F32, name=tag + f"sqf{i}") for i in range(2)]
    rstd = p.sbuf([128, TT], F32, name=tag + "rstd")
    tmp_sqrt = p.sbuf([128, TT], F32, name=tag + "tsq")
    g1 = p.sbuf([128, DC], F32, name=tag + "g1")
    p.dma('sp', g1[:], gain_ffn.rearrange("(c p) -> p c", p=128), writes=['gains'], key=tag + 'g',
          allow_slow_non_contiguous=True)
    if gain_next is not None:
        g2 = p.sbuf([128, DC], F32, name=tag + "g2")
        p.dma('sp', g2[:], gain_next.rearrange("(c p) -> p c", p=128), writes=['gains'], key=tag + 'g',
              allow_slow_non_contiguous=True)

    blocks = []
    for db in range(4):
        blocks.append(('out', db, s_out[db], EC * 512, [(tag, 's_out', db)]))
    for fb in range(NFB):
        blocks.append(('up', fb, s_up[fb], 16 * 512, [(tag, 's_up', fb)]))
    for db in range(16):
        blocks.append(('dn', db, s_dn[db], FC * 128, [(tag, 's_dn', db, q) for q in range(D_FF // min(2048, D_FF))]))
    nblk = len(blocks)
    wcount = [0]

    def load_block(gi):
        kind, bi, src, n, tok = blocks[gi % nblk]
        slot = gi % 2
        p.dma('sp', wring[slot][:, :n], src, reads=tok, writes=[(tag, 'wr', slot)], key=tag + f'wr{slot}')

    xtok = lambda c: (tag, 'x', c)
    total_blocks = nt * nblk
    load_block(0)
    gi = 0
    for t in range(nt):
        ts = slice(t * TT, (t + 1) * TT)
        for c in range(DC):
            p.dma(aq, xt[:, c, :], xT[c * 128:(c + 1) * 128, ts], writes=[xtok(c)], key=tag + 'xin')
        if mix_load is None:
            p.dma(aq, mt[:], mixT[:, ts].rearrange("(c p) t -> p c t", p=128),
                  writes=[(tag, 'mix')], key=tag + 'min')
            mixtoks = [(tag, 'mix')]
        else:
            mixtoks = mix_load(t, mt, (tag, 'mix'))
        for db in range(4):
            if gi + 1 < total_blocks:
                load_block(gi + 1)
            slot = gi % 2
            w = wring[slot]
            for dj in range(4):
                dc = db * 4 + dj
                b = cx.bank()
                for ec in range(EC):
                    p.op('pe', lambda e, b=b, w=w, ec=ec, dj=dj: e.matmul(
                        cx.ps[b][:, :TT], lhsT=w[:, ec * 512 + dj * 128: ec * 512 + (dj + 1) * 128],
                        rhs=mt[:, ec, :], start=(ec == 0), stop=(ec == EC - 1)),
                        reads=[(tag, 'wr', slot)] + mixtoks, writes=[('ps', b)])
                p.op('dve', lambda e, b=b, dc=dc: e.tensor_tensor(out=xt[:, dc, :], in0=xt[:, dc, :],
                                                                   in1=cx.ps[b][:, :TT], op=ALU.add),
                     reads=[('ps', b), xtok(dc)], writes=[xtok(dc)])
            gi += 1
        emit_rmsnorm_fm(cx, xt, xtok, DC, TT, g1, h2, lambda c: (tag, 'h2', c), sq_ring, rstd, D_MODEL, tmp_sqrt)
        for fb in range(NFB):
            if gi + 1 < total_blocks:
                load_block(gi + 1)
            slot = gi % 2
            w = wring[slot]
            for fj in range(4):
                fc = fb * 4 + fj
                b = cx.bank()
                for dc in range(DC):
                    p.op('pe', lambda e, b=b, w=w, dc=dc, fj=fj: e.matmul(
                        cx.ps[b][:, :TT], lhsT=w[:, dc * 512 + fj * 128: dc * 512 + (fj + 1) * 128],
                        rhs=h2[:, dc, :], start=(dc == 0), stop=(dc == DC - 1)),
                        reads=[(tag, 'wr', slot), (tag, 'h2', dc)], writes=[('ps', b)])
                sf = sqf[fc % 2]
                p.op('act', lambda e, b=b, sf=sf: e.activation(out=sf[:], in_=cx.ps[b][:, :TT], func=AF.Square),
                     reads=[('ps', b)], writes=[(tag, 'sqf', fc % 2)])
                p.op('dve', lambda e, b=b, sf=sf, fc=fc: e.scalar_tensor_tensor(
                    out=at[:, fc, :], in0=cx.ps[b][:, :TT], scalar=0.0, in1=sf[:], op0=ALU.is_gt, op1=ALU.mult),
                    reads=[('ps', b), (tag, 'sqf', fc % 2)], writes=[(tag, 'a', fc)])
            gi += 1
        for db in range(16):
            if gi + 1 < total_blocks:
                load_block(gi + 1)
            slot = gi % 2
            w = wring[slot]
            b = cx.bank()
            for fc in range(FC):
                p.op('pe', lambda e, b=b, w=w, fc=fc: e.matmul(
                    cx.ps[b][:, :TT], lhsT=w[:, fc * 128:(fc + 1) * 128], rhs=at[:, fc, :],
                    start=(fc == 0), stop=(fc == FC - 1)),
                    reads=[(tag, 'wr', slot), (tag, 'a', fc)], writes=[('ps', b)])
            if hnT is None and hn_store is None:
                st = stage[db % 2]
                p.op('dve', lambda e, b=b, db=db, st=st: e.tensor_tensor(out=st[:], in0=xt[:, db, :],
                                                                         in1=cx.ps[b][:, :TT], op=ALU.add),
                     reads=[('ps', b), xtok(db)], writes=[(tag, 'st', db % 2)])
                p.dma(aq, outT[db * 128:(db + 1) * 128, ts], st[:], reads=[(tag, 'st', db % 2)],
                      writes=[(tag, 'outT')], key=tag + 'out')
            else:
                p.op('dve', lambda e, b=b, db=db: e.tensor_tensor(out=xt[:, db, :], in0=xt[:, db, :],
                                                                   in1=cx.ps[b][:, :TT], op=ALU.add),
                     reads=[('ps', b), xtok(db)], writes=[xtok(db)])
                p.dma(aq, outT[db * 128:(db + 1) * 128, ts], xt[:, db, :], reads=[xtok(db)],
                      writes=[(tag, 'outT')], key=tag + 'out')
            gi += 1
        if hnT is not None or hn_store is not None:
            emit_rmsnorm_fm(cx, xt, xtok, DC, TT, g2, h2, lambda c: (tag, 'h2', c), sq_ring, rstd, D_MODEL, tmp_sqrt)
            if hn_store is None:
                p.dma(aq, hnT[:, ts].rearrange("(c p) t -> p c t", p=128), h2[:],
                      reads=[(tag, 'h2', c) for c in range(DC)], writes=[(tag, 'hnT')], key=tag + 'hout')
            else:
                hn_store(t, h2, [(tag, 'h2', c) for c in range(DC)])


def build_dense_program(T, EC, with_hn, do_cast=True):
    nc = bass.Bass("TRN2", target_bir_lowering=False)
    E = EC * 128
    xT = nc.dram_tensor("xT", [D_MODEL, T], F32, kind="ExternalInput").ap()
    mixT = nc.dram_tensor("mixT", [E, T], BF16, kind="ExternalInput").ap()
    w_out = nc.dram_tensor("w_out", [E, D_MODEL], F32, kind="ExternalInput").ap()
    w_up = nc.dram_tensor("w_up", [D_MODEL, D_FF], F32, kind="ExternalInput").ap()
    w_dn = nc.dram_tensor("w_dn", [D_FF, D_MODEL], F32, kind="ExternalInput").ap()
    g_ffn = nc.dram_tensor("g_ffn", [D_MODEL], F32, kind="ExternalInput").ap()
    outT = nc.dram_tensor("outT", [D_MODEL, T], F32, kind="ExternalOutput").ap()
    g_next = hnT = None
    if with_hn:
        g_next = nc.dram_tensor("g_next", [D_MODEL], F32, kind="ExternalInput").ap()
        hnT = nc.dram_tensor("hnT", [D_MODEL, T], BF16, kind="ExternalOutput").ap()
    s_out = nc.dram_tensor("s_out", [4, 128, EC * 512], BF16, kind="Internal").ap()
    s_up = nc.dram_tensor("s_up", [NFB, 128, 16 * 512], BF16, kind="Internal").ap()
    s_dn = nc.dram_tensor("s_dn", [16, 128, FC * 128], BF16, kind="Internal").ap()
    p = Prog(nc)
    cx = Ctx(p)
    emit_cast_dense_weights(p, w_out, w_up, w_dn, s_out, s_up, s_dn, EC, 'L')
    emit_dense(cx, T, EC, xT, mixT, s_out, s_up, s_dn, g_ffn, outT, 'L', gain_next=g_next, hnT=hnT)
    p.emit(final_wait_keys=['Lout'] + (['Lhout'] if with_hn else []))
    return nc


M0_NCOL = 1736
QO, KO, XO, BO, CO, VO, DTO, ZO = 0, 256, 384, 896, 1024, 1152, 1216, 1224
NEG = -30000.0


class Ring:
    def __init__(self, p, n, shape, dt, name):
        self.tiles = [p.sbuf(shape, dt, name=f"{name}{i}") for i in range(n)]
        self.name = name
        self.i = 0

    def next(self):
        k = self.i % len(self.tiles)
        self.i += 1
        return self.tiles[k], (self.name, k)


def host_consts():
    bf = ml_dtypes.bfloat16
    i = np.arange(128)
    U = (i[:, None] <= i[None, :]).astype(np.float32)
    c = {}
    c['ident_f'] = np.eye(128, dtype=np.float32)
    c['ident_b'] = np.eye(128, dtype=np.float32).astype(bf)
    c['U'] = U
    c['negU'] = -U
    c['ones_f'] = np.ones((128, 128), np.float32)
    nm = np.where(i[None, :] < i[:, None], NEG, 0.0).astype(np.float32)
    c['negmask4'] = np.tile(nm, (1, 4)).astype(bf)
    prev_swa = (i[:, None] > i[None, :]).astype(np.float32)
    prev_dil = (i[:, None] >= i[None, :]).astype(np.float32)
    own = (i[:, None] <= i[None, :]).astype(np.float32)
    c['swamask'] = np.tile(np.concatenate([prev_swa, own], 1), (1, 2)).astype(bf)
    c['dilmask'] = np.tile(np.concatenate([prev_dil, own], 1), (1, 2)).astype(bf)
    bo = np.zeros((128, 128), np.float32)
    bo[:64, :64] = 1
    bo[64:, 64:] = 1
    c['blockones'] = bo.astype(bf)
    op = np.zeros((128, 256), np.float32)
    op[:, 0:64] = 1
    op[:, 128 + 64:256] = 1
    c['onespad'] = op.astype(bf)
    return c


CONST_SPECS = dict(ident_f=([128, 128], F32), ident_b=([128, 128], BF16), U=([128, 128], F32),
                   negU=([128, 128], F32), ones_f=([128, 128], F32), negmask4=([128, 512], BF16),
                   swamask=([128, 512], BF16), dilmask=([128, 512], BF16), blockones=([128, 128], BF16),
                   onespad=([128, 256], BF16))


def load_consts(cx, nc, names):
    p = cx.p
    cx.c = {}
    for n in names:
        shp, dt = CONST_SPECS[n]
        src = nc.dram_tensor("c_" + n, shp, dt, kind="ExternalInput").ap()
        t = p.sbuf(shp, dt, name="k_" + n)
        p.dma('sp', t[:], src, writes=[('const', n)], key='consts')
        cx.c[n] = t


def emit_mixer0(cx, S, d, tag='m0', out_tile=None, per_tile=None):
    p = cx.p
    C = cx.c
    TT = 512
    nt = S // TT
    CT = lambda n: ('const', n)
    W = p.sbuf([128, DC, M0_NCOL], BF16, name='m0W')
    for c4 in range(4):
        p.dma('pool', W[:, c4 * 4:(c4 + 1) * 4, :],
              d['w_in'][c4 * 512:(c4 + 1) * 512, :].rearrange("(c p) n -> p c n", p=128),
              writes=[('m0W', c4)], key='m0w')
    Wtok = [('m0W', c4) for c4 in range(4)]
    gmix = p.sbuf([128, DC], F32, name='m0gmix')
    p.dma('sp', gmix[:], d['g_mix'].rearrange("(c p) -> p c", p=128), writes=['gains'], key='m0s',
          allow_slow_non_contiguous=True)
    small = {}
    for n, shp in (('qk_gain', [128, 2]), ('sink2', [128, 2]), ('conv_w', [128, 6, 4]), ('conv_b', [128, 6]),
                   ('dtb', [128, 8]), ('alog', [128, 8]), ('dskip', [128, 8]), ('gate', [128, 512])):
        t = p.sbuf(shp, F32, name='m0_' + n)
        p.dma('sp', t[:], d[n], writes=[('m0s', n)], key='m0s')
        small[n] = t
    esink = p.sbuf([128, 2], F32, name='m0esink')
    p.op('act', lambda e: e.activation(out=esink[:], in_=small['sink2'][:], func=AF.Exp),
         reads=[('m0s', 'sink2')], writes=['esink'])
    a_bc = p.sbuf([128, 8], F32, name='m0a')
    p.op('act', lambda e: e.activation(out=a_bc[:], in_=small['alog'][:], func=AF.Exp),
         reads=[('m0s', 'alog')], writes=['a_bc0'])
    na_bc = p.sbuf([128, 8], F32, name='m0na')
    p.op('dve', lambda e: e.tensor_scalar(out=na_bc[:], in0=a_bc[:], scalar1=-1.0, scalar2=None, op0=ALU.mult),
         reads=['a_bc0'], writes=['a_bc'])

    xring = Ring(p, 4, [128, TT], F32, 'm0xr')
    hT = p.sbuf([128, DC, TT], BF16, name='m0h')
    sq_ring = [p.sbuf([128, TT], BF16, name=f'm0sq{i}') for i in range(2)]
    qT = p.sbuf([128, 2, TT], BF16, name='m0qT')
    kT = p.sbuf([128, 2, TT], BF16, name='m0kT')
    qsb = Ring(p, 1, [128, TT], F32, 'm0qsb')
    qsq = Ring(p, 2, [128, TT], BF16, 'm0qsq')
    qln = Ring(p, 1, [128, TT], F32, 'm0qln')
    qrs = Ring(p, 1, [128, TT], F32, 'm0qrs')
    rstd = qrs.tiles[0]
    tmpl = qln.tiles[0]
    cinr = Ring(p, 2, [128, TT + 3], F32, 'm0cin')
    chist = p.sbuf([128, 6, 3], F32, name='m0chist')
    cacc = Ring(p, 2, [128, TT], F32, 'm0cacc')
    ctmp = Ring(p, 1, [128, TT], F32, 'm0ctmp')
    xsT = p.sbuf([128, 4, TT], F32, name='m0xsT')
    BT = p.sbuf([128, TT], BF16, name='m0BT')
    CTt = p.sbuf([128, TT], BF16, name='m0CT')
    Vpad = [p.sbuf([128, 8, 128], BF16, name=f'm0V{i}') for i in range(2)]
    praw = Ring(p, 2, [128, 512], BF16, 'm0praw')
    pT = [Ring(p, 2, [128, 512], BF16, f'm0pT{i}') for i in range(2)]
    tden = Ring(p, 1, [128, TT], F32, 'm0tden')
    tden2 = Ring(p, 1, [128, TT], F32, 'm0tden2')
    mixst = [p.sbuf([128, 6, TT], BF16, name='m0mix0')] * 2
    H = p.sbuf([128, 512], F32, name='m0H')
    Hbf = p.sbuf([128, 512], BF16, name='m0Hbf')
    t8 = Ring(p, 4, [128, 8], F32, 'm0t8')
    e8 = Ring(p, 2, [128, 8], F32, 'm0e8')
    dt8 = Ring(p, 2, [128, 8], F32, 'm0dt8')
    dA8 = Ring(p, 2, [128, 8], F32, 'm0dA8')
    dAU = Ring(p, 1, [128, 8, 128], F32, 'm0dAU')
    cs16 = Ring(p, 2, [128, 16], F32, 'm0cs16')
    d8 = Ring(p, 2, [128, 8], F32, 'm0d8')
    ed8 = Ring(p, 2, [128, 8], F32, 'm0ed8')
    w28 = Ring(p, 2, [128, 8], F32, 'm0w28')
    ecum8 = Ring(p, 2, [128, 8], F32, 'm0ecum8')
    dect8 = Ring(p, 2, [128, 8], F32, 'm0dect8')
    Esb = Ring(p, 2, [128, 512], F32, 'm0E')
    MT = Ring(p, 2, [128, 8, 128], BF16, 'm0MT')
    xtok = Ring(p, 1, [128, 512], F32, 'm0xtok')
    xdt = Ring(p, 2, [128, 8, 64], BF16, 'm0xdt')
    xdec = Ring(p, 2, [128, 512], BF16, 'm0xdec')
    xD = Ring(p, 1, [128, 512], F32, 'm0xD')
    Btok = Ring(p, 2, [128, 128], BF16, 'm0Btok')
    y1 = Ring(p, 2, [128, 512], F32, 'm0y1')
    sz = Ring(p, 1, [128, 512], F32, 'm0sz')
    junk = Ring(p, 1, [128, 512], BF16, 'm0junk')
    ss1 = Ring(p, 2, [128, 1], F32, 'm0ss1')
    ln1 = Ring(p, 2, [128, 1], F32, 'm0ln1')
    rs1 = Ring(p, 2, [128, 1], F32, 'm0rs1')
    ybf = Ring(p, 2, [128, 512], BF16, 'm0ybf')
    psbf, psbfy = cx.psbf2

    p.op('pool', lambda e: e.memset(chist[:], 0.0), writes=[('cinh', c6) for c6 in range(6)])
    p.op('pool', lambda e: e.memset(kT[:], 0.0), writes=[('kT', 0), ('kT', 1)])
    for i in range(2):
        p.op('pool', lambda e, i=i: e.memset(Vpad[i][:], 0.0), writes=[('Vpad', i, s) for s in range(8)])
    p.op('pool', lambda e: e.memset(H[:], 0.0), writes=['H'])
    p.op('pool', lambda e: e.memset(Hbf[:], 0.0), writes=['Hbf'])

    for t in range(nt):
        ts = slice(t * TT, (t + 1) * TT)
        sl = t % 2
        emit_rmsnorm_stream(cx, lambda c: d['xT'][c * 128:(c + 1) * 128, ts], DC, TT, gmix, hT, lambda c: ('m0h', c),
                            xring, sq_ring, rstd, tmpl, D_MODEL, 'sp', 'm0x')
        hreads = [('m0h', c) for c in range(DC)]
        mst = mixst[sl]
        mtok = lambda c: ('mixst', 0, c)

        def proj_fm(col0):
            b = cx.bank()
            for dc in range(DC):
                p.op('pe', lambda e, b=b, dc=dc: e.matmul(cx.ps[b][:, :TT], lhsT=W[:, dc, col0:col0 + 128],
                                                          rhs=hT[:, dc, :], start=(dc == 0), stop=(dc == DC - 1)),
                     reads=[('m0h', dc), Wtok[dc // 4]], writes=[('ps', b)])
            return b

        for qi in range(3):
            b = proj_fm(QO + qi * 128)
            qs, qst = qsb.next()
            sq, sqt = qsq.next()
            p.op('act', lambda e, b=b, qs=qs: e.activation(out=qs[:], in_=cx.ps[b][:, :TT], func=AF.Copy),
                 reads=[('ps', b)], writes=[qst])
            p.op('act', lambda e, b=b, sq=sq: e.activation(out=sq[:], in_=cx.ps[b][:, :TT], func=AF.Square),
                 reads=[('ps', b)], writes=[sqt])
            b2 = cx.bank()
            p.op('pe', lambda e, b2=b2, sq=sq: e.matmul(cx.ps[b2][:, :TT], lhsT=C['blockones'][:], rhs=sq[:],
                                                         start=True, stop=True),
                 reads=[sqt, CT('blockones')], writes=[('ps', b2)])
            ln, lnt = qln.next()
            rs, rst = qrs.next()
            p.op('act', lambda e, b2=b2, ln=ln: e.activation(out=ln[:], in_=cx.ps[b2][:, :TT], func=AF.Ln,
                                                              bias=cx.eps[:, 0:1], scale=1.0 / 64),
                 reads=[('ps', b2), 'eps_t'], writes=[lnt])
            p.op('act', lambda e, ln=ln, rs=rs: e.activation(out=rs[:], in_=ln[:], func=AF.Exp, scale=-0.5),
                 reads=[lnt], writes=[rst])
            if qi < 2:
                dst, dtok, gcol = qT[:, qi, :], ('qT', qi), 0
            else:
                dst, dtok, gcol = kT[:, sl, :], ('kT', sl), 1
            p.op('dve', lambda e, qs=qs, rs=rs, dst=dst, gcol=gcol: e.scalar_tensor_tensor(
                out=dst, in0=qs[:], scalar=small['qk_gain'][:, gcol:gcol + 1], in1=rs[:], op0=ALU.mult, op1=ALU.mult),
                reads=[qst, rst, ('m0s', 'qk_gain')], writes=[dtok])

        for c6 in range(6):
            b = proj_fm(XO + c6 * 128)
            cin, cint = cinr.next()
            p.op('act', lambda e, b=b, cin=cin: e.activation(out=cin[:, 3:3 + TT], in_=cx.ps[b][:, :TT], func=AF.Copy),
                 reads=[('ps', b)], writes=[(cint, 'd')])
            p.op('pool', lambda e, cin=cin, c6=c6: e.tensor_copy(out=cin[:, 0:3], in_=chist[:, c6, :]),
                 reads=[('cinh', c6)], writes=[(cint, 'h')])
            acc, acct = cacc.next()
            cw = small['conv_w']
            p.op('act', lambda e, acc=acc, cin=cin, c6=c6: e.activation(out=acc[:], in_=cin[:, 0:TT], func=AF.Copy, scale=cw[:, c6, 0:1]),
                 reads=[(cint, 'd'), (cint, 'h'), ('m0s', 'conv_w')], writes=[acct])
            for j in range(1, 4):
                p.op('dve', lambda e, acc=acc, cin=cin, c6=c6, j=j: e.scalar_tensor_tensor(
                    out=acc[:], in0=cin[:, j:j + TT], scalar=cw[:, c6, j:j + 1], in1=acc[:], op0=ALU.mult, op1=ALU.add),
                    reads=[(cint, 'd'), (cint, 'h'), acct], writes=[acct])
            p.op('pool', lambda e, cin=cin, c6=c6: e.tensor_copy(out=chist[:, c6, :], in_=cin[:, TT:TT + 3]),
                 reads=[(cint, 'd')], writes=[('cinh', c6)])
            if c6 < 4:
                dst, dtok = xsT[:, c6, :], ('xsT', c6)
            elif c6 == 4:
                dst, dtok = BT[:], 'BT'
            else:
                dst, dtok = CTt[:], 'CT'
            p.op('act', lambda e, acc=acc, dst=dst, c6=c6: e.activation(out=dst, in_=acc[:], func=AF.Silu,
                                                                         bias=small['conv_b'][:, c6:c6 + 1]),
                 reads=[acct, ('m0s', 'conv_b')], writes=[dtok])

        pv = []
        pz = []
        for ci in range(4):
            cs = slice(ci * 128, (ci + 1) * 128)
            bv = cx.bank()
            for dc in range(DC):
                p.op('pe', lambda e, bv=bv, dc=dc, cs=cs: e.matmul(cx.ps[bv][:, 0:72], lhsT=hT[:, dc, cs],
                                                                   rhs=W[:, dc, VO:VO + 72], start=(dc == 0), stop=(dc == DC - 1)),
                     reads=[('m0h', dc), Wtok[dc // 4]], writes=[('ps', bv)])
            G = t * 4 + ci
            vs = G % 8
            p.op('act', lambda e, bv=bv, vs=vs: e.activation(out=Vpad[0][:, vs, 0:64], in_=cx.ps[bv][:, 0:64], func=AF.Copy),
                 reads=[('ps', bv)], writes=[('Vpad', 0, vs)])
            p.op('act', lambda e, bv=bv, vs=vs: e.activation(out=Vpad[1][:, vs, 64:128], in_=cx.ps[bv][:, 0:64], func=AF.Copy),
                 reads=[('ps', bv)], writes=[('Vpad', 1, vs)])
            tt8, tt8t = t8.next()
            p.op('dve', lambda e, bv=bv, tt8=tt8: e.tensor_tensor(out=tt8[:], in0=cx.ps[bv][:, 64:72], in1=small['dtb'][:], op=ALU.add),
                 reads=[('ps', bv), ('m0s', 'dtb')], writes=[tt8t])
            pv.append((tt8, tt8t))

        for pr in range(2):
            nb = cx.bank()
            db = cx.bank()
            for cp in range(2):
                pts = []
                for hp in range(2):
                    rows = slice(64 * hp, 64 * hp + 64)
                    b = cx.bank()
                    for cj in range(2):
                        ci = cp * 2 + cj
                        qs_ = qT[rows, pr, ci * 128:(ci + 1) * 128]
                        if ci == 0:
                            kprev, kpt = kT[rows, 1 - sl, 384:512], ('kT', 1 - sl)
                        else:
                            kprev, kpt = kT[rows, sl, (ci - 1) * 128:ci * 128], ('kT', sl)
                        kown = kT[rows, sl, ci * 128:(ci + 1) * 128]
                        p.op('pe', lambda e, b=b, cj=cj, kprev=kprev, qs_=qs_: e.matmul(
                            cx.ps[b][:, (cj * 2) * 128:(cj * 2 + 1) * 128], lhsT=kprev, rhs=qs_, start=True, stop=True),
                            reads=[kpt, ('qT', pr)], writes=[('ps', b)])
                        p.op('pe', lambda e, b=b, cj=cj, kown=kown, qs_=qs_: e.matmul(
                            cx.ps[b][:, (cj * 2 + 1) * 128:(cj * 2 + 2) * 128], lhsT=kown, rhs=qs_, start=True, stop=True),
                            reads=[('kT', sl), ('qT', pr)], writes=[('ps', b)])
                    pr_, prt = praw.next()
                    p.op('act', lambda e, b=b, pr_=pr_: e.activation(out=pr_[:], in_=cx.ps[b][:, :], func=AF.Exp, scale=0.125),
                         reads=[('ps', b)], writes=[prt])
                    pt_, ptt = pT[hp].next()
                    p.op('pool', lambda e, pr_=pr_, pt_=pt_: e.tensor_tensor(out=pt_[:], in0=pr_[:], in1=C['swamask'][:], op=ALU.mult),
                         reads=[prt, CT('swamask')], writes=[ptt])
                    pts.append((pt_, ptt))
                for cj in range(2):
                    ci = cp * 2 + cj
                    G = t * 4 + ci
                    terms = []
                    for hp in range(2):
                        for kb in range(2):
                            if G == 0 and kb == 0:
                                continue
                            terms.append((hp, kb))
                    for which, bank, in ((0, nb), (1, db)):
                        for n_, (hp, kb) in enumerate(terms):
                            vslot = (G - 1 + kb) % 8
                            if which == 0:
                                lhsT, lt = Vpad[hp][:, vslot, :], ('Vpad', hp, vslot)
                            else:
                                lhsT, lt = C['onespad'][:, hp * 128:(hp + 1) * 128], CT('onespad')
                            rhs = pts[hp][0][:, (cj * 2 + kb) * 128:(cj * 2 + kb + 1) * 128]
                            p.op('pe', lambda e, bank=bank, ci=ci, lhsT=lhsT, rhs=rhs, n_=n_, nn=len(terms): e.matmul(
                                cx.ps[bank][:, ci * 128:(ci + 1) * 128], lhsT=lhsT, rhs=rhs, start=(n_ == 0), stop=(n_ == nn - 1)),
                                reads=[lt, pts[hp][1]], writes=[('ps', bank)])
            td, tdt = tden.next()
            td2, td2t = tden2.next()
            p.op('dve', lambda e, db=db, td=td, pr=pr: e.tensor_scalar(out=td[:], in0=cx.ps[db][:, :TT], scalar1=esink[:, pr:pr + 1],
                                                                        scalar2=None, op0=ALU.add),
                 reads=[('ps', db), 'esink'], writes=[tdt])
            p.op('dve', lambda e, td=td, td2=td2: e.reciprocal(out=td2[:], in_=td[:]), reads=[tdt], writes=[td2t])
            p.op('dve', lambda e, nb=nb, td2=td2, pr=pr: e.tensor_tensor(out=mst[:, pr, :], in0=cx.ps[nb][:, :TT], in1=td2[:], op=ALU.mult),
                 reads=[('ps', nb), td2t], writes=[mtok(pr)])

        for ci in range(4):
            cs = slice(ci * 128, (ci + 1) * 128)
            tt8, tt8t = pv[ci]
            ee, eet = e8.next()
            dt_, dtt = dt8.next()
            dA, dAt = dA8.next()
            p.op('act', lambda e, tt8=tt8, ee=ee: e.activation(out=ee[:], in_=tt8[:], func=AF.Exp), reads=[tt8t], writes=[eet])
            p.op('act', lambda e, ee=ee, dt_=dt_: e.activation(out=dt_[:], in_=ee[:], func=AF.Ln, bias=1.0), reads=[eet], writes=[dtt])
            p.op('dve', lambda e, dt_=dt_, dA=dA: e.tensor_tensor(out=dA[:], in0=dt_[:], in1=na_bc[:], op=ALU.mult),
                 reads=[dtt, 'a_bc'], writes=[dAt])
            dau, daut = dAU.next()
            p.op('pool', lambda e, dau=dau, dA=dA: e.tensor_tensor(out=dau[:], in0=C['U'][:].unsqueeze(1).to_broadcast([128, 8, 128]),
                                                                    in1=dA[:].unsqueeze(2).to_broadcast([128, 8, 128]), op=ALU.mult),
                 reads=[dAt, CT('U')], writes=[daut])
            bs = cx.bank()
            p.op('pe', lambda e, bs=bs, dA=dA: e.matmul(cx.ps[bs][:, 0:8], lhsT=C['U'][:], rhs=dA[:], start=True, stop=True),
                 reads=[dAt, CT('U')], writes=[('ps', bs)])
            p.op('pe', lambda e, bs=bs, dA=dA: e.matmul(cx.ps[bs][:, 8:16], lhsT=C['ones_f'][:], rhs=dA[:], start=True, stop=True),
                 reads=[dAt, CT('ones_f')], writes=[('ps', bs)])
            cs_, cst = cs16.next()
            p.op('act', lambda e, bs=bs, cs_=cs_: e.activation(out=cs_[:], in_=cx.ps[bs][:, 0:16], func=AF.Copy),
                 reads=[('ps', bs)], writes=[cst])
            dd, ddt = d8.next()
            p.op('dve', lambda e, cs_=cs_, dd=dd: e.tensor_tensor(out=dd[:], in0=cs_[:, 8:16], in1=cs_[:, 0:8], op=ALU.subtract),
                 reads=[cst], writes=[ddt])
            ed, edt = ed8.next()
            p.op('act', lambda e, dd=dd, ed=ed: e.activation(out=ed[:], in_=dd[:], func=AF.Exp), reads=[ddt], writes=[edt])
            w2, w2t = w28.next()
            p.op('dve', lambda e, ed=ed, dt_=dt_, w2=w2: e.tensor_tensor(out=w2[:], in0=ed[:], in1=dt_[:], op=ALU.mult),
                 reads=[edt, dtt], writes=[w2t])
            ec, ect = ecum8.next()
            p.op('act', lambda e, cs_=cs_, ec=ec: e.activation(out=ec[:], in_=cs_[:, 0:8], func=AF.Exp), reads=[cst], writes=[ect])
            dct, dctt = dect8.next()
            p.op('act', lambda e, cs_=cs_, dct=dct: e.activation(out=dct[:], in_=cs_[:, 8:16], func=AF.Exp), reads=[cst], writes=[dctt])
            bcb = cx.bank()
            p.op('pe', lambda e, bcb=bcb, cs=cs: e.matmul(cx.ps[bcb][:, 0:128], lhsT=BT[:, cs], rhs=CTt[:, cs], start=True, stop=True),
                 reads=['BT', 'CT'], writes=[('ps', bcb)])
            mt_, mtt = MT.next()
            for half in range(2):
                bsg = cx.bank()
                hs = slice(half * 4, half * 4 + 4)
                p.op('pe', lambda e, bsg=bsg, dau=dau, hs=hs: e.matmul(cx.ps[bsg][:, :], lhsT=C['ones_f'][:],
                                                                       rhs=dau[:, hs, :], start=True, stop=False),
                     reads=[daut, CT('ones_f')], writes=[('ps', bsg)])
                p.op('pe', lambda e, bsg=bsg, dA=dA, hs=hs: e.matmul(cx.ps[bsg][:, :], lhsT=C['negU'][:],
                                                                      rhs=dA[:, hs].unsqueeze(2).to_broadcast([128, 4, 128]),
                                                                      start=False, stop=False),
                     reads=[dAt, CT('negU')], writes=[('ps', bsg)])
                p.op('pe', lambda e, bsg=bsg: e.matmul(cx.ps[bsg][:, :], lhsT=C['ident_b'][:], rhs=C['negmask4'][:],
                                                       start=False, stop=True),
                     reads=[CT('ident_b'), CT('negmask4')], writes=[('ps', bsg)])
                E_, Et = Esb.next()
                p.op('act', lambda e, bsg=bsg, E_=E_: e.activation(out=E_[:], in_=cx.ps[bsg][:, :], func=AF.Exp),
                     reads=[('ps', bsg)], writes=[Et])
                p.op('dve', lambda e, E_=E_, mt_=mt_, hs=hs, bcb=bcb: e.tensor_tensor(
                    out=mt_[:, hs, :], in0=E_[:].rearrange("p (h t) -> p h t", h=4),
                    in1=cx.ps[bcb][:, 0:128].unsqueeze(1).to_broadcast([128, 4, 128]), op=ALU.mult),
                    reads=[Et, ('ps', bcb)], writes=[(mtt, half)])
            bx = cx.bank()
            for c in range(4):
                p.op('pe', lambda e, bx=bx, c=c, cs=cs: e.transpose(out=cx.ps[bx][:, c * 128:(c + 1) * 128], in_=xsT[:, c, cs],
                                                                    identity=C['ident_f'][:]),
                     reads=[('xsT', c), CT('ident_f')], writes=[('ps', bx)])
            xk, xkt = xtok.next()
            p.op('act', lambda e, bx=bx, xk=xk: e.activation(out=xk[:], in_=cx.ps[bx][:, :], func=AF.Copy),
                 reads=[('ps', bx)], writes=[xkt])
            xd, xdt_t = xdt.next()
            xk3 = xk[:].rearrange("p (h j) -> p h j", h=8)
            p.op('dve', lambda e, xk3=xk3, xd=xd, dt_=dt_: e.tensor_tensor(out=xd[:], in0=xk3,
                                                                           in1=dt_[:].unsqueeze(2).to_broadcast([128, 8, 64]), op=ALU.mult),
                 reads=[xkt, dtt], writes=[xdt_t])
            xc, xct = xdec.next()
            p.op('dve', lambda e, xk3=xk3, xc=xc, w2=w2: e.tensor_tensor(out=xc[:].rearrange("p (h j) -> p h j", h=8), in0=xk3,
                                                                         in1=w2[:].unsqueeze(2).to_broadcast([128, 8, 64]), op=ALU.mult),
                 reads=[xkt, w2t], writes=[xct])
            xD_, xDt = xD.next()
            p.op('pool', lambda e, xk3=xk3, xD_=xD_: e.tensor_tensor(out=xD_[:].rearrange("p (h j) -> p h j", h=8), in0=xk3,
                                                                     in1=small['dskip'][:].unsqueeze(2).to_broadcast([128, 8, 64]), op=ALU.mult),
                 reads=[xkt, ('m0s', 'dskip')], writes=[xDt])
            p.op('pe', lambda e, cs=cs: e.transpose(out=psbf[:, 0:128], in_=BT[:, cs], identity=C['ident_b'][:]),
                 reads=['BT', CT('ident_b')], writes=['psbf'])
            bt_, btt = Btok.next()
            p.op('act', lambda e, bt_=bt_: e.activation(out=bt_[:], in_=psbf[:, 0:128], func=AF.Copy), reads=['psbf'], writes=[btt])
            bst = cx.bank()
            p.op('pe', lambda e, bst=bst, bt_=bt_, xc=xc: e.matmul(cx.ps[bst][:, :], lhsT=bt_[:], rhs=xc[:], start=True, stop=True),
                 reads=[btt, xct], writes=[('ps', bst)])
            byo = cx.bank()
            p.op('pe', lambda e, byo=byo, cs=cs: e.matmul(cx.ps[byo][:, :], lhsT=CTt[:, cs], rhs=Hbf[:], start=True, stop=True),
                 reads=['CT', 'Hbf'], writes=[('ps', byo)])
            byd = cx.bank()
            for h in range(8):
                p.op('pe', lambda e, byd=byd, h=h, mt_=mt_, xd=xd: e.matmul(cx.ps[byd][:, h * 64:(h + 1) * 64], lhsT=mt_[:, h, :],
                                                                            rhs=xd[:, h, :], start=True, stop=True),
                     reads=[(mtt, h // 4), xdt_t], writes=[('ps', byd)])
            ya, yat = y1.next()
            p.op('dve', lambda e, byo=byo, ya=ya, ec=ec: e.tensor_tensor(out=ya[:].rearrange("p (h j) -> p h j", h=8),
                                                                         in0=cx.ps[byo][:, :].rearrange("p (h j) -> p h j", h=8),
                                                                         in1=ec[:].unsqueeze(2).to_broadcast([128, 8, 64]), op=ALU.mult),
                 reads=[('ps', byo), ect], writes=[yat])
            p.op('dve', lambda e, byd=byd, ya=ya: e.tensor_tensor(out=ya[:], in0=ya[:], in1=cx.ps[byd][:, :], op=ALU.add),
                 reads=[('ps', byd), yat], writes=[yat])
            p.op('pool', lambda e, ya=ya, xD_=xD_: e.tensor_tensor(out=ya[:], in0=ya[:], in1=xD_[:], op=ALU.add),
                 reads=[yat, xDt], writes=[yat])
            yc, yct = ya, yat
            p.op('dve', lambda e, dct=dct: e.tensor_tensor(out=H[:].rearrange("p (h j) -> p h j", h=8),
                                                           in0=H[:].rearrange("p (h j) -> p h j", h=8),
                                                           in1=dct[:].unsqueeze(2).to_broadcast([128, 8, 64]), op=ALU.mult),
                 reads=['H', dctt], writes=['H'])
            p.op('dve', lambda e, bst=bst: e.tensor_tensor(out=H[:], in0=H[:], in1=cx.ps[bst][:, :], op=ALU.add),
                 reads=['H', ('ps', bst)], writes=['H'])
            p.op('pool', lambda e: e.tensor_copy(out=Hbf[:], in_=H[:]), reads=['H'], writes=['Hbf'])
            bz = cx.bank()
            for dc in range(DC):
                p.op('pe', lambda e, bz=bz, dc=dc, cs=cs: e.matmul(cx.ps[bz][:, :], lhsT=hT[:, dc, cs], rhs=W[:, dc, ZO:ZO + 512],
                                                                   start=(dc == 0), stop=(dc == DC - 1)),
                     reads=[('m0h', dc), Wtok[dc // 4]], writes=[('ps', bz)])
            sz_, szt = sz.next()
            p.op('act', lambda e, bz=bz, sz_=sz_: e.activation(out=sz_[:], in_=cx.ps[bz][:, :], func=AF.Silu),
                 reads=[('ps', bz)], writes=[szt])
            gy_, gyt = yc, yct
            p.op('pool', lambda e, gy_=gy_, sz_=sz_: e.tensor_tensor(out=gy_[:], in0=gy_[:], in1=sz_[:], op=ALU.mult),
                 reads=[yct, szt], writes=[gyt])
            ss_, sst = ss1.next()
            jk, jkt = junk.next()
            p.op('pool', lambda e, ss_=ss_: e.memset(ss_[:], 0.0), writes=[sst])
            p.op('act', lambda e, gy_=gy_, jk=jk, ss_=ss_: e.activation(out=jk[:], in_=gy_[:], func=AF.Square, accum_out=ss_[:, 0:1]),
                 reads=[gyt, sst], writes=[jkt, sst])
            l1, l1t = ln1.next()
            r1, r1t = rs1.next()
            p.op('act', lambda e, ss_=ss_, l1=l1: e.activation(out=l1[:], in_=ss_[:], func=AF.Ln, bias=cx.eps[:, 0:1], scale=1.0 / 512),
                 reads=[sst, 'eps_t'], writes=[l1t])
            p.op('act', lambda e, l1=l1, r1=r1: e.activation(out=r1[:], in_=l1[:], func=AF.Exp, scale=-0.5), reads=[l1t], writes=[r1t])
            yf, yft = ybf.next()
            p.op('dve', lambda e, gy_=gy_, r1=r1, yf=yf: e.scalar_tensor_tensor(out=yf[:], in0=gy_[:], scalar=r1[:, 0:1], in1=small['gate'][:],
                                                                                op0=ALU.mult, op1=ALU.mult),
                 reads=[gyt, r1t, ('m0s', 'gate')], writes=[yft])
            for c in range(4):
                p.op('pe', lambda e, c=c, yf=yf: e.transpose(out=psbfy[:, c * 128:(c + 1) * 128], in_=yf[:, c * 128:(c + 1) * 128],
                                                              identity=C['ident_b'][:]),
                     reads=[yft, CT('ident_b')], writes=['psbf_y'])
            p.op('act', lambda e, cs=cs: e.activation(out=mst[:, 2:6, cs], in_=psbfy[:, 0:512].rearrange("p (c t) -> p c t", c=4), func=AF.Copy),
                 reads=['psbf_y'], writes=[mtok(2 + ci)])
        if out_tile is None:
            p.dma('pool', d['mixT'][:, ts].rearrange("(c p) t -> p c t", p=128), mst[:],
                  reads=[mtok(c) for c in range(6)], writes=['mixT_out'], key='m0out')
        else:
            out_tile(t, mst, [mtok(c) for c in range(6)])
        if per_tile is not None:
            per_tile(t)


AB_Q, AB_K, AB_V, AB_Z, AB_X, AB_DT = 0, 1024, 1152, 1280, 3328, 6400


def m0_host_inputs(inp, b, g, S):
    w = inp['ab_w_in'][0]
    kv = g // 2
    cols = np.concatenate([
        np.arange(AB_Q + g * 256, AB_Q + (g + 1) * 256),
        np.arange(AB_K + kv * 64, AB_K + (kv + 1) * 64), np.arange(AB_K + kv * 64, AB_K + (kv + 1) * 64),
        np.arange(AB_X + g * 512, AB_X + (g + 1) * 512),
        np.arange(AB_X + 2048 + g * 128, AB_X + 2048 + (g + 1) * 128),
        np.arange(AB_X + 2560 + g * 128, AB_X + 2560 + (g + 1) * 128),
        np.arange(AB_V + kv * 64, AB_V + (kv + 1) * 64),
        np.arange(AB_DT + g * 8, AB_DT + (g + 1) * 8),
        np.arange(AB_Z + g * 512, AB_Z + (g + 1) * 512)])
    assert len(cols) == M0_NCOL
    chan = np.concatenate([np.arange(g * 512, (g + 1) * 512), 2048 + np.arange(g * 128, (g + 1) * 128),
                           2560 + np.arange(g * 128, (g + 1) * 128)])
    cw = inp['ab_conv_w'][0][:, chan]
    cb = inp['ab_conv_b'][0][chan]
    sk = inp['ab_sinks'][0][4 * g:4 * g + 4]
    rep = lambda v: np.ascontiguousarray(np.broadcast_to(np.asarray(v, np.float32)[None, :], (128, len(v))))
    d = dict(
        w_in=np.ascontiguousarray(w[:, cols]),
        g_mix=np.ascontiguousarray(inp['norm_mix'][0]),
        qk_gain=np.ascontiguousarray(np.stack([np.tile(inp['ab_q_norm'][0], 2), np.tile(inp['ab_k_norm'][0], 2)], 1)),
        sink2=np.ascontiguousarray(np.stack([np.repeat(sk[0:2], 64), np.repeat(sk[2:4], 64)], 1)),
        conv_w=np.ascontiguousarray(cw.reshape(4, 6, 128).transpose(2, 1, 0)),
        conv_b=np.ascontiguousarray(cb.reshape(6, 128).T),
        dtb=rep(inp['ab_dt_bias'][0][8 * g:8 * g + 8]),
        alog=rep(inp['ab_a_log'][0][8 * g:8 * g + 8]),
        dskip=rep(inp['ab_d_skip'][0][8 * g:8 * g + 8]),
        gate=rep(inp['ab_gate_norm'][0][512 * g:512 * (g + 1)]),
    )
    return {k: np.asarray(v, np.float32) for k, v in d.items()}


M0_SMALL = dict(qk_gain=[128, 2], sink2=[128, 2], conv_w=[128, 6, 4], conv_b=[128, 6], dtb=[128, 8],
                alog=[128, 8], dskip=[128, 8], gate=[128, 512])
M0_CONSTS = ['ident_f', 'ident_b', 'U', 'negU', 'ones_f', 'negmask4', 'swamask', 'blockones', 'onespad']


def const_inputs(names):
    hc = host_consts()
    return {"c_" + n: hc[n] for n in names}


def build_mixer0_program(S):
    nc = bass.Bass("TRN2", target_bir_lowering=False)
    d = {}
    d['xT'] = nc.dram_tensor("xT", [D_MODEL, S], F32, kind="ExternalInput").ap()
    d['w_in'] = nc.dram_tensor("w_in", [D_MODEL, M0_NCOL], F32, kind="ExternalInput").ap()
    d['g_mix'] = nc.dram_tensor("g_mix", [D_MODEL], F32, kind="ExternalInput").ap()
    for n, shp in M0_SMALL.items():
        d[n] = nc.dram_tensor(n, shp, F32, kind="ExternalInput").ap()
    d['mixT'] = nc.dram_tensor("mixT", [768, S], BF16, kind="ExternalOutput").ap()
    p = Prog(nc)
    cx = Ctx(p)
    load_consts(cx, nc, M0_CONSTS)
    emit_mixer0(cx, S, d)
    p.emit(final_wait_keys=['m0out'])
    return nc


def emit_mixer1(cx, S, d, tag='m1', h_src=None, out_pair=None):
    p = cx.p
    C = cx.c
    TT = 512
    ST = 2048
    nst = S // ST
    CT = lambda n: ('const', n)
    W = p.sbuf([128, DC, 1536], BF16, name='m1W')
    for c4 in range(4):
        p.dma('pool', W[:, c4 * 4:(c4 + 1) * 4, :],
              d['w_qkv'][c4 * 512:(c4 + 1) * 512, :].rearrange("(c p) n -> p c n", p=128),
              writes=[('m1W', c4)], key='m1w')
    Wtok = [('m1W', c4) for c4 in range(4)]
    qkg = p.sbuf([128, 2], F32, name='m1qkg')
    p.dma('sp', qkg[:], d['qk_gain'], writes=[('m1s', 'qk_gain')], key='m1s')
    hT = p.sbuf([128, DC, TT], BF16, name='m1h')
    qT = p.sbuf([128, 4, ST], BF16, name='m1qT')
    kT = p.sbuf([128, 4, 2, ST], BF16, name='m1kT')
    vT = p.sbuf([128, 4, 2, ST], BF16, name='m1vT')
    accN = p.sbuf([128, ST], F32, name='m1accN')
    accD = p.sbuf([128, ST], F32, name='m1accD')
    oT = p.sbuf([128, ST], BF16, name='m1oT')
    qsb = Ring(p, 1, [128, TT], F32, 'm1qsb')
    qsq = Ring(p, 2, [128, TT], BF16, 'm1qsq')
    qln = Ring(p, 1, [128, TT], F32, 'm1qln')
    qrs = Ring(p, 1, [128, TT], F32, 'm1qrs')
    praw = Ring(p, 2, [128, 512], BF16, 'm1praw')
    pT = [Ring(p, 2, [128, 512], BF16, f'm1pT{i}') for i in range(2)]
    VE = Ring(p, 8, [128, 128], BF16, 'm1VE')
    VO = Ring(p, 8, [128, 128], BF16, 'm1VO')
    psbf2 = cx.psbf2
    for r_ in (VE, VO):
        for i, t_ in enumerate(r_.tiles):
            p.op('pool', lambda e, t_=t_: e.memset(t_[:], 0.0), writes=[(r_.name, i)])

    for M in range(nst):
        sl = M % 2
        for tt in range(4):
            t = M * 4 + tt
            ts = slice(t * TT, (t + 1) * TT)
            lc = slice(tt * TT, (tt + 1) * TT)
            for c4 in range(4):
                p.dma('sp', hT[:, c4 * 4:(c4 + 1) * 4, :],
                      (d['hT'][c4 * 512:(c4 + 1) * 512, ts] if h_src is None else h_src(t, c4)).rearrange("(c p) t -> p c t", p=128),
                      writes=[('m1h', c4 * 4 + j) for j in range(4)], key=f'm1h{c4}')
            for fc in range(12):
                b = cx.bank()
                for dc in range(DC):
                    p.op('pe', lambda e, b=b, dc=dc, fc=fc: e.matmul(cx.ps[b][:, :TT], lhsT=W[:, dc, fc * 128:(fc + 1) * 128],
                                                                     rhs=hT[:, dc, :], start=(dc == 0), stop=(dc == DC - 1)),
                         reads=[('m1h', dc), Wtok[dc // 4]], writes=[('ps', b)])
                kind, c = fc // 4, fc % 4
                if kind == 2:
                    vdst = vT[:, c, sl, lc]
                    p.op('act', lambda e, b=b, vdst=vdst: e.activation(out=vdst, in_=cx.ps[b][:, :TT], func=AF.Copy),
                         reads=[('ps', b)], writes=[('vT', c, sl, tt)])
                    continue
                qs, qst = qsb.next()
                sq, sqt = qsq.next()
                p.op('act', lambda e, b=b, qs=qs: e.activation(out=qs[:], in_=cx.ps[b][:, :TT], func=AF.Copy),
                     reads=[('ps', b)], writes=[qst])
                p.op('act', lambda e, b=b, sq=sq: e.activation(out=sq[:], in_=cx.ps[b][:, :TT], func=AF.Square),
                     reads=[('ps', b)], writes=[sqt])
                b2 = cx.bank()
                p.op('pe', lambda e, b2=b2, sq=sq: e.matmul(cx.ps[b2][:, :TT], lhsT=C['blockones'][:], rhs=sq[:], start=True, stop=True),
                     reads=[sqt, CT('blockones')], writes=[('ps', b2)])
                ln, lnt = qln.next()
                rs, rst = qrs.next()
                p.op('act', lambda e, b2=b2, ln=ln: e.activation(out=ln[:], in_=cx.ps[b2][:, :TT], func=AF.Ln,
                                                                  bias=cx.eps[:, 0:1], scale=1.0 / 64),
                     reads=[('ps', b2), 'eps_t'], writes=[lnt])
                p.op('act', lambda e, ln=ln, rs=rs: e.activation(out=rs[:], in_=ln[:], func=AF.Exp, scale=-0.5),
                     reads=[lnt], writes=[rst])
                if kind == 0:
                    dst, dtok = qT[:, c, lc], ('qT1', c, tt)
                else:
                    dst, dtok = kT[:, c, sl, lc], ('kT1', c, sl, tt)
                p.op('dve', lambda e, qs=qs, rs=rs, dst=dst, kind=kind: e.scalar_tensor_tensor(
                    out=dst, in0=qs[:], scalar=qkg[:, kind:kind + 1], in1=rs[:], op0=ALU.mult, op1=ALU.mult),
                    reads=[qst, rst, ('m1s', 'qk_gain')], writes=[dtok])

        def colslice(start, r):
            return slice(start, start + 127 * r + 1, r)

        def toks(name, c, slot, start, r):
            lo = start // TT
            hi = (start + 127 * r) // TT
            return [(name, c, slot, j) for j in range(lo, hi + 1)]

        for pr in range(4):
            for r in (1, 4, 16):
                nblk = ST // (128 * r)
                if r == 1:
                    quads = [[(0, m) for m in range(q4 * 4, q4 * 4 + 4)] for q4 in range(4)]
                elif r == 4:
                    quads = [[(c, m) for c in range(4)] for m in range(4)]
                else:
                    quads = [[(c, 0) for c in range(q4 * 4, q4 * 4 + 4)] for q4 in range(4)]
                for qi, quad in enumerate(quads):
                    nb = cx.bank()
                    db = cx.bank()
                    for half in range(2):
                        units = quad[half * 2:half * 2 + 2]
                        info = []
                        for (c, m) in units:
                            own_start = r * 128 * m + c
                            if m > 0:
                                prev = (sl, r * 128 * (m - 1) + c)
                            elif M > 0:
                                prev = (1 - sl, ST - 128 * r + c)
                            else:
                                prev = None
                            info.append((own_start, prev))
                        pts = []
                        for hp in range(2):
                            rows = slice(64 * hp, 64 * hp + 64)
                            b = cx.bank()
                            for uj, (own_start, prev) in enumerate(info):
                                qcols = colslice(own_start, r)
                                qs_ = qT[rows, pr, qcols]
                                qtk = [('qT1', pr, j) for j in range(own_start // TT, (own_start + 127 * r) // TT + 1)]
                                if prev is not None:
                                    kp = kT[rows, pr, prev[0], colslice(prev[1], r)]
                                    p.op('pe', lambda e, b=b, uj=uj, kp=kp, qs_=qs_: e.matmul(
                                        cx.ps[b][:, (uj * 2) * 128:(uj * 2 + 1) * 128], lhsT=kp, rhs=qs_, start=True, stop=True),
                                        reads=toks('kT1', pr, prev[0], prev[1], r) + qtk, writes=[('ps', b)])
                                ko = kT[rows, pr, sl, qcols]
                                p.op('pe', lambda e, b=b, uj=uj, ko=ko, qs_=qs_: e.matmul(
                                    cx.ps[b][:, (uj * 2 + 1) * 128:(uj * 2 + 2) * 128], lhsT=ko, rhs=qs_, start=True, stop=True),
                                    reads=toks('kT1', pr, sl, own_start, r) + qtk, writes=[('ps', b)])
                            pr_, prt = praw.next()
                            p.op('act', lambda e, b=b, pr_=pr_: e.activation(out=pr_[:], in_=cx.ps[b][:, :], func=AF.Exp, scale=0.125),
                                 reads=[('ps', b)], writes=[prt])
                            pt_, ptt = pT[hp].next()
                            p.op('pool', lambda e, pr_=pr_, pt_=pt_: e.tensor_tensor(out=pt_[:], in0=pr_[:], in1=C['dilmask'][:], op=ALU.mult),
                                 reads=[prt, CT('dilmask')], writes=[ptt])
                            pts.append((pt_, ptt))
                        for uj, (own_start, prev) in enumerate(info):
                            u = half * 2 + uj
                            vtiles = {}
                            for kb, src in ((0, prev), (1, (sl, own_start))):
                                if src is None:
                                    continue
                                boff = 0
                                psbf = psbf2[kb]
                                vsrc = vT[:, pr, src[0], colslice(src[1], r)]
                                p.op('pe', lambda e, vsrc=vsrc, boff=boff, psbf=psbf: e.transpose(out=psbf[:, boff:boff + 128],
                                                                                        in_=vsrc, identity=C['ident_b'][:]),
                                     reads=toks('vT', pr, src[0], src[1], r) + [CT('ident_b')], writes=[('psbf', kb)])
                                ve, vet = VE.next()
                                vo, vot = VO.next()
                                p.op('act', lambda e, ve=ve, boff=boff, psbf=psbf: e.activation(out=ve[:, 0:64], in_=psbf[:, boff:boff + 64], func=AF.Copy),
                                     reads=[('psbf', kb)], writes=[vet])
                                p.op('act', lambda e, vo=vo, boff=boff, psbf=psbf: e.activation(out=vo[:, 64:128], in_=psbf[:, boff + 64:boff + 128], func=AF.Copy),
                                     reads=[('psbf', kb)], writes=[vot])
                                vtiles[kb] = ((ve, vet), (vo, vot))
                            terms = [(hp, kb) for hp in range(2) for kb in range(2) if kb in vtiles]
                            for which, bank in ((0, nb), (1, db)):
                                for n_, (hp, kb) in enumerate(terms):
                                    if which == 0:
                                        vt_, vtt_ = vtiles[kb][hp]
                                        lhsT, lt = vt_[:], vtt_
                                    else:
                                        lhsT, lt = C['onespad'][:, hp * 128:(hp + 1) * 128], CT('onespad')
                                    rhs = pts[hp][0][:, (uj * 2 + kb) * 128:(uj * 2 + kb + 1) * 128]
                                    p.op('pe', lambda e, bank=bank, u=u, lhsT=lhsT, rhs=rhs, n_=n_, nn=len(terms): e.matmul(
                                        cx.ps[bank][:, u * 128:(u + 1) * 128], lhsT=lhsT, rhs=rhs, start=(n_ == 0), stop=(n_ == nn - 1)),
                                        reads=[lt, pts[hp][1]], writes=[('ps', bank)])
                    for acc, bank, an in ((accN, nb, 'accN'), (accD, db, 'accD')):
                        if r == 1:
                            va = acc[:, qi * 512:(qi + 1) * 512]
                            pv = cx.ps[bank][:, :]
                        elif r == 4:
                            va = acc[:, qi * 512:(qi + 1) * 512].rearrange("p (i c) -> p c i", c=4)
                            pv = cx.ps[bank][:, :].rearrange("p (c i) -> p c i", c=4)
                        else:
                            va = acc[:, :].rearrange("p (i c) -> p c i", c=16)[:, qi * 4:(qi + 1) * 4, :]
                            pv = cx.ps[bank][:, :].rearrange("p (c i) -> p c i", c=4)
                        if r == 1:
                            p.op('dve', lambda e, va=va, pv=pv: e.tensor_copy(out=va, in_=pv),
                                 reads=[('ps', bank)], writes=[an])
                        else:
                            p.op('dve', lambda e, va=va, pv=pv: e.tensor_tensor(out=va, in0=va, in1=pv, op=ALU.add),
                                 reads=[('ps', bank), an], writes=[an])
            p.op('dve', lambda e: e.reciprocal(out=accD[:], in_=accD[:]), reads=['accD'], writes=['accD'])
            p.op('dve', lambda e: e.tensor_tensor(out=oT[:], in0=accN[:], in1=accD[:], op=ALU.mult),
                 reads=['accN', 'accD'], writes=['oT1'])
            if out_pair is None:
                p.dma('pool', d['oT'][pr * 128:(pr + 1) * 128, M * ST:(M + 1) * ST], oT[:], reads=['oT1'], writes=['oT_out'], key='m1out')
            else:
                out_pair(M, pr, oT, ['oT1'])


def m1_host_inputs(inp, g):
    w = inp['c_w_qkv'][0]
    cols = np.concatenate([np.arange(k * 2048 + g * 512, k * 2048 + (g + 1) * 512) for k in range(3)])
    return dict(w_qkv=np.ascontiguousarray(w[:, cols]),
                qk_gain=np.ascontiguousarray(np.stack([np.tile(inp['c_q_norm'][0], 2), np.tile(inp['c_k_norm'][0], 2)], 1)).astype(np.float32))


M1_CONSTS = ['ident_b', 'dilmask', 'blockones', 'onespad']


def build_mixer1_program(S):
    nc = bass.Bass("TRN2", target_bir_lowering=False)
    d = {}
    d['hT'] = nc.dram_tensor("hT", [D_MODEL, S], BF16, kind="ExternalInput").ap()
    d['w_qkv'] = nc.dram_tensor("w_qkv", [D_MODEL, 1536], F32, kind="ExternalInput").ap()
    d['qk_gain'] = nc.dram_tensor("qk_gain", [128, 2], F32, kind="ExternalInput").ap()
    d['oT'] = nc.dram_tensor("oT", [512, S], BF16, kind="ExternalOutput").ap()
    p = Prog(nc)
    cx = Ctx(p)
    load_consts(cx, nc, M1_CONSTS)
    emit_mixer1(cx, S, d)
    p.emit(final_wait_keys=['m1out'])
    return nc


I32 = mybir.dt.int32
GROUPS = [[0, 1, 2, 3], [4, 5, 6, 7]]
ALL_CONSTS = ['ident_f', 'ident_b', 'U', 'negU', 'ones_f', 'negmask4', 'swamask', 'dilmask', 'blockones', 'onespad']


def build_fused_program(S, debug=False):
    TSEG = S // 4
    nt_all = S // 512
    ntseg = TSEG // 512
    nst = S // 2048
    nc = bass.Bass("TRN2", target_bir_lowering=False)
    ext = lambda n, shp, dt=F32: nc.dram_tensor(n, list(shp), dt, kind="ExternalInput").ap()
    itn = lambda n, shp, dt=BF16: nc.dram_tensor(n, list(shp), dt, kind="Internal").ap()
    d = {}
    d['xT'] = ext("xT", [D_MODEL, S])
    xTs = ext("xTs", [D_MODEL, TSEG])
    d['w_in'] = ext("w_in", [D_MODEL, M0_NCOL])
    d['g_mix'] = ext("g_mix", [D_MODEL])
    for n, shp in M0_SMALL.items():
        d[n] = ext(n, shp)
    rank = ext("rank", [1, 1], I32)
    w_out0 = ext("w_out0", [3072, D_MODEL])
    w_up0 = ext("w_up0", [D_MODEL, D_FF])
    w_dn0 = ext("w_dn0", [D_FF, D_MODEL])
    g_ffn0 = ext("g_ffn0", [D_MODEL])
    g_mix1 = ext("g_mix1", [D_MODEL])
    d1 = {}
    d1['w_qkv'] = ext("w_qkv", [D_MODEL, 1536])
    d1['qk_gain'] = ext("qk_gain1", [128, 2])
    w_out1 = ext("w_out1", [2048, D_MODEL])
    w_up1 = ext("w_up1", [D_MODEL, D_FF])
    w_dn1 = ext("w_dn1", [D_FF, D_MODEL])
    g_ffn1 = ext("g_ffn1", [D_MODEL])
    outT = nc.dram_tensor("outT", [D_MODEL, TSEG], F32, kind="ExternalOutput").ap()
    sc = []
    for l, EC in ((0, 24), (1, 16)):
        sc.append((itn(f"s_out{l}", [4, 128, EC * 512]), itn(f"s_up{l}", [NFB, 128, 16 * 512]),
                   itn(f"s_dn{l}", [16, 128, FC * 128])))
    mix0_loc = itn("mix0_loc", [nt_all, 768, 512])
    G0 = itn("G0", [nt_all, 4 * 768, 512])
    x1T = itn("x1T", [D_MODEL, TSEG], F32)
    h1_loc = itn("h1_loc", [ntseg, 2, 1024, 512])
    G1 = itn("G1", [ntseg, 2, 4 * 1024, 512])
    o_loc = itn("o_loc", [nst, 4, 128, 2048])
    G2 = itn("G2", [nst, 4, 4 * 128, 2048])

    if debug:
        dbg_G0 = nc.dram_tensor("dbg_G0", [4 * 768, 512], BF16, kind="ExternalOutput").ap()
        dbg_x1 = nc.dram_tensor("dbg_x1", [D_MODEL, TSEG], F32, kind="ExternalOutput").ap()
        dbg_G1 = nc.dram_tensor("dbg_G1", [4 * 1024, 512], BF16, kind="ExternalOutput").ap()
        dbg_G2 = nc.dram_tensor("dbg_G2", [4 * 128, 2048], BF16, kind="ExternalOutput").ap()
        dbg_mix = nc.dram_tensor("dbg_mix", [768, 512], BF16, kind="ExternalOutput").ap()
    p = Prog(nc)
    cx = Ctx(p)
    load_consts(cx, nc, ALL_CONSTS)

    def rankval(e, scale):
        k = (id(e), 'rank')
        if k not in p.regs:
            r = e.alloc_register(f"rk{p.phase}")
            e.reg_load(r, rank[0:1, 0:1])
            p.regs[k] = e.snap(e.snap(r, min_val=0, max_val=3) * scale, min_val=0, max_val=3 * scale)
        return p.regs[k]

    p.begin_phase()
    casts = (cast_dense_weight_ops(p, w_out0, w_up0, w_dn0, *sc[0], 24, 'L0') +
             cast_dense_weight_ops(p, w_out1, w_up1, w_dn1, *sc[1], 16, 'L1'))
    per = -(-len(casts) // max(1, nt_all - 2))

    def per_tile(t):
        for _ in range(per):
            if casts:
                casts.pop(0)()

    def out_tile0(t, mst, reads):
        p.dma('act', mix0_loc[t].rearrange("(c p) t -> p c t", p=128), mst[:], reads=reads,
              writes=[('mix0_loc', t)], key='m0out')
        p.cc("AllGather", mix0_loc[t], G0[t], GROUPS, reads=[('mix0_loc', t)], writes=[('G0', t)], key='cc0')

    emit_mixer0(cx, S, d, out_tile=out_tile0, per_tile=per_tile)
    while casts:
        casts.pop(0)()
    p.end_phase()

    p.begin_phase()
    if debug:
        p.dma('sp', dbg_G0, G0[0], writes=['dbg0'], key='dbg')
        p.dma('sp', dbg_mix, mix0_loc[0], writes=['dbg0m'], key='dbg')

    stg0 = itn("stg0", [2, 4 * 768, 512])

    def mix_load0(t, mt, tok):
        sslot = t % 2

        def fn(e):
            src = G0[t:][bass.ds(rankval(e, ntseg), 1)]
            return e.dma_start(out=stg0[sslot].rearrange("(a b) t -> a b t", a=128),
                               in_=src.rearrange("o (a b) t -> a (o b) t", a=128))
        p.op('sp', fn, writes=[('stg0', sslot)], dma_key='L0stg')
        toks_ = []
        for r in range(4):
            p.dma('sp', mt[:, r * 2:(r + 1) * 2, :],
                  stg0[sslot, r * 768:r * 768 + 256, :].rearrange("(c p) t -> p c t", p=128),
                  reads=[('stg0', sslot)], writes=[(tok, 'a', r)], key='L0min')
            p.dma('sp', mt[:, 8 + r * 4:8 + (r + 1) * 4, :],
                  stg0[sslot, r * 768 + 256:(r + 1) * 768, :].rearrange("(c p) t -> p c t", p=128),
                  reads=[('stg0', sslot)], writes=[(tok, 'y', r)], key='L0min')
            toks_ += [(tok, 'a', r), (tok, 'y', r)]
        return toks_

    def hn_store0(t, h2, reads):
        for hf in range(2):
            p.dma('act', h1_loc[t, hf].rearrange("(c p) t -> p c t", p=128), h2[:, hf * 8:(hf + 1) * 8, :], reads=reads,
                  writes=[('h1_loc', t, hf)], key='L0hout')
            p.cc("AllGather", h1_loc[t, hf], G1[t, hf], GROUPS, reads=[('h1_loc', t, hf)], writes=[('G1', t, hf)], key='cc1')

    emit_dense(cx, TSEG, 24, xTs, None, *sc[0], g_ffn0, x1T, 'L0', gain_next=g_mix1, mix_load=mix_load0,
               hn_store=hn_store0, aq='act')
    p.end_phase()

    p.begin_phase()
    if debug:
        p.dma('sp', dbg_x1, x1T, writes=['dbg1'], key='dbg')
        p.dma('sp', dbg_G1, G1[0, 0], writes=['dbg2'], key='dbg')

    def h_src(t, c4):
        r, lt = divmod(t, ntseg)
        return G1[lt, c4 // 2, r * 1024 + (c4 % 2) * 512: r * 1024 + (c4 % 2) * 512 + 512, :]

    def out_pair1(M, pr, oT, reads):
        p.dma('act', o_loc[M, pr], oT[:], reads=reads, writes=[('o_loc', M, pr)], key='m1out')
        p.cc("AllGather", o_loc[M, pr], G2[M, pr], GROUPS, reads=[('o_loc', M, pr)], writes=[('G2', M, pr)], key='cc2')

    emit_mixer1(cx, S, d1, h_src=h_src, out_pair=out_pair1)
    p.end_phase()

    p.begin_phase()
    if debug:
        p.dma('sp', dbg_G2, G2[0, 0], writes=['dbg3'], key='dbg')
    stseg = TSEG // 2048

    stg2 = itn("stg2", [2, 4, 512, 2048])

    def mix_load1(t, mt, tok):
        j = t // 4
        sslot = j % 2
        if t % 4 == 0:
            def fn(e):
                src = G2[j:][bass.ds(rankval(e, stseg), 1)]
                return e.dma_start(out=stg2[sslot].rearrange("q (a b) t -> a q (b t)", a=128),
                                   in_=src.rearrange("o q (a b) t -> a (o q) (b t)", a=128))
            p.op('sp', fn, writes=[('stg2', sslot)], dma_key='L1stg')
        for r in range(4):
            p.dma('sp', mt[:, r * 4:(r + 1) * 4, :],
                  stg2[sslot, :, r * 128:(r + 1) * 128, (t % 4) * 512:(t % 4 + 1) * 512].rearrange("q p t -> p q t"),
                  reads=[('stg2', sslot)], writes=[(tok, 'q', r)], key='L1min')
        return [(tok, 'q', r) for r in range(4)]

    emit_dense(cx, TSEG, 16, x1T, None, *sc[1], g_ffn1, outT, 'L1', mix_load=mix_load1, aq='act')
    p.end_phase()
    p.close()
    return nc


BATCH = 2
SEQ = 16384
NCORES = 8
_PROGS = {}


def _prog(name, fn, *a):
    key = (name,) + tuple(a)
    if key not in _PROGS:
        _PROGS[key] = fn(*a)
    return _PROGS[key]


def fused_in_maps(inp, S):
    TSEG = S // 4
    x = inp['x']
    xT = [np.ascontiguousarray(x[b].T) for b in range(x.shape[0])]
    cst = const_inputs(ALL_CONSTS)
    maps = []
    for c in range(NCORES):
        b, g = divmod(c, 4)
        m = m0_host_inputs(inp, b, g, S)
        m['xT'] = xT[b]
        m['xTs'] = np.ascontiguousarray(xT[b][:, g * TSEG:(g + 1) * TSEG])
        m['rank'] = np.array([[g]], np.int32)
        m1 = m1_host_inputs(inp, g)
        m['w_qkv'] = m1['w_qkv']
        m['qk_gain1'] = m1['qk_gain']
        m.update(w_out0=inp['ab_w_out'][0], w_up0=inp['w_up'][0], w_dn0=inp['w_down'][0], g_ffn0=inp['norm_ffn'][0],
                 g_mix1=inp['norm_mix'][1], w_out1=inp['c_w_o'][0], w_up1=inp['w_up'][1], w_dn1=inp['w_down'][1],
                 g_ffn1=inp['norm_ffn'][1])
        m.update(cst)
        maps.append(m)
    return maps


def kernel(**inp):
    inp = {k: np.asarray(v) for k, v in inp.items()}
    S = inp['x'].shape[1]
    TSEG = S // 4
    nc = _prog('fused', build_fused_program, S)
    res = run_bass_kernel_spmd(nc, fused_in_maps(inp, S), core_ids=list(range(NCORES))).results
    out = np.empty((inp['x'].shape[0], S, D_MODEL), np.float32)
    for c in range(NCORES):
        b, g = divmod(c, 4)
        out[b, g * TSEG:(g + 1) * TSEG, :] = np.asarray(res[c]['outT']).T
    return out
```

```python
import bisect
import contextlib
import numpy as np
import ml_dtypes
import concourse.bass as bass
import concourse.mybir as mybir
from concourse.bass_utils import run_bass_kernel_spmd

F32 = mybir.dt.float32
BF16 = mybir.dt.bfloat16
AF = mybir.ActivationFunctionType
ALU = mybir.AluOpType
AX = mybir.AxisListType

D_MODEL = 2048
DC = D_MODEL // 128
D_FF = 8192
FC = D_FF // 128
NFB = D_FF // 512
EPS = 1e-5
SAME_ENGINE_SYNC = True
SCHEDULE = True


class Prog:
    ENGS = ("pe", "act", "dve", "pool")

    def __init__(self, nc):
        self.nc = nc
        self.ops = []
        self.top = contextlib.ExitStack()
        self.pstack = None
        self.last_w = {}
        self.readers = {}
        self.n_sb = 0
        self.cnt = {e: 0 for e in self.ENGS}
        self.dma_cum = {}
        self.dma_lists = {}
        self.sems = {e: self.top.enter_context(nc.semaphore("s_" + e)) for e in self.ENGS}
        self.bar = self.top.enter_context(nc.semaphore("s_bar"))
        self.dsems = {}
        self.phase = 0
        self.phase_start = 0
        self.regs = {}

    def _stk(self):
        return self.pstack if self.pstack is not None else self.top

    def sbuf(self, shape, dt, name=None):
        self.n_sb += 1
        return self._stk().enter_context(
            self.nc.sbuf_tensor(name or f"sb{self.n_sb}", list(shape), dt))

    def psum(self, shape, dt, name=None):
        self.n_sb += 1
        return self._stk().enter_context(
            self.nc.psum_tensor(name or f"ps{self.n_sb}", list(shape), dt))

    @staticmethod
    def _is_psum(t):
        return (isinstance(t, tuple) and t and t[0] in ('ps', 'psbf')) or (isinstance(t, str) and t.startswith('psbf'))

    def op(self, eng, fn, reads=(), writes=(), dma_key=None, inc=16):
        extra = [('psacc', t) for t in reads if self._is_psum(t)]
        if extra:
            writes = list(writes) + extra
        raw, war = set(), set()
        for b in reads:
            if b in self.last_w:
                raw.add(self.last_w[b])
        for b in writes:
            if b in self.last_w:
                war.add(self.last_w[b])
            war.update(self.readers.get(b, ()))
        idx = len(self.ops)
        war -= raw
        self.ops.append(dict(eng=eng, fn=fn, raw=raw, war=war, dma_key=dma_key,
                             needed=False, cnt=None, inc=inc))
        for b in reads:
            self.readers.setdefault(b, []).append(idx)
        for b in writes:
            self.last_w[b] = idx
            self.readers[b] = []
        return idx

    def dma(self, q, out, in_, reads=(), writes=(), key=None, **kw):
        assert key is not None
        return self.op(q, lambda e: e.dma_start(out=out, in_=in_, **kw),
                       reads=reads, writes=writes, dma_key=key)

    def cc(self, kind, in_ap, out_ap, groups, reads=(), writes=(), key=None):
        return self.op('pool', lambda e: e.collective_compute(kind, ALU.bypass, replica_groups=groups,
                                                              ins=[in_ap], outs=[out_ap]),
                       reads=reads, writes=writes, dma_key=key, inc=1)

    def _synced(self, x, y, kind):
        if x["eng"] != y["eng"]:
            return True
        if x["eng"] in ("pe", "sp"):
            return False
        return SAME_ENGINE_SYNC and kind == "raw"

    def begin_phase(self):
        self.pstack = contextlib.ExitStack()
        self.regs = {}

    COST = dict(pe=0.16, act=0.27, dve=0.45, pool=0.45, sp=0.15)

    def _schedule(self):
        import heapq
        ops = self.ops
        ps = self.phase_start
        n = len(ops) - ps
        if n == 0 or not SCHEDULE:
            return
        preds = [None] * n
        succs = [[] for _ in range(n)]
        npred = [0] * n
        for i in range(n):
            y = ops[ps + i]
            pr = set(x - ps for x in y["raw"] if x >= ps) | set(x - ps for x in y["war"] if x >= ps)
            preds[i] = pr
            npred[i] = len(pr)
            for x in pr:
                succs[x].append(i)
        engs = ("pe", "act", "dve", "pool", "sp")
        efree = {e: 0.0 for e in engs}
        byready = {e: [] for e in engs}
        byidx = {e: [] for e in engs}
        finish = [0.0] * n
        ready = [0.0] * n
        for i in range(n):
            if npred[i] == 0:
                heapq.heappush(byready[ops[ps + i]["eng"]], (0.0, i))
        order = []
        lo = 0
        done = [False] * n
        while len(order) < n:
            best = None
            for e in engs:
                hr, hi = byready[e], byidx[e]
                while hr and hr[0][0] <= efree[e]:
                    heapq.heappush(hi, heapq.heappop(hr)[1])
                if hi:
                    cand = (efree[e], hi[0], e, True)
                elif hr:
                    cand = (hr[0][0], hr[0][1], e, False)
                else:
                    continue
                if best is None or cand[:2] < best[:2]:
                    best = cand
            st, i, e, fromidx = best
            if fromidx:
                heapq.heappop(byidx[e])
            else:
                heapq.heappop(byready[e])
            y = ops[ps + i]
            cost = y.get("cost") or self.COST[e]
            if y["dma_key"] is not None:
                efree[e] = st + 0.12
                fin = st + (y.get("cost") or (80.0 if y["inc"] == 1 else 3.0))
            else:
                efree[e] = st + cost
                fin = st + cost
            finish[i] = fin
            done[i] = True
            order.append(i)
            for sidx in succs[i]:
                lat = 0.15 if (ops[ps + sidx]["eng"] == e and y["dma_key"] is None) else 1.2
                r_ = fin + lat
                if r_ > ready[sidx]:
                    ready[sidx] = r_
                npred[sidx] -= 1
                if npred[sidx] == 0:
                    heapq.heappush(byready[ops[ps + sidx]["eng"]], (ready[sidx], sidx))
        remap = {ps + old: ps + new for new, old in enumerate(order)}
        newops = [ops[ps + old] for old in order]
        for y in newops:
            y["raw"] = set(remap.get(x, x) for x in y["raw"])
            y["war"] = set(remap.get(x, x) for x in y["war"])
        ops[ps:] = newops
        self.last_w = {k: remap.get(v, v) for k, v in self.last_w.items()}
        self.readers = {k: [remap.get(v, v) for v in vs] for k, vs in self.readers.items()}

    def end_phase(self):
        nc = self.nc
        ops = self.ops
        ps = self.phase_start
        self._schedule()
        last = {}
        for i in range(ps, len(ops)):
            y = ops[i]
            if y["dma_key"] is None:
                last[y["eng"]] = i
            for kind in ("raw", "war"):
                for xi in y[kind]:
                    if xi < ps:
                        continue
                    x = ops[xi]
                    if x["dma_key"] is None and self._synced(x, y, kind):
                        x["needed"] = True
        for e, i in last.items():
            if e in self.ENGS:
                ops[i]["needed"] = True
        for i in range(ps, len(ops)):
            x = ops[i]
            if x["dma_key"] is not None:
                k = x["dma_key"]
                self.dma_cum[k] = self.dma_cum.get(k, 0) + x["inc"]
                self.dma_lists.setdefault(k, ([], []))
                self.dma_lists[k][0].append(i)
                self.dma_lists[k][1].append(self.dma_cum[k])
                if k not in self.dsems:
                    self.dsems[k] = self.top.enter_context(nc.semaphore("d_" + str(k)))
            elif x["needed"]:
                self.cnt[x["eng"]] += 1
                x["cnt"] = self.cnt[x["eng"]]
        dma_lists, dsems, sems = self.dma_lists, self.dsems, self.sems
        phase = self.phase

        def waits_for(yi):
            y = ops[yi]
            w = {}
            for kind in ("raw", "war"):
                for xi in y[kind]:
                    if xi < ps:
                        continue
                    x = ops[xi]
                    if x["dma_key"] is not None:
                        k = x["dma_key"]
                        idxs, cums = dma_lists[k]
                        pos = bisect.bisect_left(idxs, yi)
                        v = cums[pos - 1]
                        key = ("d", k)
                    else:
                        if not self._synced(x, y, kind):
                            continue
                        v = x["cnt"]
                        key = ("c", x["eng"])
                    if w.get(key, 0) < v:
                        w[key] = v
            return w

        def emit_engine(ename, e):
            seen = {}
            if phase > 0:
                e.wait_ge(self.bar, phase)
            for yi in range(ps, len(ops)):
                y = ops[yi]
                if y["eng"] != ename:
                    continue
                for key, v in waits_for(yi).items():
                    if seen.get(key, 0) >= v:
                        continue
                    seen[key] = v
                    sm = dsems[key[1]] if key[0] == "d" else sems[key[1]]
                    e.wait_ge(sm, v)
                ins = y["fn"](e)
                if y["dma_key"] is not None:
                    ins.then_inc(dsems[y["dma_key"]], y["inc"])
                elif y["needed"]:
                    ins.then_inc(sems[ename], 1)
            if ename == "sp":
                for k, v in self.dma_cum.items():
                    e.wait_ge(dsems[k], v)
                for en in self.ENGS:
                    if self.cnt[en] > 0:
                        e.wait_ge(sems[en], self.cnt[en])
                e.sem_inc(self.bar, 1)

        with nc.Block() as block:
            @block.sync
            def _(e):
                emit_engine("sp", e)

            @block.tensor
            def _(e):
                emit_engine("pe", e)

            @block.scalar
            def _(e):
                emit_engine("act", e)

            @block.vector
            def _(e):
                emit_engine("dve", e)

            @block.gpsimd
            def _(e):
                emit_engine("pool", e)
        if self.pstack is not None:
            self.pstack.close()
            self.pstack = None
        self.phase += 1
        self.phase_start = len(ops)

    def emit(self, final_wait_keys=()):
        self.end_phase()
        self.top.close()

    def close(self):
        self.top.close()


class Ctx:
    def __init__(self, p):
        self.p = p
        self.ps = [p.psum([128, 512], F32, name=f"psb{i}") for i in range(6)]
        self.psbf2 = [p.psum([128, 1024], BF16, name=f"psbf{i}") for i in range(2)]
        self.ps_rr = 0
        self.ones_bf = p.sbuf([128, 128], BF16, name="ones_bf")
        self.eps = p.sbuf([128, 1], F32, name="eps_t")
        p.op('pool', lambda e: e.memset(self.ones_bf[:], 1.0), writes=['ones_bf'])
        p.op('pool', lambda e: e.memset(self.eps[:], EPS), writes=['eps_t'])
        self.uid = 0

    def bank(self, lo=0, hi=6):
        i = lo + self.ps_rr % (hi - lo)
        self.ps_rr += 1
        return i

    def u(self):
        self.uid += 1
        return self.uid


def emit_rmsnorm_fm(cx, xt, xtok, nchunk, width, gain, out_bf, out_tok, sq_ring, rstd, dim, tmp_sqrt):
    p = cx.p
    b = cx.bank()
    ps = cx.ps[b]
    for c in range(nchunk):
        sq = sq_ring[c % len(sq_ring)]
        sqt = ('sq', id(sq_ring), c % len(sq_ring))
        p.op('act', lambda e, c=c, sq=sq: e.activation(out=sq[:, :width], in_=xt[:, c, :width], func=AF.Square),
             reads=[xtok(c)], writes=[sqt])
        p.op('pe', lambda e, c=c, sq=sq: e.matmul(ps[:, :width], lhsT=cx.ones_bf[:], rhs=sq[:, :width],
                                                   start=(c == 0), stop=(c == nchunk - 1)),
             reads=[sqt, 'ones_bf'], writes=[('ps', b)])
    p.op('act', lambda e: e.activation(out=tmp_sqrt[:, :width], in_=ps[:, :width], func=AF.Ln,
                                       bias=cx.eps[:, 0:1], scale=1.0 / dim),
         reads=[('ps', b), 'eps_t'], writes=[('tmp_sqrt', id(tmp_sqrt))])
    p.op('act', lambda e: e.activation(out=rstd[:, :width], in_=tmp_sqrt[:, :width], func=AF.Exp, scale=-0.5),
         reads=[('tmp_sqrt', id(tmp_sqrt))], writes=[('rstd', id(rstd))])
    for c in range(nchunk):
        p.op('dve', lambda e, c=c: e.scalar_tensor_tensor(out=out_bf[:, c, :width], in0=xt[:, c, :width],
                                                          scalar=gain[:, c:c + 1], in1=rstd[:, :width],
                                                          op0=ALU.mult, op1=ALU.mult),
             reads=[xtok(c), ('rstd', id(rstd)), 'gains'], writes=[out_tok(c)])


def emit_rmsnorm_stream(cx, src, nchunk, width, gain, out_bf, out_tok, xring, sq_ring, rstd, tmpl, dim, q, key):
    p = cx.p
    b = cx.bank()
    ps = cx.ps[b]
    for c in range(nchunk):
        xr, xrt = xring.next()
        p.dma(q, xr[:, :width], src(c), writes=[xrt], key=key)
        sq = sq_ring[c % len(sq_ring)]
        sqt = ('sq', id(sq_ring), c % len(sq_ring))
        p.op('act', lambda e, xr=xr, sq=sq: e.activation(out=sq[:, :width], in_=xr[:, :width], func=AF.Square),
             reads=[xrt], writes=[sqt])
        p.op('pe', lambda e, c=c, sq=sq: e.matmul(ps[:, :width], lhsT=cx.ones_bf[:], rhs=sq[:, :width],
                                                   start=(c == 0), stop=(c == nchunk - 1)),
             reads=[sqt, 'ones_bf'], writes=[('ps', b)])
    p.op('act', lambda e: e.activation(out=tmpl[:, :width], in_=ps[:, :width], func=AF.Ln,
                                       bias=cx.eps[:, 0:1], scale=1.0 / dim),
         reads=[('ps', b), 'eps_t'], writes=[('tmpl', id(tmpl))])
    p.op('act', lambda e: e.activation(out=rstd[:, :width], in_=tmpl[:, :width], func=AF.Exp, scale=-0.5),
         reads=[('tmpl', id(tmpl))], writes=[('rstd', id(rstd))])
    for c in range(nchunk):
        xr, xrt = xring.next()
        p.dma(q, xr[:, :width], src(c), writes=[xrt], key=key)
        p.op('dve', lambda e, c=c, xr=xr: e.scalar_tensor_tensor(out=out_bf[:, c, :width], in0=xr[:, :width],
                                                                 scalar=gain[:, c:c + 1], in1=rstd[:, :width],
                                                                 op0=ALU.mult, op1=ALU.mult),
             reads=[xrt, ('rstd', id(rstd)), 'gains'], writes=[out_tok(c)])


def cast_dense_weight_ops(p, w_out, w_up, w_dn, s_out, s_up, s_dn, EC, tag):
    ops = []
    for db in range(4):
        ops.append(lambda db=db: p.dma('pool', s_out[db].rearrange("p (c j) -> p c j", j=512),
                                       w_out[:, db * 512:(db + 1) * 512].rearrange("(c p) j -> p c j", p=128),
                                       writes=[(tag, 's_out', db)], key='cast'))
    for fb in range(NFB):
        ops.append(lambda fb=fb: p.dma('pool', s_up[fb].rearrange("p (c j) -> p c j", j=512),
                                       w_up[:, fb * 512:(fb + 1) * 512].rearrange("(c p) j -> p c j", p=128),
                                       writes=[(tag, 's_up', fb)], key='cast'))
    rows = min(2048, D_FF)
    for db in range(16):
        for q in range(D_FF // rows):
            ops.append(lambda db=db, q=q: p.dma(
                'pool', s_dn[db][:, q * rows:(q + 1) * rows].rearrange("p (c j) -> p c j", j=128),
                w_dn[q * rows:(q + 1) * rows, db * 128:(db + 1) * 128].rearrange("(c p) j -> p c j", p=128),
                writes=[(tag, 's_dn', db, q)], key='cast'))
    return ops


def emit_cast_dense_weights(p, w_out, w_up, w_dn, s_out, s_up, s_dn, EC, tag):
    for f in cast_dense_weight_ops(p, w_out, w_up, w_dn, s_out, s_up, s_dn, EC, tag):
        f()


def emit_dense(cx, T, EC, xT, mixT, s_out, s_up, s_dn, gain_ffn, outT, tag,
               gain_next=None, hnT=None, TT=512, mix_load=None, hn_store=None, aq='pool'):
    p = cx.p
    nt = T // TT
    xt = p.sbuf([128, DC, TT], F32, name=tag + "x")
    mt = p.sbuf([128, EC, TT], BF16, name=tag + "mix")
    h2 = p.sbuf([128, DC, TT], BF16, name=tag + "h2")
    at = p.sbuf([128, FC, TT], BF16, name=tag + "a")
    WR = 24 * 512
    wring = [p.sbuf([128, WR], BF16, name=tag + f"w{i}") for i in range(2)]
    stage = [p.sbuf([128, TT], F32, name=tag + f"st{i}") for i in range(2)]
    sq_ring = [p.sbuf([128, TT], BF16, name=tag + f"sq{i}") for i in range(2)]
    sqf = [p.sbuf([128, TT], F32, name=tag + f"sqf{i}") for i in range(2)]
    rstd = p.sbuf([128, TT], F32, name=tag + "rstd")
    tmp_sqrt = p.sbuf([128, TT], F32, name=tag + "tsq")
    g1 = p.sbuf([128, DC], F32, name=tag + "g1")
    p.dma('sp', g1[:], gain_ffn.rearrange("(c p) -> p c", p=128), writes=['gains'], key=tag + 'g',
          allow_slow_non_contiguous=True)
    if gain_next is not None:
        g2 = p.sbuf([128, DC], F32, name=tag + "g2")
        p.dma('sp', g2[:], gain_next.rearrange("(c p) -> p c", p=128), writes=['gains'], key=tag + 'g',
              allow_slow_non_contiguous=True)

    blocks = []
    for db in range(4):
        blocks.append(('out', db, s_out[db], EC * 512, [(tag, 's_out', db)]))
    for fb in range(NFB):
        blocks.append(('up', fb, s_up[fb], 16 * 512, [(tag, 's_up', fb)]))
    for db in range(16):
        blocks.append(('dn', db, s_dn[db], FC * 128, [(tag, 's_dn', db, q) for q in range(D_FF // min(2048, D_FF))]))
    nblk = len(blocks)
    wcount = [0]

    def load_block(gi):
        kind, bi, src, n, tok = blocks[gi % nblk]
        slot = gi % 2
        p.dma('sp', wring[slot][:, :n], src, reads=tok, writes=[(tag, 'wr', slot)], key=tag + f'wr{slot}')

    xtok = lambda c: (tag, 'x', c)
    total_blocks = nt * nblk
    load_block(0)
    gi = 0
    for t in range(nt):
        ts = slice(t * TT, (t + 1) * TT)
        for c in range(DC):
            p.dma(aq, xt[:, c, :], xT[c * 128:(c + 1) * 128, ts], writes=[xtok(c)], key=tag + 'xin')
        if mix_load is None:
            p.dma(aq, mt[:], mixT[:, ts].rearrange("(c p) t -> p c t", p=128),
                  writes=[(tag, 'mix')], key=tag + 'min')
            mixtoks = [(tag, 'mix')]
        else:
            mixtoks = mix_load(t, mt, (tag, 'mix'))
        for db in range(4):
            if gi + 1 < total_blocks:
                load_block(gi + 1)
            slot = gi % 2
            w = wring[slot]
            for dj in range(4):
                dc = db * 4 + dj
                b = cx.bank()
                for ec in range(EC):
                    p.op('pe', lambda e, b=b, w=w, ec=ec, dj=dj: e.matmul(
                        cx.ps[b][:, :TT], lhsT=w[:, ec * 512 + dj * 128: ec * 512 + (dj + 1) * 128],
                        rhs=mt[:, ec, :], start=(ec == 0), stop=(ec == EC - 1)),
                        reads=[(tag, 'wr', slot)] + mixtoks, writes=[('ps', b)])
                p.op('dve', lambda e, b=b, dc=dc: e.tensor_tensor(out=xt[:, dc, :], in0=xt[:, dc, :],
                                                                   in1=cx.ps[b][:, :TT], op=ALU.add),
                     reads=[('ps', b), xtok(dc)], writes=[xtok(dc)])
            gi += 1
        emit_rmsnorm_fm(cx, xt, xtok, DC, TT, g1, h2, lambda c: (tag, 'h2', c), sq_ring, rstd, D_MODEL, tmp_sqrt)
        for fb in range(NFB):
            if gi + 1 < total_blocks:
                load_block(gi + 1)
            slot = gi % 2
            w = wring[slot]
            for fj in range(4):
                fc = fb * 4 + fj
                b = cx.bank()
                for dc in range(DC):
                    p.op('pe', lambda e, b=b, w=w, dc=dc, fj=fj: e.matmul(
                        cx.ps[b][:, :TT], lhsT=w[:, dc * 512 + fj * 128: dc * 512 + (fj + 1) * 128],
                        rhs=h2[:, dc, :], start=(dc == 0), stop=(dc == DC - 1)),
                        reads=[(tag, 'wr', slot), (tag, 'h2', dc)], writes=[('ps', b)])
                sf = sqf[fc % 2]
                p.op('act', lambda e, b=b, sf=sf: e.activation(out=sf[:], in_=cx.ps[b][:, :TT], func=AF.Square),
                     reads=[('ps', b)], writes=[(tag, 'sqf', fc % 2)])
                p.op('dve', lambda e, b=b, sf=sf, fc=fc: e.scalar_tensor_tensor(
                    out=at[:, fc, :], in0=cx.ps[b][:, :TT], scalar=0.0, in1=sf[:], op0=ALU.is_gt, op1=ALU.mult),
                    reads=[('ps', b), (tag, 'sqf', fc % 2)], writes=[(tag, 'a', fc)])
            gi += 1
        for db in range(16):
            if gi + 1 < total_blocks:
                load_block(gi + 1)
            slot = gi % 2
            w = wring[slot]
            b = cx.bank()
            for fc in range(FC):
                p.op('pe', lambda e, b=b, w=w, fc=fc: e.matmul(
                    cx.ps[b][:, :TT], lhsT=w[:, fc * 128:(fc + 1) * 128], rhs=at[:, fc, :],
                    start=(fc == 0), stop=(fc == FC - 1)),
                    reads=[(tag, 'wr', slot), (tag, 'a', fc)], writes=[('ps', b)])
            if hnT is None and hn_store is None:
                st = stage[db % 2]
                p.op('dve', lambda e, b=b, db=db, st=st: e.tensor_tensor(out=st[:], in0=xt[:, db, :],
                                                                         in1=cx.ps[b][:, :TT], op=ALU.add),
                     reads=[('ps', b), xtok(db)], writes=[(tag, 'st', db % 2)])
                p.dma(aq, outT[db * 128:(db + 1) * 128, ts], st[:], reads=[(tag, 'st', db % 2)],
                      writes=[(tag, 'outT')], key=tag + 'out')
            else:
                p.op('dve', lambda e, b=b, db=db: e.tensor_tensor(out=xt[:, db, :], in0=xt[:, db, :],
                                                                   in1=cx.ps[b][:, :TT], op=ALU.add),
                     reads=[('ps', b), xtok(db)], writes=[xtok(db)])
                p.dma(aq, outT[db * 128:(db + 1) * 128, ts], xt[:, db, :], reads=[xtok(db)],
                      writes=[(tag, 'outT')], key=tag + 'out')
            gi += 1
        if hnT is not None or hn_store is not None:
            emit_rmsnorm_fm(cx, xt, xtok, DC, TT, g2, h2, lambda c: (tag, 'h2', c), sq_ring, rstd, D_MODEL, tmp_sqrt)
            if hn_store is None:
                p.dma(aq, hnT[:, ts].rearrange("(c p) t -> p c t", p=128), h2[:],
                      reads=[(tag, 'h2', c) for c in range(DC)], writes=[(tag, 'hnT')], key=tag + 'hout')
            else:
                hn_store(t, h2, [(tag, 'h2', c) for c in range(DC)])


def build_dense_program(T, EC, with_hn, do_cast=True):
    nc = bass.Bass("TRN2", target_bir_lowering=False)
    E = EC * 128
    xT = nc.dram_tensor("xT", [D_MODEL, T], F32, kind="ExternalInput").ap()
    mixT = nc.dram_tensor("mixT", [E, T], BF16, kind="ExternalInput").ap()
    w_out = nc.dram_tensor("w_out", [E, D_MODEL], F32, kind="ExternalInput").ap()
    w_up = nc.dram_tensor("w_up", [D_MODEL, D_FF], F32, kind="ExternalInput").ap()
    w_dn = nc.dram_tensor("w_dn", [D_FF, D_MODEL], F32, kind="ExternalInput").ap()
    g_ffn = nc.dram_tensor("g_ffn", [D_MODEL], F32, kind="ExternalInput").ap()
    outT = nc.dram_tensor("outT", [D_MODEL, T], F32, kind="ExternalOutput").ap()
    g_next = hnT = None
    if with_hn:
        g_next = nc.dram_tensor("g_next", [D_MODEL], F32, kind="ExternalInput").ap()
        hnT = nc.dram_tensor("hnT", [D_MODEL, T], BF16, kind="ExternalOutput").ap()
    s_out = nc.dram_tensor("s_out", [4, 128, EC * 512], BF16, kind="Internal").ap()
    s_up = nc.dram_tensor("s_up", [NFB, 128, 16 * 512], BF16, kind="Internal").ap()
    s_dn = nc.dram_tensor("s_dn", [16, 128, FC * 128], BF16, kind="Internal").ap()
    p = Prog(nc)
    cx = Ctx(p)
    emit_cast_dense_weights(p, w_out, w_up, w_dn, s_out, s_up, s_dn, EC, 'L')
    emit_dense(cx, T, EC, xT, mixT, s_out, s_up, s_dn, g_ffn, outT, 'L', gain_next=g_next, hnT=hnT)
    p.emit(final_wait_keys=['Lout'] + (['Lhout'] if with_hn else []))
    return nc


M0_NCOL = 1736
QO, KO, XO, BO, CO, VO, DTO, ZO = 0, 256, 384, 896, 1024, 1152, 1216, 1224
NEG = -30000.0


class Ring:
    def __init__(self, p, n, shape, dt, name):
        self.tiles = [p.sbuf(shape, dt, name=f"{name}{i}") for i in range(n)]
        self.name = name
        self.i = 0

    def next(self):
        k = self.i % len(self.tiles)
        self.i += 1
        return self.tiles[k], (self.name, k)


def host_consts():
    bf = ml_dtypes.bfloat16
    i = np.arange(128)
    U = (i[:, None] <= i[None, :]).astype(np.float32)
    c = {}
    c['ident_f'] = np.eye(128, dtype=np.float32)
    c['ident_b'] = np.eye(128, dtype=np.float32).astype(bf)
    c['U'] = U
    c['negU'] = -U
    c['ones_f'] = np.ones((128, 128), np.float32)
    nm = np.where(i[None, :] < i[:, None], NEG, 0.0).astype(np.float32)
    c['negmask4'] = np.tile(nm, (1, 4)).astype(bf)
    prev_swa = (i[:, None] > i[None, :]).astype(np.float32)
    prev_dil = (i[:, None] >= i[None, :]).astype(np.float32)
    own = (i[:, None] <= i[None, :]).astype(np.float32)
    c['swamask'] = np.tile(np.concatenate([prev_swa, own], 1), (1, 2)).astype(bf)
    c['dilmask'] = np.tile(np.concatenate([prev_dil, own], 1), (1, 2)).astype(bf)
    bo = np.zeros((128, 128), np.float32)
    bo[:64, :64] = 1
    bo[64:, 64:] = 1
    c['blockones'] = bo.astype(bf)
    op = np.zeros((128, 256), np.float32)
    op[:, 0:64] = 1
    op[:, 128 + 64:256] = 1
    c['onespad'] = op.astype(bf)
    return c


CONST_SPECS = dict(ident_f=([128, 128], F32), ident_b=([128, 128], BF16), U=([128, 128], F32),
                   negU=([128, 128], F32), ones_f=([128, 128], F32), negmask4=([128, 512], BF16),
                   swamask=([128, 512], BF16), dilmask=([128, 512], BF16), blockones=([128, 128], BF16),
                   onespad=([128, 256], BF16))


def load_consts(cx, nc, names):
    p = cx.p
    cx.c = {}
    for n in names:
        shp, dt = CONST_SPECS[n]
        src = nc.dram_tensor("c_" + n, shp, dt, kind="ExternalInput").ap()
        t = p.sbuf(shp, dt, name="k_" + n)
        p.dma('sp', t[:], src, writes=[('const', n)], key='consts')
        cx.c[n] = t


def emit_mixer0(cx, S, d, tag='m0', out_tile=None, per_tile=None):
    p = cx.p
    C = cx.c
    TT = 512
    nt = S // TT
    CT = lambda n: ('const', n)
    W = p.sbuf([128, DC, M0_NCOL], BF16, name='m0W')
    for c4 in range(4):
        p.dma('pool', W[:, c4 * 4:(c4 + 1) * 4, :],
              d['w_in'][c4 * 512:(c4 + 1) * 512, :].rearrange("(c p) n -> p c n", p=128),
              writes=[('m0W', c4)], key='m0w')
    Wtok = [('m0W', c4) for c4 in range(4)]
    gmix = p.sbuf([128, DC], F32, name='m0gmix')
    p.dma('sp', gmix[:], d['g_mix'].rearrange("(c p) -> p c", p=128), writes=['gains'], key='m0s',
          allow_slow_non_contiguous=True)
    small = {}
    for n, shp in (('qk_gain', [128, 2]), ('sink2', [128, 2]), ('conv_w', [128, 6, 4]), ('conv_b', [128, 6]),
                   ('dtb', [128, 8]), ('alog', [128, 8]), ('dskip', [128, 8]), ('gate', [128, 512])):
        t = p.sbuf(shp, F32, name='m0_' + n)
        p.dma('sp', t[:], d[n], writes=[('m0s', n)], key='m0s')
        small[n] = t
    esink = p.sbuf([128, 2], F32, name='m0esink')
    p.op('act', lambda e: e.activation(out=esink[:], in_=small['sink2'][:], func=AF.Exp),
         reads=[('m0s', 'sink2')], writes=['esink'])
    a_bc = p.sbuf([128, 8], F32, name='m0a')
    p.op('act', lambda e: e.activation(out=a_bc[:], in_=small['alog'][:], func=AF.Exp),
         reads=[('m0s', 'alog')], writes=['a_bc0'])
    na_bc = p.sbuf([128, 8], F32, name='m0na')
    p.op('dve', lambda e: e.tensor_scalar(out=na_bc[:], in0=a_bc[:], scalar1=-1.0, scalar2=None, op0=ALU.mult),
         reads=['a_bc0'], writes=['a_bc'])

    xring = Ring(p, 4, [128, TT], F32, 'm0xr')
    hT = p.sbuf([128, DC, TT], BF16, name='m0h')
    sq_ring = [p.sbuf([128, TT], BF16, name=f'm0sq{i}') for i in range(2)]
    qT = p.sbuf([128, 2, TT], BF16, name='m0qT')
    kT = p.sbuf([128, 2, TT], BF16, name='m0kT')
    qsb = Ring(p, 1, [128, TT], F32, 'm0qsb')
    qsq = Ring(p, 2, [128, TT], BF16, 'm0qsq')
    qln = Ring(p, 1, [128, TT], F32, 'm0qln')
    qrs = Ring(p, 1, [128, TT], F32, 'm0qrs')
    rstd = qrs.tiles[0]
    tmpl = qln.tiles[0]
    cinr = Ring(p, 2, [128, TT + 3], F32, 'm0cin')
    chist = p.sbuf([128, 6, 3], F32, name='m0chist')
    cacc = Ring(p, 2, [128, TT], F32, 'm0cacc')
    ctmp = Ring(p, 1, [128, TT], F32, 'm0ctmp')
    xsT = p.sbuf([128, 4, TT], F32, name='m0xsT')
    BT = p.sbuf([128, TT], BF16, name='m0BT')
    CTt = p.sbuf([128, TT], BF16, name='m0CT')
    Vpad = [p.sbuf([128, 8, 128], BF16, name=f'm0V{i}') for i in range(2)]
    praw = Ring(p, 2, [128, 512], BF16, 'm0praw')
    pT = [Ring(p, 2, [128, 512], BF16, f'm0pT{i}') for i in range(2)]
    tden = Ring(p, 1, [128, TT], F32, 'm0tden')
    tden2 = Ring(p, 1, [128, TT], F32, 'm0tden2')
    mixst = [p.sbuf([128, 6, TT], BF16, name='m0mix0')] * 2
    H = p.sbuf([128, 512], F32, name='m0H')
    Hbf = p.sbuf([128, 512], BF16, name='m0Hbf')
    t8 = Ring(p, 4, [128, 8], F32, 'm0t8')
    e8 = Ring(p, 2, [128, 8], F32, 'm0e8')
    dt8 = Ring(p, 2, [128, 8], F32, 'm0dt8')
    dA8 = Ring(p, 2, [128, 8], F32, 'm0dA8')
    dAU = Ring(p, 1, [128, 8, 128], F32, 'm0dAU')
    cs16 = Ring(p, 2, [128, 16], F32, 'm0cs16')
    d8 = Ring(p, 2, [128, 8], F32, 'm0d8')
    ed8 = Ring(p, 2, [128, 8], F32, 'm0ed8')
    w28 = Ring(p, 2, [128, 8], F32, 'm0w28')
    ecum8 = Ring(p, 2, [128, 8], F32, 'm0ecum8')
    dect8 = Ring(p, 2, [128, 8], F32, 'm0dect8')
    Esb = Ring(p, 2, [128, 512], F32, 'm0E')
    MT = Ring(p, 2, [128, 8, 128], BF16, 'm0MT')
    xtok = Ring(p, 1, [128, 512], F32, 'm0xtok')
    xdt = Ring(p, 2, [128, 8, 64], BF16, 'm0xdt')
    xdec = Ring(p, 2, [128, 512], BF16, 'm0xdec')
    xD = Ring(p, 1, [128, 512], F32, 'm0xD')
    Btok = Ring(p, 2, [128, 128], BF16, 'm0Btok')
    y1 = Ring(p, 2, [128, 512], F32, 'm0y1')
    sz = Ring(p, 4, [128, 512], F32, 'm0sz')
    junk = Ring(p, 1, [128, 512], BF16, 'm0junk')
    ss1 = Ring(p, 2, [128, 1], F32, 'm0ss1')
    ln1 = Ring(p, 2, [128, 1], F32, 'm0ln1')
    rs1 = Ring(p, 2, [128, 1], F32, 'm0rs1')
    ybf = Ring(p, 2, [128, 512], BF16, 'm0ybf')
    psbf, psbfy = cx.psbf2

    p.op('pool', lambda e: e.memset(chist[:], 0.0), writes=[('cinh', c6) for c6 in range(6)])
    p.op('pool', lambda e: e.memset(kT[:], 0.0), writes=[('kT', 0), ('kT', 1)])
    for i in range(2):
        p.op('pool', lambda e, i=i: e.memset(Vpad[i][:], 0.0), writes=[('Vpad', i, s) for s in range(8)])
    p.op('pool', lambda e: e.memset(H[:], 0.0), writes=['H'])
    p.op('pool', lambda e: e.memset(Hbf[:], 0.0), writes=['Hbf'])

    for t in range(nt):
        ts = slice(t * TT, (t + 1) * TT)
        sl = t % 2
        emit_rmsnorm_stream(cx, lambda c: d['xT'][c * 128:(c + 1) * 128, ts], DC, TT, gmix, hT, lambda c: ('m0h', c),
                            xring, sq_ring, rstd, tmpl, D_MODEL, 'sp', 'm0x')
        hreads = [('m0h', c) for c in range(DC)]
        mst = mixst[sl]
        mtok = lambda c: ('mixst', 0, c)

        def proj_fm(col0):
            b = cx.bank()
            for dc in range(DC):
                p.op('pe', lambda e, b=b, dc=dc: e.matmul(cx.ps[b][:, :TT], lhsT=W[:, dc, col0:col0 + 128],
                                                          rhs=hT[:, dc, :], start=(dc == 0), stop=(dc == DC - 1)),
                     reads=[('m0h', dc), Wtok[dc // 4]], writes=[('ps', b)])
            return b

        for qi in range(3):
            b = proj_fm(QO + qi * 128)
            qs, qst = qsb.next()
            sq, sqt = qsq.next()
            p.op('act', lambda e, b=b, qs=qs: e.activation(out=qs[:], in_=cx.ps[b][:, :TT], func=AF.Copy),
                 reads=[('ps', b)], writes=[qst])
            p.op('act', lambda e, b=b, sq=sq: e.activation(out=sq[:], in_=cx.ps[b][:, :TT], func=AF.Square),
                 reads=[('ps', b)], writes=[sqt])
            b2 = cx.bank()
            p.op('pe', lambda e, b2=b2, sq=sq: e.matmul(cx.ps[b2][:, :TT], lhsT=C['blockones'][:], rhs=sq[:],
                                                         start=True, stop=True),
                 reads=[sqt, CT('blockones')], writes=[('ps', b2)])
            ln, lnt = qln.next()
            rs, rst = qrs.next()
            p.op('act', lambda e, b2=b2, ln=ln: e.activation(out=ln[:], in_=cx.ps[b2][:, :TT], func=AF.Ln,
                                                              bias=cx.eps[:, 0:1], scale=1.0 / 64),
                 reads=[('ps', b2), 'eps_t'], writes=[lnt])
            p.op('act', lambda e, ln=ln, rs=rs: e.activation(out=rs[:], in_=ln[:], func=AF.Exp, scale=-0.5),
                 reads=[lnt], writes=[rst])
            if qi < 2:
                dst, dtok, gcol = qT[:, qi, :], ('qT', qi), 0
            else:
                dst, dtok, gcol = kT[:, sl, :], ('kT', sl), 1
            p.op('dve', lambda e, qs=qs, rs=rs, dst=dst, gcol=gcol: e.scalar_tensor_tensor(
                out=dst, in0=qs[:], scalar=small['qk_gain'][:, gcol:gcol + 1], in1=rs[:], op0=ALU.mult, op1=ALU.mult),
                reads=[qst, rst, ('m0s', 'qk_gain')], writes=[dtok])

        for c6 in range(6):
            b = proj_fm(XO + c6 * 128)
            cin, cint = cinr.next()
            p.op('act', lambda e, b=b, cin=cin: e.activation(out=cin[:, 3:3 + TT], in_=cx.ps[b][:, :TT], func=AF.Copy),
                 reads=[('ps', b)], writes=[(cint, 'd')])
            p.op('pool', lambda e, cin=cin, c6=c6: e.tensor_copy(out=cin[:, 0:3], in_=chist[:, c6, :]),
                 reads=[('cinh', c6)], writes=[(cint, 'h')])
            acc, acct = cacc.next()
            cw = small['conv_w']
            p.op('act', lambda e, acc=acc, cin=cin, c6=c6: e.activation(out=acc[:], in_=cin[:, 0:TT], func=AF.Copy, scale=cw[:, c6, 0:1]),
                 reads=[(cint, 'd'), (cint, 'h'), ('m0s', 'conv_w')], writes=[acct])
            for j in range(1, 4):
                p.op('dve', lambda e, acc=acc, cin=cin, c6=c6, j=j: e.scalar_tensor_tensor(
                    out=acc[:], in0=cin[:, j:j + TT], scalar=cw[:, c6, j:j + 1], in1=acc[:], op0=ALU.mult, op1=ALU.add),
                    reads=[(cint, 'd'), (cint, 'h'), acct], writes=[acct])
            p.op('pool', lambda e, cin=cin, c6=c6: e.tensor_copy(out=chist[:, c6, :], in_=cin[:, TT:TT + 3]),
                 reads=[(cint, 'd')], writes=[('cinh', c6)])
            if c6 < 4:
                dst, dtok = xsT[:, c6, :], ('xsT', c6)
            elif c6 == 4:
                dst, dtok = BT[:], 'BT'
            else:
                dst, dtok = CTt[:], 'CT'
            p.op('act', lambda e, acc=acc, dst=dst, c6=c6: e.activation(out=dst, in_=acc[:], func=AF.Silu,
                                                                         bias=small['conv_b'][:, c6:c6 + 1]),
                 reads=[acct, ('m0s', 'conv_b')], writes=[dtok])

        pv = []
        szl = []
        for ci in range(4):
            cs = slice(ci * 128, (ci + 1) * 128)
            bv = cx.bank()
            for dc in range(DC):
                p.op('pe', lambda e, bv=bv, dc=dc, cs=cs: e.matmul(cx.ps[bv][:, 0:72], lhsT=hT[:, dc, cs],
                                                                   rhs=W[:, dc, VO:VO + 72], start=(dc == 0), stop=(dc == DC - 1)),
                     reads=[('m0h', dc), Wtok[dc // 4]], writes=[('ps', bv)])
            G = t * 4 + ci
            vs = G % 8
            p.op('act', lambda e, bv=bv, vs=vs: e.activation(out=Vpad[0][:, vs, 0:64], in_=cx.ps[bv][:, 0:64], func=AF.Copy),
                 reads=[('ps', bv)], writes=[('Vpad', 0, vs)])
            p.op('act', lambda e, bv=bv, vs=vs: e.activation(out=Vpad[1][:, vs, 64:128], in_=cx.ps[bv][:, 0:64], func=AF.Copy),
                 reads=[('ps', bv)], writes=[('Vpad', 1, vs)])
            tt8, tt8t = t8.next()
            p.op('dve', lambda e, bv=bv, tt8=tt8: e.tensor_tensor(out=tt8[:], in0=cx.ps[bv][:, 64:72], in1=small['dtb'][:], op=ALU.add),
                 reads=[('ps', bv), ('m0s', 'dtb')], writes=[tt8t])
            pv.append((tt8, tt8t))
            bz = cx.bank()
            for dc in range(DC):
                p.op('pe', lambda e, bz=bz, dc=dc, cs=cs: e.matmul(cx.ps[bz][:, :], lhsT=hT[:, dc, cs], rhs=W[:, dc, ZO:ZO + 512],
                                                                   start=(dc == 0), stop=(dc == DC - 1)),
                     reads=[('m0h', dc), Wtok[dc // 4]], writes=[('ps', bz)])
            sz_, szt = sz.next()
            p.op('act', lambda e, bz=bz, sz_=sz_: e.activation(out=sz_[:], in_=cx.ps[bz][:, :], func=AF.Silu),
                 reads=[('ps', bz)], writes=[szt])
            szl.append((sz_, szt))

        for pr in range(2):
            nb = cx.bank()
            db = cx.bank()
            for cp in range(2):
                pts = []
                for hp in range(2):
                    rows = slice(64 * hp, 64 * hp + 64)
                    b = cx.bank()
                    for cj in range(2):
                        ci = cp * 2 + cj
                        qs_ = qT[rows, pr, ci * 128:(ci + 1) * 128]
                        if ci == 0:
                            kprev, kpt = kT[rows, 1 - sl, 384:512], ('kT', 1 - sl)
                        else:
                            kprev, kpt = kT[rows, sl, (ci - 1) * 128:ci * 128], ('kT', sl)
                        kown = kT[rows, sl, ci * 128:(ci + 1) * 128]
                        p.op('pe', lambda e, b=b, cj=cj, kprev=kprev, qs_=qs_: e.matmul(
                            cx.ps[b][:, (cj * 2) * 128:(cj * 2 + 1) * 128], lhsT=kprev, rhs=qs_, start=True, stop=True),
                            reads=[kpt, ('qT', pr)], writes=[('ps', b)])
                        p.op('pe', lambda e, b=b, cj=cj, kown=kown, qs_=qs_: e.matmul(
                            cx.ps[b][:, (cj * 2 + 1) * 128:(cj * 2 + 2) * 128], lhsT=kown, rhs=qs_, start=True, stop=True),
                            reads=[('kT', sl), ('qT', pr)], writes=[('ps', b)])
                    pr_, prt = praw.next()
                    p.op('act', lambda e, b=b, pr_=pr_: e.activation(out=pr_[:], in_=cx.ps[b][:, :], func=AF.Exp, scale=0.125),
                         reads=[('ps', b)], writes=[prt])
                    pt_, ptt = pT[hp].next()
                    p.op('pool', lambda e, pr_=pr_, pt_=pt_: e.tensor_tensor(out=pt_[:], in0=pr_[:], in1=C['swamask'][:], op=ALU.mult),
                         reads=[prt, CT('swamask')], writes=[ptt])
                    pts.append((pt_, ptt))
                for cj in range(2):
                    ci = cp * 2 + cj
                    G = t * 4 + ci
                    terms = []
                    for hp in range(2):
                        for kb in range(2):
                            if G == 0 and kb == 0:
                                continue
                            terms.append((hp, kb))
                    for which, bank, in ((0, nb), (1, db)):
                        for n_, (hp, kb) in enumerate(terms):
                            vslot = (G - 1 + kb) % 8
                            if which == 0:
                                lhsT, lt = Vpad[hp][:, vslot, :], ('Vpad', hp, vslot)
                            else:
                                lhsT, lt = C['onespad'][:, hp * 128:(hp + 1) * 128], CT('onespad')
                            rhs = pts[hp][0][:, (cj * 2 + kb) * 128:(cj * 2 + kb + 1) * 128]
                            p.op('pe', lambda e, bank=bank, ci=ci, lhsT=lhsT, rhs=rhs, n_=n_, nn=len(terms): e.matmul(
                                cx.ps[bank][:, ci * 128:(ci + 1) * 128], lhsT=lhsT, rhs=rhs, start=(n_ == 0), stop=(n_ == nn - 1)),
                                reads=[lt, pts[hp][1]], writes=[('ps', bank)])
            td, tdt = tden.next()
            td2, td2t = tden2.next()
            p.op('dve', lambda e, db=db, td=td, pr=pr: e.tensor_scalar(out=td[:], in0=cx.ps[db][:, :TT], scalar1=esink[:, pr:pr + 1],
                                                                        scalar2=None, op0=ALU.add),
                 reads=[('ps', db), 'esink'], writes=[tdt])
            p.op('dve', lambda e, td=td, td2=td2: e.reciprocal(out=td2[:], in_=td[:]), reads=[tdt], writes=[td2t])
            p.op('dve', lambda e, nb=nb, td2=td2, pr=pr: e.tensor_tensor(out=mst[:, pr, :], in0=cx.ps[nb][:, :TT], in1=td2[:], op=ALU.mult),
                 reads=[('ps', nb), td2t], writes=[mtok(pr)])

        for ci in range(4):
            cs = slice(ci * 128, (ci + 1) * 128)
            tt8, tt8t = pv[ci]
            ee, eet = e8.next()
            dt_, dtt = dt8.next()
            dA, dAt = dA8.next()
            p.op('act', lambda e, tt8=tt8, ee=ee: e.activation(out=ee[:], in_=tt8[:], func=AF.Exp), reads=[tt8t], writes=[eet])
            p.op('act', lambda e, ee=ee, dt_=dt_: e.activation(out=dt_[:], in_=ee[:], func=AF.Ln, bias=1.0), reads=[eet], writes=[dtt])
            p.op('dve', lambda e, dt_=dt_, dA=dA: e.tensor_tensor(out=dA[:], in0=dt_[:], in1=na_bc[:], op=ALU.mult),
                 reads=[dtt, 'a_bc'], writes=[dAt])
            dau, daut = dAU.next()
            p.op('pool', lambda e, dau=dau, dA=dA: e.tensor_tensor(out=dau[:], in0=C['U'][:].unsqueeze(1).to_broadcast([128, 8, 128]),
                                                                    in1=dA[:].unsqueeze(2).to_broadcast([128, 8, 128]), op=ALU.mult),
                 reads=[dAt, CT('U')], writes=[daut])
            bs = cx.bank()
            p.op('pe', lambda e, bs=bs, dA=dA: e.matmul(cx.ps[bs][:, 0:8], lhsT=C['U'][:], rhs=dA[:], start=True, stop=True),
                 reads=[dAt, CT('U')], writes=[('ps', bs)])
            p.op('pe', lambda e, bs=bs, dA=dA: e.matmul(cx.ps[bs][:, 8:16], lhsT=C['ones_f'][:], rhs=dA[:], start=True, stop=True),
                 reads=[dAt, CT('ones_f')], writes=[('ps', bs)])
            cs_, cst = cs16.next()
            p.op('act', lambda e, bs=bs, cs_=cs_: e.activation(out=cs_[:], in_=cx.ps[bs][:, 0:16], func=AF.Copy),
                 reads=[('ps', bs)], writes=[cst])
            dd, ddt = d8.next()
            p.op('dve', lambda e, cs_=cs_, dd=dd: e.tensor_tensor(out=dd[:], in0=cs_[:, 8:16], in1=cs_[:, 0:8], op=ALU.subtract),
                 reads=[cst], writes=[ddt])
            ed, edt = ed8.next()
            p.op('act', lambda e, dd=dd, ed=ed: e.activation(out=ed[:], in_=dd[:], func=AF.Exp), reads=[ddt], writes=[edt])
            w2, w2t = w28.next()
            p.op('dve', lambda e, ed=ed, dt_=dt_, w2=w2: e.tensor_tensor(out=w2[:], in0=ed[:], in1=dt_[:], op=ALU.mult),
                 reads=[edt, dtt], writes=[w2t])
            ec, ect = ecum8.next()
            p.op('act', lambda e, cs_=cs_, ec=ec: e.activation(out=ec[:], in_=cs_[:, 0:8], func=AF.Exp), reads=[cst], writes=[ect])
            dct, dctt = dect8.next()
            p.op('act', lambda e, cs_=cs_, dct=dct: e.activation(out=dct[:], in_=cs_[:, 8:16], func=AF.Exp), reads=[cst], writes=[dctt])
            bcb = cx.bank()
            p.op('pe', lambda e, bcb=bcb, cs=cs: e.matmul(cx.ps[bcb][:, 0:128], lhsT=BT[:, cs], rhs=CTt[:, cs], start=True, stop=True),
                 reads=['BT', 'CT'], writes=[('ps', bcb)])
            mt_, mtt = MT.next()
            for half in range(2):
                bsg = cx.bank()
                hs = slice(half * 4, half * 4 + 4)
                p.op('pe', lambda e, bsg=bsg, dau=dau, hs=hs: e.matmul(cx.ps[bsg][:, :], lhsT=C['ones_f'][:],
                                                                       rhs=dau[:, hs, :], start=True, stop=False),
                     reads=[daut, CT('ones_f')], writes=[('ps', bsg)])
                p.op('pe', lambda e, bsg=bsg, dA=dA, hs=hs: e.matmul(cx.ps[bsg][:, :], lhsT=C['negU'][:],
                                                                      rhs=dA[:, hs].unsqueeze(2).to_broadcast([128, 4, 128]),
                                                                      start=False, stop=False),
                     reads=[dAt, CT('negU')], writes=[('ps', bsg)])
                p.op('pe', lambda e, bsg=bsg: e.matmul(cx.ps[bsg][:, :], lhsT=C['ident_b'][:], rhs=C['negmask4'][:],
                                                       start=False, stop=True),
                     reads=[CT('ident_b'), CT('negmask4')], writes=[('ps', bsg)])
                E_, Et = Esb.next()
                p.op('act', lambda e, bsg=bsg, E_=E_: e.activation(out=E_[:], in_=cx.ps[bsg][:, :], func=AF.Exp),
                     reads=[('ps', bsg)], writes=[Et])
                p.op('dve', lambda e, E_=E_, mt_=mt_, hs=hs, bcb=bcb: e.tensor_tensor(
                    out=mt_[:, hs, :], in0=E_[:].rearrange("p (h t) -> p h t", h=4),
                    in1=cx.ps[bcb][:, 0:128].unsqueeze(1).to_broadcast([128, 4, 128]), op=ALU.mult),
                    reads=[Et, ('ps', bcb)], writes=[(mtt, half)])
            bx = cx.bank()
            for c in range(4):
                p.op('pe', lambda e, bx=bx, c=c, cs=cs: e.transpose(out=cx.ps[bx][:, c * 128:(c + 1) * 128], in_=xsT[:, c, cs],
                                                                    identity=C['ident_f'][:]),
                     reads=[('xsT', c), CT('ident_f')], writes=[('ps', bx)])
            xk, xkt = xtok.next()
            p.op('act', lambda e, bx=bx, xk=xk: e.activation(out=xk[:], in_=cx.ps[bx][:, :], func=AF.Copy),
                 reads=[('ps', bx)], writes=[xkt])
            xd, xdt_t = xdt.next()
            xk3 = xk[:].rearrange("p (h j) -> p h j", h=8)
            p.op('dve', lambda e, xk3=xk3, xd=xd, dt_=dt_: e.tensor_tensor(out=xd[:], in0=xk3,
                                                                           in1=dt_[:].unsqueeze(2).to_broadcast([128, 8, 64]), op=ALU.mult),
                 reads=[xkt, dtt], writes=[xdt_t])
            xc, xct = xdec.next()
            p.op('dve', lambda e, xk3=xk3, xc=xc, w2=w2: e.tensor_tensor(out=xc[:].rearrange("p (h j) -> p h j", h=8), in0=xk3,
                                                                         in1=w2[:].unsqueeze(2).to_broadcast([128, 8, 64]), op=ALU.mult),
                 reads=[xkt, w2t], writes=[xct])
            xD_, xDt = xD.next()
            p.op('pool', lambda e, xk3=xk3, xD_=xD_: e.tensor_tensor(out=xD_[:].rearrange("p (h j) -> p h j", h=8), in0=xk3,
                                                                     in1=small['dskip'][:].unsqueeze(2).to_broadcast([128, 8, 64]), op=ALU.mult),
                 reads=[xkt, ('m0s', 'dskip')], writes=[xDt])
            p.op('pe', lambda e, cs=cs: e.transpose(out=psbf[:, 0:128], in_=BT[:, cs], identity=C['ident_b'][:]),
                 reads=['BT', CT('ident_b')], writes=['psbf'])
            bt_, btt = Btok.next()
            p.op('act', lambda e, bt_=bt_: e.activation(out=bt_[:], in_=psbf[:, 0:128], func=AF.Copy), reads=['psbf'], writes=[btt])
            bst = cx.bank()
            p.op('pe', lambda e, bst=bst, bt_=bt_, xc=xc: e.matmul(cx.ps[bst][:, :], lhsT=bt_[:], rhs=xc[:], start=True, stop=True),
                 reads=[btt, xct], writes=[('ps', bst)])
            byo = cx.bank()
            p.op('pe', lambda e, byo=byo, cs=cs: e.matmul(cx.ps[byo][:, :], lhsT=CTt[:, cs], rhs=Hbf[:], start=True, stop=True),
                 reads=['CT', 'Hbf'], writes=[('ps', byo)])
            byd = cx.bank()
            for h in range(8):
                p.op('pe', lambda e, byd=byd, h=h, mt_=mt_, xd=xd: e.matmul(cx.ps[byd][:, h * 64:(h + 1) * 64], lhsT=mt_[:, h, :],
                                                                            rhs=xd[:, h, :], start=True, stop=True),
                     reads=[(mtt, h // 4), xdt_t], writes=[('ps', byd)])
            ya, yat = y1.next()
            p.op('dve', lambda e, byo=byo, ya=ya, ec=ec: e.tensor_tensor(out=ya[:].rearrange("p (h j) -> p h j", h=8),
                                                                         in0=cx.ps[byo][:, :].rearrange("p (h j) -> p h j", h=8),
                                                                         in1=ec[:].unsqueeze(2).to_broadcast([128, 8, 64]), op=ALU.mult),
                 reads=[('ps', byo), ect], writes=[yat])
            p.op('dve', lambda e, byd=byd, ya=ya: e.tensor_tensor(out=ya[:], in0=ya[:], in1=cx.ps[byd][:, :], op=ALU.add),
                 reads=[('ps', byd), yat], writes=[yat])
            p.op('pool', lambda e, ya=ya, xD_=xD_: e.tensor_tensor(out=ya[:], in0=ya[:], in1=xD_[:], op=ALU.add),
                 reads=[yat, xDt], writes=[yat])
            yc, yct = ya, yat
            p.op('dve', lambda e, dct=dct: e.tensor_tensor(out=H[:].rearrange("p (h j) -> p h j", h=8),
                                                           in0=H[:].rearrange("p (h j) -> p h j", h=8),
                                                           in1=dct[:].unsqueeze(2).to_broadcast([128, 8, 64]), op=ALU.mult),
                 reads=['H', dctt], writes=['H'])
            p.op('dve', lambda e, bst=bst: e.tensor_tensor(out=H[:], in0=H[:], in1=cx.ps[bst][:, :], op=ALU.add),
                 reads=['H', ('ps', bst)], writes=['H'])
            p.op('pool', lambda e: e.tensor_copy(out=Hbf[:], in_=H[:]), reads=['H'], writes=['Hbf'])
            sz_, szt = szl[ci]
            gy_, gyt = yc, yct
            p.op('pool', lambda e, gy_=gy_, sz_=sz_: e.tensor_tensor(out=gy_[:], in0=gy_[:], in1=sz_[:], op=ALU.mult),
                 reads=[yct, szt], writes=[gyt])
            ss_, sst = ss1.next()
            jk, jkt = junk.next()
            p.op('pool', lambda e, ss_=ss_: e.memset(ss_[:], 0.0), writes=[sst])
            p.op('act', lambda e, gy_=gy_, jk=jk, ss_=ss_: e.activation(out=jk[:], in_=gy_[:], func=AF.Square, accum_out=ss_[:, 0:1]),
                 reads=[gyt, sst], writes=[jkt, sst])
            l1, l1t = ln1.next()
            r1, r1t = rs1.next()
            p.op('act', lambda e, ss_=ss_, l1=l1: e.activation(out=l1[:], in_=ss_[:], func=AF.Ln, bias=cx.eps[:, 0:1], scale=1.0 / 512),
                 reads=[sst, 'eps_t'], writes=[l1t])
            p.op('act', lambda e, l1=l1, r1=r1: e.activation(out=r1[:], in_=l1[:], func=AF.Exp, scale=-0.5), reads=[l1t], writes=[r1t])
            yf, yft = ybf.next()
            p.op('dve', lambda e, gy_=gy_, r1=r1, yf=yf: e.scalar_tensor_tensor(out=yf[:], in0=gy_[:], scalar=r1[:, 0:1], in1=small['gate'][:],
                                                                                op0=ALU.mult, op1=ALU.mult),
                 reads=[gyt, r1t, ('m0s', 'gate')], writes=[yft])
            for c in range(4):
                p.op('pe', lambda e, c=c, yf=yf: e.transpose(out=psbfy[:, c * 128:(c + 1) * 128], in_=yf[:, c * 128:(c + 1) * 128],
                                                              identity=C['ident_b'][:]),
                     reads=[yft, CT('ident_b')], writes=['psbf_y'])
            p.op('act', lambda e, cs=cs: e.activation(out=mst[:, 2:6, cs], in_=psbfy[:, 0:512].rearrange("p (c t) -> p c t", c=4), func=AF.Copy),
                 reads=['psbf_y'], writes=[mtok(2 + ci)])
        if out_tile is None:
            p.dma('pool', d['mixT'][:, ts].rearrange("(c p) t -> p c t", p=128), mst[:],
                  reads=[mtok(c) for c in range(6)], writes=['mixT_out'], key='m0out')
        else:
            out_tile(t, mst, [mtok(c) for c in range(6)])
        if per_tile is not None:
            per_tile(t)


AB_Q, AB_K, AB_V, AB_Z, AB_X, AB_DT = 0, 1024, 1152, 1280, 3328, 6400


def m0_host_inputs(inp, b, g, S):
    w = inp['ab_w_in'][0]
    kv = g // 2
    cols = np.concatenate([
        np.arange(AB_Q + g * 256, AB_Q + (g + 1) * 256),
        np.arange(AB_K + kv * 64, AB_K + (kv + 1) * 64), np.arange(AB_K + kv * 64, AB_K + (kv + 1) * 64),
        np.arange(AB_X + g * 512, AB_X + (g + 1) * 512),
        np.arange(AB_X + 2048 + g * 128, AB_X + 2048 + (g + 1) * 128),
        np.arange(AB_X + 2560 + g * 128, AB_X + 2560 + (g + 1) * 128),
        np.arange(AB_V + kv * 64, AB_V + (kv + 1) * 64),
        np.arange(AB_DT + g * 8, AB_DT + (g + 1) * 8),
        np.arange(AB_Z + g * 512, AB_Z + (g + 1) * 512)])
    assert len(cols) == M0_NCOL
    chan = np.concatenate([np.arange(g * 512, (g + 1) * 512), 2048 + np.arange(g * 128, (g + 1) * 128),
                           2560 + np.arange(g * 128, (g + 1) * 128)])
    cw = inp['ab_conv_w'][0][:, chan]
    cb = inp['ab_conv_b'][0][chan]
    sk = inp['ab_sinks'][0][4 * g:4 * g + 4]
    rep = lambda v: np.ascontiguousarray(np.broadcast_to(np.asarray(v, np.float32)[None, :], (128, len(v))))
    d = dict(
        w_in=np.ascontiguousarray(w[:, cols]),
        g_mix=np.ascontiguousarray(inp['norm_mix'][0]),
        qk_gain=np.ascontiguousarray(np.stack([np.tile(inp['ab_q_norm'][0], 2), np.tile(inp['ab_k_norm'][0], 2)], 1)),
        sink2=np.ascontiguousarray(np.stack([np.repeat(sk[0:2], 64), np.repeat(sk[2:4], 64)], 1)),
        conv_w=np.ascontiguousarray(cw.reshape(4, 6, 128).transpose(2, 1, 0)),
        conv_b=np.ascontiguousarray(cb.reshape(6, 128).T),
        dtb=rep(inp['ab_dt_bias'][0][8 * g:8 * g + 8]),
        alog=rep(inp['ab_a_log'][0][8 * g:8 * g + 8]),
        dskip=rep(inp['ab_d_skip'][0][8 * g:8 * g + 8]),
        gate=rep(inp['ab_gate_norm'][0][512 * g:512 * (g + 1)]),
    )
    return {k: np.asarray(v, np.float32) for k, v in d.items()}


M0_SMALL = dict(qk_gain=[128, 2], sink2=[128, 2], conv_w=[128, 6, 4], conv_b=[128, 6], dtb=[128, 8],
                alog=[128, 8], dskip=[128, 8], gate=[128, 512])
M0_CONSTS = ['ident_f', 'ident_b', 'U', 'negU', 'ones_f', 'negmask4', 'swamask', 'blockones', 'onespad']


def const_inputs(names):
    hc = host_consts()
    return {"c_" + n: hc[n] for n in names}


def build_mixer0_program(S):
    nc = bass.Bass("TRN2", target_bir_lowering=False)
    d = {}
    d['xT'] = nc.dram_tensor("xT", [D_MODEL, S], F32, kind="ExternalInput").ap()
    d['w_in'] = nc.dram_tensor("w_in", [D_MODEL, M0_NCOL], F32, kind="ExternalInput").ap()
    d['g_mix'] = nc.dram_tensor("g_mix", [D_MODEL], F32, kind="ExternalInput").ap()
    for n, shp in M0_SMALL.items():
        d[n] = nc.dram_tensor(n, shp, F32, kind="ExternalInput").ap()
    d['mixT'] = nc.dram_tensor("mixT", [768, S], BF16, kind="ExternalOutput").ap()
    p = Prog(nc)
    cx = Ctx(p)
    load_consts(cx, nc, M0_CONSTS)
    emit_mixer0(cx, S, d)
    p.emit(final_wait_keys=['m0out'])
    return nc


def emit_mixer1(cx, S, d, tag='m1', h_src=None, out_pair=None):
    p = cx.p
    C = cx.c
    TT = 512
    ST = 2048
    nst = S // ST
    CT = lambda n: ('const', n)
    W = p.sbuf([128, DC, 1536], BF16, name='m1W')
    for c4 in range(4):
        p.dma('pool', W[:, c4 * 4:(c4 + 1) * 4, :],
              d['w_qkv'][c4 * 512:(c4 + 1) * 512, :].rearrange("(c p) n -> p c n", p=128),
              writes=[('m1W', c4)], key='m1w')
    Wtok = [('m1W', c4) for c4 in range(4)]
    qkg = p.sbuf([128, 2], F32, name='m1qkg')
    p.dma('sp', qkg[:], d['qk_gain'], writes=[('m1s', 'qk_gain')], key='m1s')
    hT = p.sbuf([128, DC, TT], BF16, name='m1h')
    qT = p.sbuf([128, 4, ST], BF16, name='m1qT')
    kT = p.sbuf([128, 4, 2, ST], BF16, name='m1kT')
    vT = p.sbuf([128, 4, 2, ST], BF16, name='m1vT')
    accN = p.sbuf([128, ST], F32, name='m1accN')
    accD = p.sbuf([128, ST], F32, name='m1accD')
    oT = p.sbuf([128, ST], BF16, name='m1oT')
    qsb = Ring(p, 1, [128, TT], F32, 'm1qsb')
    qsq = Ring(p, 2, [128, TT], BF16, 'm1qsq')
    qln = Ring(p, 1, [128, TT], F32, 'm1qln')
    qrs = Ring(p, 1, [128, TT], F32, 'm1qrs')
    praw = Ring(p, 3, [128, 512], BF16, 'm1praw')
    pT = [Ring(p, 3, [128, 512], BF16, f'm1pT{i}') for i in range(2)]
    VE = Ring(p, 16, [128, 128], BF16, 'm1VE')
    VO = Ring(p, 16, [128, 128], BF16, 'm1VO')
    vcache = {}
    psbf2 = cx.psbf2
    for r_ in (VE, VO):
        for i, t_ in enumerate(r_.tiles):
            p.op('pool', lambda e, t_=t_: e.memset(t_[:], 0.0), writes=[(r_.name, i)])

    for M in range(nst):
        sl = M % 2
        for tt in range(4):
            t = M * 4 + tt
            ts = slice(t * TT, (t + 1) * TT)
            lc = slice(tt * TT, (tt + 1) * TT)
            for c4 in range(4):
                p.dma('sp', hT[:, c4 * 4:(c4 + 1) * 4, :],
                      (d['hT'][c4 * 512:(c4 + 1) * 512, ts] if h_src is None else h_src(t, c4)).rearrange("(c p) t -> p c t", p=128),
                      writes=[('m1h', c4 * 4 + j) for j in range(4)], key=f'm1h{c4}')
            for fc in range(12):
                b = cx.bank()
                for dc in range(DC):
                    p.op('pe', lambda e, b=b, dc=dc, fc=fc: e.matmul(cx.ps[b][:, :TT], lhsT=W[:, dc, fc * 128:(fc + 1) * 128],
                                                                     rhs=hT[:, dc, :], start=(dc == 0), stop=(dc == DC - 1)),
                         reads=[('m1h', dc), Wtok[dc // 4]], writes=[('ps', b)])
                kind, c = fc // 4, fc % 4
                if kind == 2:
                    vdst = vT[:, c, sl, lc]
                    p.op('act', lambda e, b=b, vdst=vdst: e.activation(out=vdst, in_=cx.ps[b][:, :TT], func=AF.Copy),
                         reads=[('ps', b)], writes=[('vT', c, sl, tt)])
                    continue
                qs, qst = qsb.next()
                sq, sqt = qsq.next()
                p.op('act', lambda e, b=b, qs=qs: e.activation(out=qs[:], in_=cx.ps[b][:, :TT], func=AF.Copy),
                     reads=[('ps', b)], writes=[qst])
                p.op('act', lambda e, b=b, sq=sq: e.activation(out=sq[:], in_=cx.ps[b][:, :TT], func=AF.Square),
                     reads=[('ps', b)], writes=[sqt])
                b2 = cx.bank()
                p.op('pe', lambda e, b2=b2, sq=sq: e.matmul(cx.ps[b2][:, :TT], lhsT=C['blockones'][:], rhs=sq[:], start=True, stop=True),
                     reads=[sqt, CT('blockones')], writes=[('ps', b2)])
                ln, lnt = qln.next()
                rs, rst = qrs.next()
                p.op('act', lambda e, b2=b2, ln=ln: e.activation(out=ln[:], in_=cx.ps[b2][:, :TT], func=AF.Ln,
                                                                  bias=cx.eps[:, 0:1], scale=1.0 / 64),
                     reads=[('ps', b2), 'eps_t'], writes=[lnt])
                p.op('act', lambda e, ln=ln, rs=rs: e.activation(out=rs[:], in_=ln[:], func=AF.Exp, scale=-0.5),
                     reads=[lnt], writes=[rst])
                if kind == 0:
                    dst, dtok = qT[:, c, lc], ('qT1', c, tt)
                else:
                    dst, dtok = kT[:, c, sl, lc], ('kT1', c, sl, tt)
                p.op('dve', lambda e, qs=qs, rs=rs, dst=dst, kind=kind: e.scalar_tensor_tensor(
                    out=dst, in0=qs[:], scalar=qkg[:, kind:kind + 1], in1=rs[:], op0=ALU.mult, op1=ALU.mult),
                    reads=[qst, rst, ('m1s', 'qk_gain')], writes=[dtok])

        def colslice(start, r):
            return slice(start, start + 127 * r + 1, r)

        def toks(name, c, slot, start, r):
            lo = start // TT
            hi = (start + 127 * r) // TT
            return [(name, c, slot, j) for j in range(lo, hi + 1)]

        for pr in range(4):
            for r in (1, 4, 16):
                nblk = ST // (128 * r)
                if r == 1:
                    quads = [[(0, m) for m in range(q4 * 4, q4 * 4 + 4)] for q4 in range(4)]
                elif r == 4:
                    quads = [[(c, m) for c in range(4)] for m in range(4)]
                else:
                    quads = [[(c, 0) for c in range(q4 * 4, q4 * 4 + 4)] for q4 in range(4)]
                for qi, quad in enumerate(quads):
                    nb = cx.bank()
                    db = cx.bank()
                    for half in range(2):
                        units = quad[half * 2:half * 2 + 2]
                        info = []
                        for (c, m) in units:
                            own_start = r * 128 * m + c
                            if m > 0:
                                prev = (sl, r * 128 * (m - 1) + c)
                            elif M > 0:
                                prev = (1 - sl, ST - 128 * r + c)
                            else:
                                prev = None
                            info.append((own_start, prev))
                        pts = []
                        for hp in range(2):
                            rows = slice(64 * hp, 64 * hp + 64)
                            b = cx.bank()
                            for uj, (own_start, prev) in enumerate(info):
                                qcols = colslice(own_start, r)
                                qs_ = qT[rows, pr, qcols]
                                qtk = [('qT1', pr, j) for j in range(own_start // TT, (own_start + 127 * r) // TT + 1)]
                                if prev is not None:
                                    kp = kT[rows, pr, prev[0], colslice(prev[1], r)]
                                    p.op('pe', lambda e, b=b, uj=uj, kp=kp, qs_=qs_: e.matmul(
                                        cx.ps[b][:, (uj * 2) * 128:(uj * 2 + 1) * 128], lhsT=kp, rhs=qs_, start=True, stop=True),
                                        reads=toks('kT1', pr, prev[0], prev[1], r) + qtk, writes=[('ps', b)])
                                ko = kT[rows, pr, sl, qcols]
                                p.op('pe', lambda e, b=b, uj=uj, ko=ko, qs_=qs_: e.matmul(
                                    cx.ps[b][:, (uj * 2 + 1) * 128:(uj * 2 + 2) * 128], lhsT=ko, rhs=qs_, start=True, stop=True),
                                    reads=toks('kT1', pr, sl, own_start, r) + qtk, writes=[('ps', b)])
                            pr_, prt = praw.next()
                            p.op('act', lambda e, b=b, pr_=pr_: e.activation(out=pr_[:], in_=cx.ps[b][:, :], func=AF.Exp, scale=0.125),
                                 reads=[('ps', b)], writes=[prt])
                            pt_, ptt = pT[hp].next()
                            p.op('pool', lambda e, pr_=pr_, pt_=pt_: e.tensor_tensor(out=pt_[:], in0=pr_[:], in1=C['dilmask'][:], op=ALU.mult),
                                 reads=[prt, CT('dilmask')], writes=[ptt])
                            pts.append((pt_, ptt))
                        for uj, (own_start, prev) in enumerate(info):
                            u = half * 2 + uj
                            vtiles = {}
                            for kb, src in ((0, prev), (1, (sl, own_start))):
                                if src is None:
                                    continue
                                ck = (M, pr, r, src[0], src[1])
                                if ck in vcache and VE.i - vcache[ck][2] <= 12:
                                    vtiles[kb] = vcache[ck][:2]
                                    continue
                                boff = 0
                                psbf = psbf2[kb]
                                vsrc = vT[:, pr, src[0], colslice(src[1], r)]
                                p.op('pe', lambda e, vsrc=vsrc, boff=boff, psbf=psbf: e.transpose(out=psbf[:, boff:boff + 128],
                                                                                        in_=vsrc, identity=C['ident_b'][:]),
                                     reads=toks('vT', pr, src[0], src[1], r) + [CT('ident_b')], writes=[('psbf', kb)])
                                ve, vet = VE.next()
                                vo, vot = VO.next()
                                p.op('act', lambda e, ve=ve, boff=boff, psbf=psbf: e.activation(out=ve[:, 0:64], in_=psbf[:, boff:boff + 64], func=AF.Copy),
                                     reads=[('psbf', kb)], writes=[vet])
                                p.op('act', lambda e, vo=vo, boff=boff, psbf=psbf: e.activation(out=vo[:, 64:128], in_=psbf[:, boff + 64:boff + 128], func=AF.Copy),
                                     reads=[('psbf', kb)], writes=[vot])
                                vtiles[kb] = ((ve, vet), (vo, vot))
                                if kb == 1:
                                    vcache[ck] = ((ve, vet), (vo, vot), VE.i)
                            terms = [(hp, kb) for hp in range(2) for kb in range(2) if kb in vtiles]
                            for which, bank in ((0, nb), (1, db)):
                                for n_, (hp, kb) in enumerate(terms):
                                    if which == 0:
                                        vt_, vtt_ = vtiles[kb][hp]
                                        lhsT, lt = vt_[:], vtt_
                                    else:
                                        lhsT, lt = C['onespad'][:, hp * 128:(hp + 1) * 128], CT('onespad')
                                    rhs = pts[hp][0][:, (uj * 2 + kb) * 128:(uj * 2 + kb + 1) * 128]
                                    p.op('pe', lambda e, bank=bank, u=u, lhsT=lhsT, rhs=rhs, n_=n_, nn=len(terms): e.matmul(
                                        cx.ps[bank][:, u * 128:(u + 1) * 128], lhsT=lhsT, rhs=rhs, start=(n_ == 0), stop=(n_ == nn - 1)),
                                        reads=[lt, pts[hp][1]], writes=[('ps', bank)])
                    for acc, bank, an in ((accN, nb, 'accN'), (accD, db, 'accD')):
                        if r == 1:
                            va = acc[:, qi * 512:(qi + 1) * 512]
                            pv = cx.ps[bank][:, :]
                        elif r == 4:
                            va = acc[:, qi * 512:(qi + 1) * 512].rearrange("p (i c) -> p c i", c=4)
                            pv = cx.ps[bank][:, :].rearrange("p (c i) -> p c i", c=4)
                        else:
                            va = acc[:, :].rearrange("p (i c) -> p c i", c=16)[:, qi * 4:(qi + 1) * 4, :]
                            pv = cx.ps[bank][:, :].rearrange("p (c i) -> p c i", c=4)
                        if r == 1:
                            p.op('dve', lambda e, va=va, pv=pv: e.tensor_copy(out=va, in_=pv),
                                 reads=[('ps', bank)], writes=[an])
                        else:
                            p.op('dve', lambda e, va=va, pv=pv: e.tensor_tensor(out=va, in0=va, in1=pv, op=ALU.add),
                                 reads=[('ps', bank), an], writes=[an])
            p.op('dve', lambda e: e.reciprocal(out=accD[:], in_=accD[:]), reads=['accD'], writes=['accD'])
            p.op('dve', lambda e: e.tensor_tensor(out=oT[:], in0=accN[:], in1=accD[:], op=ALU.mult),
                 reads=['accN', 'accD'], writes=['oT1'])
            if out_pair is None:
                p.dma('pool', d['oT'][pr * 128:(pr + 1) * 128, M * ST:(M + 1) * ST], oT[:], reads=['oT1'], writes=['oT_out'], key='m1out')
            else:
                out_pair(M, pr, oT, ['oT1'])


def m1_host_inputs(inp, g):
    w = inp['c_w_qkv'][0]
    cols = np.concatenate([np.arange(k * 2048 + g * 512, k * 2048 + (g + 1) * 512) for k in range(3)])
    return dict(w_qkv=np.ascontiguousarray(w[:, cols]),
                qk_gain=np.ascontiguousarray(np.stack([np.tile(inp['c_q_norm'][0], 2), np.tile(inp['c_k_norm'][0], 2)], 1)).astype(np.float32))


M1_CONSTS = ['ident_b', 'dilmask', 'blockones', 'onespad']


def build_mixer1_program(S):
    nc = bass.Bass("TRN2", target_bir_lowering=False)
    d = {}
    d['hT'] = nc.dram_tensor("hT", [D_MODEL, S], BF16, kind="ExternalInput").ap()
    d['w_qkv'] = nc.dram_tensor("w_qkv", [D_MODEL, 1536], F32, kind="ExternalInput").ap()
    d['qk_gain'] = nc.dram_tensor("qk_gain", [128, 2], F32, kind="ExternalInput").ap()
    d['oT'] = nc.dram_tensor("oT", [512, S], BF16, kind="ExternalOutput").ap()
    p = Prog(nc)
    cx = Ctx(p)
    load_consts(cx, nc, M1_CONSTS)
    emit_mixer1(cx, S, d)
    p.emit(final_wait_keys=['m1out'])
    return nc


I32 = mybir.dt.int32
GROUPS = [[0, 1, 2, 3], [4, 5, 6, 7]]
ALL_CONSTS = ['ident_f', 'ident_b', 'U', 'negU', 'ones_f', 'negmask4', 'swamask', 'dilmask', 'blockones', 'onespad']


def build_fused_program(S, debug=False):
    TSEG = S // 4
    nt_all = S // 512
    ntseg = TSEG // 512
    nst = S // 2048
    nc = bass.Bass("TRN2", target_bir_lowering=False)
    ext = lambda n, shp, dt=F32: nc.dram_tensor(n, list(shp), dt, kind="ExternalInput").ap()
    itn = lambda n, shp, dt=BF16: nc.dram_tensor(n, list(shp), dt, kind="Internal").ap()
    d = {}
    d['xT'] = ext("xT", [D_MODEL, S])
    xTs = ext("xTs", [D_MODEL, TSEG])
    d['w_in'] = ext("w_in", [D_MODEL, M0_NCOL])
    d['g_mix'] = ext("g_mix", [D_MODEL])
    for n, shp in M0_SMALL.items():
        d[n] = ext(n, shp)
    rank = ext("rank", [1, 1], I32)
    w_out0 = ext("w_out0", [3072, D_MODEL])
    w_up0 = ext("w_up0", [D_MODEL, D_FF])
    w_dn0 = ext("w_dn0", [D_FF, D_MODEL])
    g_ffn0 = ext("g_ffn0", [D_MODEL])
    g_mix1 = ext("g_mix1", [D_MODEL])
    d1 = {}
    d1['w_qkv'] = ext("w_qkv", [D_MODEL, 1536])
    d1['qk_gain'] = ext("qk_gain1", [128, 2])
    w_out1 = ext("w_out1", [2048, D_MODEL])
    w_up1 = ext("w_up1", [D_MODEL, D_FF])
    w_dn1 = ext("w_dn1", [D_FF, D_MODEL])
    g_ffn1 = ext("g_ffn1", [D_MODEL])
    outT = nc.dram_tensor("outT", [D_MODEL, TSEG], F32, kind="ExternalOutput").ap()
    sc = []
    for l, EC in ((0, 24), (1, 16)):
        sc.append((itn(f"s_out{l}", [4, 128, EC * 512]), itn(f"s_up{l}", [NFB, 128, 16 * 512]),
                   itn(f"s_dn{l}", [16, 128, FC * 128])))
    mix0_loc = itn("mix0_loc", [nt_all, 768, 512])
    G0 = itn("G0", [nt_all, 4 * 768, 512])
    x1T = itn("x1T", [D_MODEL, TSEG], F32)
    h1_loc = itn("h1_loc", [ntseg, 2, 1024, 512])
    G1 = itn("G1", [ntseg, 2, 4 * 1024, 512])
    o_loc = itn("o_loc", [nst, 4, 128, 2048])
    G2 = itn("G2", [nst, 4, 4 * 128, 2048])

    if debug:
        dbg_G0 = nc.dram_tensor("dbg_G0", [4 * 768, 512], BF16, kind="ExternalOutput").ap()
        dbg_x1 = nc.dram_tensor("dbg_x1", [D_MODEL, TSEG], F32, kind="ExternalOutput").ap()
        dbg_G1 = nc.dram_tensor("dbg_G1", [4 * 1024, 512], BF16, kind="ExternalOutput").ap()
        dbg_G2 = nc.dram_tensor("dbg_G2", [4 * 128, 2048], BF16, kind="ExternalOutput").ap()
        dbg_mix = nc.dram_tensor("dbg_mix", [768, 512], BF16, kind="ExternalOutput").ap()
    p = Prog(nc)
    cx = Ctx(p)
    load_consts(cx, nc, ALL_CONSTS)

    def rankval(e, scale):
        k = (id(e), 'rank')
        if k not in p.regs:
            r = e.alloc_register(f"rk{p.phase}")
            e.reg_load(r, rank[0:1, 0:1])
            p.regs[k] = e.snap(e.snap(r, min_val=0, max_val=3) * scale, min_val=0, max_val=3 * scale)
        return p.regs[k]

    p.begin_phase()
    casts = (cast_dense_weight_ops(p, w_out0, w_up0, w_dn0, *sc[0], 24, 'L0') +
             cast_dense_weight_ops(p, w_out1, w_up1, w_dn1, *sc[1], 16, 'L1'))
    per = -(-len(casts) // max(1, nt_all - 2))

    def per_tile(t):
        for _ in range(per):
            if casts:
                casts.pop(0)()

    def out_tile0(t, mst, reads):
        p.dma('act', mix0_loc[t].rearrange("(c p) t -> p c t", p=128), mst[:], reads=reads,
              writes=[('mix0_loc', t)], key='m0out')
        p.cc("AllGather", mix0_loc[t], G0[t], GROUPS, reads=[('mix0_loc', t)], writes=[('G0', t)], key='cc0')

    emit_mixer0(cx, S, d, out_tile=out_tile0, per_tile=per_tile)
    while casts:
        casts.pop(0)()
    p.end_phase()

    p.begin_phase()
    if debug:
        p.dma('sp', dbg_G0, G0[0], writes=['dbg0'], key='dbg')
        p.dma('sp', dbg_mix, mix0_loc[0], writes=['dbg0m'], key='dbg')

    stg0 = itn("stg0", [2, 4 * 768, 512])

    def mix_load0(t, mt, tok):
        sslot = t % 2

        def fn(e):
            src = G0[t:][bass.ds(rankval(e, ntseg), 1)]
            return e.dma_start(out=stg0[sslot].rearrange("(a b) t -> a b t", a=128),
                               in_=src.rearrange("o (a b) t -> a (o b) t", a=128))
        p.op('sp', fn, writes=[('stg0', sslot)], dma_key='L0stg')
        toks_ = []
        for r in range(4):
            p.dma('sp', mt[:, r * 2:(r + 1) * 2, :],
                  stg0[sslot, r * 768:r * 768 + 256, :].rearrange("(c p) t -> p c t", p=128),
                  reads=[('stg0', sslot)], writes=[(tok, 'a', r)], key='L0min')
            p.dma('sp', mt[:, 8 + r * 4:8 + (r + 1) * 4, :],
                  stg0[sslot, r * 768 + 256:(r + 1) * 768, :].rearrange("(c p) t -> p c t", p=128),
                  reads=[('stg0', sslot)], writes=[(tok, 'y', r)], key='L0min')
            toks_ += [(tok, 'a', r), (tok, 'y', r)]
        return toks_

    def hn_store0(t, h2, reads):
        for hf in range(2):
            p.dma('act', h1_loc[t, hf].rearrange("(c p) t -> p c t", p=128), h2[:, hf * 8:(hf + 1) * 8, :], reads=reads,
                  writes=[('h1_loc', t, hf)], key='L0hout')
            p.cc("AllGather", h1_loc[t, hf], G1[t, hf], GROUPS, reads=[('h1_loc', t, hf)], writes=[('G1', t, hf)], key='cc1')

    emit_dense(cx, TSEG, 24, xTs, None, *sc[0], g_ffn0, x1T, 'L0', gain_next=g_mix1, mix_load=mix_load0,
               hn_store=hn_store0, aq='act')
    p.end_phase()

    p.begin_phase()
    if debug:
        p.dma('sp', dbg_x1, x1T, writes=['dbg1'], key='dbg')
        p.dma('sp', dbg_G1, G1[0, 0], writes=['dbg2'], key='dbg')

    def h_src(t, c4):
        r, lt = divmod(t, ntseg)
        return G1[lt, c4 // 2, r * 1024 + (c4 % 2) * 512: r * 1024 + (c4 % 2) * 512 + 512, :]

    def out_pair1(M, pr, oT, reads):
        p.dma('act', o_loc[M, pr], oT[:], reads=reads, writes=[('o_loc', M, pr)], key='m1out')
        p.cc("AllGather", o_loc[M, pr], G2[M, pr], GROUPS, reads=[('o_loc', M, pr)], writes=[('G2', M, pr)], key='cc2')

    emit_mixer1(cx, S, d1, h_src=h_src, out_pair=out_pair1)
    p.end_phase()

    p.begin_phase()
    if debug:
        p.dma('sp', dbg_G2, G2[0, 0], writes=['dbg3'], key='dbg')
    stseg = TSEG // 2048

    stg2 = itn("stg2", [2, 4, 512, 2048])

    def mix_load1(t, mt, tok):
        j = t // 4
        sslot = j % 2
        if t % 4 == 0:
            def fn(e):
                src = G2[j:][bass.ds(rankval(e, stseg), 1)]
                return e.dma_start(out=stg2[sslot].rearrange("q (a b) t -> a q (b t)", a=128),
                                   in_=src.rearrange("o q (a b) t -> a (o q) (b t)", a=128))
            p.op('sp', fn, writes=[('stg2', sslot)], dma_key='L1stg')
        for r in range(4):
            p.dma('sp', mt[:, r * 4:(r + 1) * 4, :],
                  stg2[sslot, :, r * 128:(r + 1) * 128, (t % 4) * 512:(t % 4 + 1) * 512].rearrange("q p t -> p q t"),
                  reads=[('stg2', sslot)], writes=[(tok, 'q', r)], key='L1min')
        return [(tok, 'q', r) for r in range(4)]

    emit_dense(cx, TSEG, 16, x1T, None, *sc[1], g_ffn1, outT, 'L1', mix_load=mix_load1, aq='act')
    p.end_phase()
    p.close()
    return nc


BATCH = 2
SEQ = 16384
NCORES = 8
_PROGS = {}


def _prog(name, fn, *a):
    key = (name,) + tuple(a)
    if key not in _PROGS:
        _PROGS[key] = fn(*a)
    return _PROGS[key]


def fused_in_maps(inp, S):
    TSEG = S // 4
    x = inp['x']
    xT = [np.ascontiguousarray(x[b].T) for b in range(x.shape[0])]
    cst = const_inputs(ALL_CONSTS)
    maps = []
    for c in range(NCORES):
        b, g = divmod(c, 4)
        m = m0_host_inputs(inp, b, g, S)
        m['xT'] = xT[b]
        m['xTs'] = np.ascontiguousarray(xT[b][:, g * TSEG:(g + 1) * TSEG])
        m['rank'] = np.array([[g]], np.int32)
        m1 = m1_host_inputs(inp, g)
        m['w_qkv'] = m1['w_qkv']
        m['qk_gain1'] = m1['qk_gain']
        m.update(w_out0=inp['ab_w_out'][0], w_up0=inp['w_up'][0], w_dn0=inp['w_down'][0], g_ffn0=inp['norm_ffn'][0],
                 g_mix1=inp['norm_mix'][1], w_out1=inp['c_w_o'][0], w_up1=inp['w_up'][1], w_dn1=inp['w_down'][1],
                 g_ffn1=inp['norm_ffn'][1])
        m.update(cst)
        maps.append(m)
    return maps


def kernel(**inp):
    inp = {k: np.asarray(v) for k, v in inp.items()}
    S = inp['x'].shape[1]
    TSEG = S // 4
    nc = _prog('fused', build_fused_program, S)
    res = run_bass_kernel_spmd(nc, fused_in_maps(inp, S), core_ids=list(range(NCORES))).results
    out = np.empty((inp['x'].shape[0], S, D_MODEL), np.float32)
    for c in range(NCORES):
        b, g = divmod(c, 4)
        out[b, g * TSEG:(g + 1) * TSEG, :] = np.asarray(res[c]['outT']).T
    return out
```

```python
import bisect
import contextlib
import numpy as np
import ml_dtypes
import concourse.bass as bass
import concourse.mybir as mybir
from concourse.bass_utils import run_bass_kernel_spmd

F32 = mybir.dt.float32
BF16 = mybir.dt.bfloat16
AF = mybir.ActivationFunctionType
ALU = mybir.AluOpType
AX = mybir.AxisListType

D_MODEL = 2048
DC = D_MODEL // 128
D_FF = 8192
FC = D_FF // 128
NFB = D_FF // 512
EPS = 1e-5
SAME_ENGINE_SYNC = True
SCHEDULE = True


class Prog:
    ENGS = ("pe", "act", "dve", "pool")

    def __init__(self, nc):
        self.nc = nc
        self.ops = []
        self.top = contextlib.ExitStack()
        self.pstack = None
        self.last_w = {}
        self.readers = {}
        self.n_sb = 0
        self.cnt = {e: 0 for e in self.ENGS}
        self.dma_cum = {}
        self.dma_lists = {}
        self.sems = {e: self.top.enter_context(nc.semaphore("s_" + e)) for e in self.ENGS}
        self.bar = self.top.enter_context(nc.semaphore("s_bar"))
        self.dsems = {}
        self.phase = 0
        self.phase_start = 0
        self.regs = {}

    def _stk(self):
        return self.pstack if self.pstack is not None else self.top

    def sbuf(self, shape, dt, name=None):
        self.n_sb += 1
        return self._stk().enter_context(
            self.nc.sbuf_tensor(name or f"sb{self.n_sb}", list(shape), dt))

    def psum(self, shape, dt, name=None):
        self.n_sb += 1
        return self._stk().enter_context(
            self.nc.psum_tensor(name or f"ps{self.n_sb}", list(shape), dt))

    @staticmethod
    def _is_psum(t):
        return (isinstance(t, tuple) and t and t[0] in ('ps', 'psbf')) or (isinstance(t, str) and t.startswith('psbf'))

    def op(self, eng, fn, reads=(), writes=(), dma_key=None, inc=16):
        extra = [('psacc', t) for t in reads if self._is_psum(t)]
        if extra:
            writes = list(writes) + extra
        raw, war = set(), set()
        for b in reads:
            if b in self.last_w:
                raw.add(self.last_w[b])
        for b in writes:
            if b in self.last_w:
                war.add(self.last_w[b])
            war.update(self.readers.get(b, ()))
        idx = len(self.ops)
        war -= raw
        self.ops.append(dict(eng=eng, fn=fn, raw=raw, war=war, dma_key=dma_key,
                             needed=False, cnt=None, inc=inc))
        for b in reads:
            self.readers.setdefault(b, []).append(idx)
        for b in writes:
            self.last_w[b] = idx
            self.readers[b] = []
        return idx

    def dma(self, q, out, in_, reads=(), writes=(), key=None, **kw):
        assert key is not None
        return self.op(q, lambda e: e.dma_start(out=out, in_=in_, **kw),
                       reads=reads, writes=writes, dma_key=key)

    def cc(self, kind, in_ap, out_ap, groups, reads=(), writes=(), key=None):
        return self.op('pool', lambda e: e.collective_compute(kind, ALU.bypass, replica_groups=groups,
                                                              ins=[in_ap], outs=[out_ap]),
                       reads=reads, writes=writes, dma_key=key, inc=1)

    def _synced(self, x, y, kind):
        if x["eng"] != y["eng"]:
            return True
        if x["eng"] in ("pe", "sp"):
            return False
        return SAME_ENGINE_SYNC and kind == "raw"

    def begin_phase(self):
        self.pstack = contextlib.ExitStack()
        self.regs = {}

    COST = dict(pe=0.16, act=0.27, dve=0.45, pool=0.45, sp=0.15)

    def _schedule(self):
        import heapq
        ops = self.ops
        ps = self.phase_start
        n = len(ops) - ps
        if n == 0 or not SCHEDULE:
            return
        preds = [None] * n
        succs = [[] for _ in range(n)]
        npred = [0] * n
        for i in range(n):
            y = ops[ps + i]
            pr = set(x - ps for x in y["raw"] if x >= ps) | set(x - ps for x in y["war"] if x >= ps)
            preds[i] = pr
            npred[i] = len(pr)
            for x in pr:
                succs[x].append(i)
        engs = ("pe", "act", "dve", "pool", "sp")
        efree = {e: 0.0 for e in engs}
        byready = {e: [] for e in engs}
        byidx = {e: [] for e in engs}
        finish = [0.0] * n
        ready = [0.0] * n
        for i in range(n):
            if npred[i] == 0:
                heapq.heappush(byready[ops[ps + i]["eng"]], (0.0, i))
        order = []
        lo = 0
        done = [False] * n
        while len(order) < n:
            best = None
            for e in engs:
                hr, hi = byready[e], byidx[e]
                while hr and hr[0][0] <= efree[e]:
                    heapq.heappush(hi, heapq.heappop(hr)[1])
                if hi:
                    cand = (efree[e], hi[0], e, True)
                elif hr:
                    cand = (hr[0][0], hr[0][1], e, False)
                else:
                    continue
                if best is None or cand[:2] < best[:2]:
                    best = cand
            st, i, e, fromidx = best
            if fromidx:
                heapq.heappop(byidx[e])
            else:
                heapq.heappop(byready[e])
            y = ops[ps + i]
            cost = y.get("cost") or self.COST[e]
            if y["dma_key"] is not None:
                efree[e] = st + 0.12
                fin = st + (y.get("cost") or (80.0 if y["inc"] == 1 else 3.0))
            else:
                efree[e] = st + cost
                fin = st + cost
            finish[i] = fin
            done[i] = True
            order.append(i)
            for sidx in succs[i]:
                lat = 0.15 if (ops[ps + sidx]["eng"] == e and y["dma_key"] is None) else 1.2
                r_ = fin + lat
                if r_ > ready[sidx]:
                    ready[sidx] = r_
                npred[sidx] -= 1
                if npred[sidx] == 0:
                    heapq.heappush(byready[ops[ps + sidx]["eng"]], (ready[sidx], sidx))
        remap = {ps + old: ps + new for new, old in enumerate(order)}
        newops = [ops[ps + old] for old in order]
        for y in newops:
            y["raw"] = set(remap.get(x, x) for x in y["raw"])
            y["war"] = set(remap.get(x, x) for x in y["war"])
        ops[ps:] = newops
        self.last_w = {k: remap.get(v, v) for k, v in self.last_w.items()}
        self.readers = {k: [remap.get(v, v) for v in vs] for k, vs in self.readers.items()}

    def end_phase(self):
        nc = self.nc
        ops = self.ops
        ps = self.phase_start
        self._schedule()
        last = {}
        for i in range(ps, len(ops)):
            y = ops[i]
            if y["dma_key"] is None:
                last[y["eng"]] = i
            for kind in ("raw", "war"):
                for xi in y[kind]:
                    if xi < ps:
                        continue
                    x = ops[xi]
                    if x["dma_key"] is None and self._synced(x, y, kind):
                        x["needed"] = True
        for e, i in last.items():
            if e in self.ENGS:
                ops[i]["needed"] = True
        for i in range(ps, len(ops)):
            x = ops[i]
            if x["dma_key"] is not None:
                k = x["dma_key"]
                self.dma_cum[k] = self.dma_cum.get(k, 0) + x["inc"]
                self.dma_lists.setdefault(k, ([], []))
                self.dma_lists[k][0].append(i)
                self.dma_lists[k][1].append(self.dma_cum[k])
                if k not in self.dsems:
                    self.dsems[k] = self.top.enter_context(nc.semaphore("d_" + str(k)))
            elif x["needed"]:
                self.cnt[x["eng"]] += 1
                x["cnt"] = self.cnt[x["eng"]]
        dma_lists, dsems, sems = self.dma_lists, self.dsems, self.sems
        phase = self.phase

        def waits_for(yi):
            y = ops[yi]
            w = {}
            for kind in ("raw", "war"):
                for xi in y[kind]:
                    if xi < ps:
                        continue
                    x = ops[xi]
                    if x["dma_key"] is not None:
                        k = x["dma_key"]
                        idxs, cums = dma_lists[k]
                        pos = bisect.bisect_left(idxs, yi)
                        v = cums[pos - 1]
                        key = ("d", k)
                    else:
                        if not self._synced(x, y, kind):
                            continue
                        v = x["cnt"]
                        key = ("c", x["eng"])
                    if w.get(key, 0) < v:
                        w[key] = v
            return w

        def emit_engine(ename, e):
            seen = {}
            if phase > 0:
                e.wait_ge(self.bar, phase)
            for yi in range(ps, len(ops)):
                y = ops[yi]
                if y["eng"] != ename:
                    continue
                for key, v in waits_for(yi).items():
                    if seen.get(key, 0) >= v:
                        continue
                    seen[key] = v
                    sm = dsems[key[1]] if key[0] == "d" else sems[key[1]]
                    e.wait_ge(sm, v)
                ins = y["fn"](e)
                if y["dma_key"] is not None:
                    ins.then_inc(dsems[y["dma_key"]], y["inc"])
                elif y["needed"]:
                    ins.then_inc(sems[ename], 1)
            if ename == "sp":
                for k, v in self.dma_cum.items():
                    e.wait_ge(dsems[k], v)
                for en in self.ENGS:
                    if self.cnt[en] > 0:
                        e.wait_ge(sems[en], self.cnt[en])
                e.sem_inc(self.bar, 1)

        with nc.Block() as block:
            @block.sync
            def _(e):
                emit_engine("sp", e)

            @block.tensor
            def _(e):
                emit_engine("pe", e)

            @block.scalar
            def _(e):
                emit_engine("act", e)

            @block.vector
            def _(e):
                emit_engine("dve", e)

            @block.gpsimd
            def _(e):
                emit_engine("pool", e)
        if self.pstack is not None:
            self.pstack.close()
            self.pstack = None
        self.phase += 1
        self.phase_start = len(ops)

    def emit(self, final_wait_keys=()):
        self.end_phase()
        self.top.close()

    def close(self):
        self.top.close()


class Ctx:
    def __init__(self, p):
        self.p = p
        self.ps = [p.psum([128, 512], F32, name=f"psb{i}") for i in range(6)]
        self.psbf2 = [p.psum([128, 1024], BF16, name=f"psbf{i}") for i in range(2)]
        self.ps_rr = 0
        self.ones_bf = p.sbuf([128, 128], BF16, name="ones_bf")
        self.eps = p.sbuf([128, 1], F32, name="eps_t")
        p.op('pool', lambda e: e.memset(self.ones_bf[:], 1.0), writes=['ones_bf'])
        p.op('pool', lambda e: e.memset(self.eps[:], EPS), writes=['eps_t'])
        self.uid = 0

    def bank(self, lo=0, hi=6):
        i = lo + self.ps_rr % (hi - lo)
        self.ps_rr += 1
        return i

    def u(self):
        self.uid += 1
        return self.uid


def emit_rmsnorm_fm(cx, xt, xtok, nchunk, width, gain, out_bf, out_tok, sq_ring, rstd, dim, tmp_sqrt):
    p = cx.p
    b = cx.bank()
    ps = cx.ps[b]
    for c in range(nchunk):
        sq = sq_ring[c % len(sq_ring)]
        sqt = ('sq', id(sq_ring), c % len(sq_ring))
        p.op('act', lambda e, c=c, sq=sq: e.activation(out=sq[:, :width], in_=xt[:, c, :width], func=AF.Square),
             reads=[xtok(c)], writes=[sqt])
        p.op('pe', lambda e, c=c, sq=sq: e.matmul(ps[:, :width], lhsT=cx.ones_bf[:], rhs=sq[:, :width],
                                                   start=(c == 0), stop=(c == nchunk - 1)),
             reads=[sqt, 'ones_bf'], writes=[('ps', b)])
    p.op('act', lambda e: e.activation(out=tmp_sqrt[:, :width], in_=ps[:, :width], func=AF.Ln,
                                       bias=cx.eps[:, 0:1], scale=1.0 / dim),
         reads=[('ps', b), 'eps_t'], writes=[('tmp_sqrt', id(tmp_sqrt))])
    p.op('act', lambda e: e.activation(out=rstd[:, :width], in_=tmp_sqrt[:, :width], func=AF.Exp, scale=-0.5),
         reads=[('tmp_sqrt', id(tmp_sqrt))], writes=[('rstd', id(rstd))])
    for c in range(nchunk):
        p.op('dve', lambda e, c=c: e.scalar_tensor_tensor(out=out_bf[:, c, :width], in0=xt[:, c, :width],
                                                          scalar=gain[:, c:c + 1], in1=rstd[:, :width],
                                                          op0=ALU.mult, op1=ALU.mult),
             reads=[xtok(c), ('rstd', id(rstd)), 'gains'], writes=[out_tok(c)])


def emit_rmsnorm_stream(cx, src, nchunk, width, gain, out_bf, out_tok, xring, sq_ring, rstd, tmpl, dim, q, key):
    p = cx.p
    b = cx.bank()
    ps = cx.ps[b]
    for c in range(nchunk):
        xr, xrt = xring.next()
        p.dma(q, xr[:, :width], src(c), writes=[xrt], key=key)
        sq = sq_ring[c % len(sq_ring)]
        sqt = ('sq', id(sq_ring), c % len(sq_ring))
        p.op('act', lambda e, xr=xr, sq=sq: e.activation(out=sq[:, :width], in_=xr[:, :width], func=AF.Square),
             reads=[xrt], writes=[sqt])
        p.op('pe', lambda e, c=c, sq=sq: e.matmul(ps[:, :width], lhsT=cx.ones_bf[:], rhs=sq[:, :width],
                                                   start=(c == 0), stop=(c == nchunk - 1)),
             reads=[sqt, 'ones_bf'], writes=[('ps', b)])
    p.op('act', lambda e: e.activation(out=tmpl[:, :width], in_=ps[:, :width], func=AF.Ln,
                                       bias=cx.eps[:, 0:1], scale=1.0 / dim),
         reads=[('ps', b), 'eps_t'], writes=[('tmpl', id(tmpl))])
    p.op('act', lambda e: e.activation(out=rstd[:, :width], in_=tmpl[:, :width], func=AF.Exp, scale=-0.5),
         reads=[('tmpl', id(tmpl))], writes=[('rstd', id(rstd))])
    for c in range(nchunk):
        xr, xrt = xring.next()
        p.dma(q, xr[:, :width], src(c), writes=[xrt], key=key)
        p.op('dve', lambda e, c=c, xr=xr: e.scalar_tensor_tensor(out=out_bf[:, c, :width], in0=xr[:, :width],
                                                                 scalar=gain[:, c:c + 1], in1=rstd[:, :width],
                                                                 op0=ALU.mult, op1=ALU.mult),
             reads=[xrt, ('rstd', id(rstd)), 'gains'], writes=[out_tok(c)])


def cast_dense_weight_ops(p, w_out, w_up, w_dn, s_out, s_up, s_dn, EC, tag):
    ops = []
    for db in range(8):
        ops.append(lambda db=db: p.dma('pool', s_out[db].rearrange("p (c j) -> p c j", j=256),
                                       w_out[:, db * 256:(db + 1) * 256].rearrange("(c p) j -> p c j", p=128),
                                       writes=[(tag, 's_out', db)], key='cast'))
    for fb in range(NFB):
        ops.append(lambda fb=fb: p.dma('pool', s_up[fb].rearrange("p (c j) -> p c j", j=512),
                                       w_up[:, fb * 512:(fb + 1) * 512].rearrange("(c p) j -> p c j", p=128),
                                       writes=[(tag, 's_up', fb)], key='cast'))
    rows = min(2048, D_FF)
    for db in range(16):
        for q in range(D_FF // rows):
            ops.append(lambda db=db, q=q: p.dma(
                'pool', s_dn[db][:, q * rows:(q + 1) * rows].rearrange("p (c j) -> p c j", j=128),
                w_dn[q * rows:(q + 1) * rows, db * 128:(db + 1) * 128].rearrange("(c p) j -> p c j", p=128),
                writes=[(tag, 's_dn', db, q)], key='cast'))
    return ops


def emit_cast_dense_weights(p, w_out, w_up, w_dn, s_out, s_up, s_dn, EC, tag):
    for f in cast_dense_weight_ops(p, w_out, w_up, w_dn, s_out, s_up, s_dn, EC, tag):
        f()


def emit_dense(cx, T, EC, xT, mixT, s_out, s_up, s_dn, gain_ffn, outT, tag,
               gain_next=None, hnT=None, TT=512, mix_load=None, hn_store=None, aq='pool'):
    p = cx.p
    nt = T // TT
    xt = p.sbuf([128, DC, TT], F32, name=tag + "x")
    mt = p.sbuf([128, EC, TT], BF16, name=tag + "mix")
    h2 = p.sbuf([128, DC, TT], BF16, name=tag + "h2")
    at = p.sbuf([128, FC, TT], BF16, name=tag + "a")
    WR = 16 * 512
    NSLOT = 3
    wring = [p.sbuf([128, WR], BF16, name=tag + f"w{i}") for i in range(NSLOT)]
    stage = [p.sbuf([128, TT], F32, name=tag + f"st{i}") for i in range(2)]
    sq_ring = [p.sbuf([128, TT], BF16, name=tag + f"sq{i}") for i in range(2)]
    sqf = [p.sbuf([128, TT], F32, name=tag + f"sqf{i}") for i in range(2)]
    rstd = p.sbuf([128, TT], F32, name=tag + "rstd")
    tmp_sqrt = p.sbuf([128, TT], F32, name=tag + "tsq")
    g1 = p.sbuf([128, DC], F32, name=tag + "g1")
    p.dma('sp', g1[:], gain_ffn.rearrange("(c p) -> p c", p=128), writes=['gains'], key=tag + 'g',
          allow_slow_non_contiguous=True)
    if gain_next is not None:
        g2 = p.sbuf([128, DC], F32, name=tag + "g2")
        p.dma('sp', g2[:], gain_next.rearrange("(c p) -> p c", p=128), writes=['gains'], key=tag + 'g',
              allow_slow_non_contiguous=True)

    blocks = []
    for db in range(8):
        blocks.append(('out', db, s_out[db], EC * 256, [(tag, 's_out', db)]))
    for fb in range(NFB):
        blocks.append(('up', fb, s_up[fb], 16 * 512, [(tag, 's_up', fb)]))
    for db in range(16):
        blocks.append(('dn', db, s_dn[db], FC * 128, [(tag, 's_dn', db, q) for q in range(D_FF // min(2048, D_FF))]))
    nblk = len(blocks)
    wcount = [0]

    def load_block(gi):
        kind, bi, src, n, tok = blocks[gi % nblk]
        slot = gi % NSLOT
        p.dma('sp', wring[slot][:, :n], src, reads=tok, writes=[(tag, 'wr', slot)], key=tag + f'wr{slot}')

    xtok = lambda c: (tag, 'x', c)
    total_blocks = nt * nblk
    load_block(0)
    load_block(1)
    gi = 0
    for t in range(nt):
        ts = slice(t * TT, (t + 1) * TT)
        for c in range(DC):
            p.dma(aq, xt[:, c, :], xT[c * 128:(c + 1) * 128, ts], writes=[xtok(c)], key=tag + 'xin')
        if mix_load is None:
            p.dma(aq, mt[:], mixT[:, ts].rearrange("(c p) t -> p c t", p=128),
                  writes=[(tag, 'mix')], key=tag + 'min')
            mixtoks = [(tag, 'mix')]
        else:
            mixtoks = mix_load(t, mt, (tag, 'mix'))
        for db in range(8):
            if gi + 2 < total_blocks:
                load_block(gi + 2)
            slot = gi % NSLOT
            w = wring[slot]
            for dj in range(2):
                dc = db * 2 + dj
                b = cx.bank()
                for ec in range(EC):
                    p.op('pe', lambda e, b=b, w=w, ec=ec, dj=dj: e.matmul(
                        cx.ps[b][:, :TT], lhsT=w[:, ec * 256 + dj * 128: ec * 256 + (dj + 1) * 128],
                        rhs=mt[:, ec, :], start=(ec == 0), stop=(ec == EC - 1)),
                        reads=[(tag, 'wr', slot)] + mixtoks, writes=[('ps', b)])
                p.op('dve', lambda e, b=b, dc=dc: e.tensor_tensor(out=xt[:, dc, :], in0=xt[:, dc, :],
                                                                   in1=cx.ps[b][:, :TT], op=ALU.add),
                     reads=[('ps', b), xtok(dc)], writes=[xtok(dc)])
            gi += 1
        emit_rmsnorm_fm(cx, xt, xtok, DC, TT, g1, h2, lambda c: (tag, 'h2', c), sq_ring, rstd, D_MODEL, tmp_sqrt)
        for fb in range(NFB):
            if gi + 2 < total_blocks:
                load_block(gi + 2)
            slot = gi % NSLOT
            w = wring[slot]
            for fj in range(4):
                fc = fb * 4 + fj
                b = cx.bank()
                for dc in range(DC):
                    p.op('pe', lambda e, b=b, w=w, dc=dc, fj=fj: e.matmul(
                        cx.ps[b][:, :TT], lhsT=w[:, dc * 512 + fj * 128: dc * 512 + (fj + 1) * 128],
                        rhs=h2[:, dc, :], start=(dc == 0), stop=(dc == DC - 1)),
                        reads=[(tag, 'wr', slot), (tag, 'h2', dc)], writes=[('ps', b)])
                sf = sqf[fc % 2]
                p.op('act', lambda e, b=b, sf=sf: e.activation(out=sf[:], in_=cx.ps[b][:, :TT], func=AF.Square),
                     reads=[('ps', b)], writes=[(tag, 'sqf', fc % 2)])
                p.op('dve', lambda e, b=b, sf=sf, fc=fc: e.scalar_tensor_tensor(
                    out=at[:, fc, :], in0=cx.ps[b][:, :TT], scalar=0.0, in1=sf[:], op0=ALU.is_gt, op1=ALU.mult),
                    reads=[('ps', b), (tag, 'sqf', fc % 2)], writes=[(tag, 'a', fc)])
            gi += 1
        for db in range(16):
            if gi + 2 < total_blocks:
                load_block(gi + 2)
            slot = gi % NSLOT
            w = wring[slot]
            b = cx.bank()
            for fc in range(FC):
                p.op('pe', lambda e, b=b, w=w, fc=fc: e.matmul(
                    cx.ps[b][:, :TT], lhsT=w[:, fc * 128:(fc + 1) * 128], rhs=at[:, fc, :],
                    start=(fc == 0), stop=(fc == FC - 1)),
                    reads=[(tag, 'wr', slot), (tag, 'a', fc)], writes=[('ps', b)])
            if hnT is None and hn_store is None:
                st = stage[db % 2]
                p.op('dve', lambda e, b=b, db=db, st=st: e.tensor_tensor(out=st[:], in0=xt[:, db, :],
                                                                         in1=cx.ps[b][:, :TT], op=ALU.add),
                     reads=[('ps', b), xtok(db)], writes=[(tag, 'st', db % 2)])
                p.dma(aq, outT[db * 128:(db + 1) * 128, ts], st[:], reads=[(tag, 'st', db % 2)],
                      writes=[(tag, 'outT')], key=tag + 'out')
            else:
                p.op('dve', lambda e, b=b, db=db: e.tensor_tensor(out=xt[:, db, :], in0=xt[:, db, :],
                                                                   in1=cx.ps[b][:, :TT], op=ALU.add),
                     reads=[('ps', b), xtok(db)], writes=[xtok(db)])
                p.dma(aq, outT[db * 128:(db + 1) * 128, ts], xt[:, db, :], reads=[xtok(db)],
                      writes=[(tag, 'outT')], key=tag + 'out')
            gi += 1
        if hnT is not None or hn_store is not None:
            emit_rmsnorm_fm(cx, xt, xtok, DC, TT, g2, h2, lambda c: (tag, 'h2', c), sq_ring, rstd, D_MODEL, tmp_sqrt)
            if hn_store is None:
                p.dma(aq, hnT[:, ts].rearrange("(c p) t -> p c t", p=128), h2[:],
                      reads=[(tag, 'h2', c) for c in range(DC)], writes=[(tag, 'hnT')], key=tag + 'hout')
            else:
                hn_store(t, h2, [(tag, 'h2', c) for c in range(DC)])


def build_dense_program(T, EC, with_hn, do_cast=True):
    nc = bass.Bass("TRN2", target_bir_lowering=False)
    E = EC * 128
    xT = nc.dram_tensor("xT", [D_MODEL, T], F32, kind="ExternalInput").ap()
    mixT = nc.dram_tensor("mixT", [E, T], BF16, kind="ExternalInput").ap()
    w_out = nc.dram_tensor("w_out", [E, D_MODEL], F32, kind="ExternalInput").ap()
    w_up = nc.dram_tensor("w_up", [D_MODEL, D_FF], F32, kind="ExternalInput").ap()
    w_dn = nc.dram_tensor("w_dn", [D_FF, D_MODEL], F32, kind="ExternalInput").ap()
    g_ffn = nc.dram_tensor("g_ffn", [D_MODEL], F32, kind="ExternalInput").ap()
    outT = nc.dram_tensor("outT", [D_MODEL, T], F32, kind="ExternalOutput").ap()
    g_next = hnT = None
    if with_hn:
        g_next = nc.dram_tensor("g_next", [D_MODEL], F32, kind="ExternalInput").ap()
        hnT = nc.dram_tensor("hnT", [D_MODEL, T], BF16, kind="ExternalOutput").ap()
    s_out = nc.dram_tensor("s_out", [8, 128, EC * 256], BF16, kind="Internal").ap()
    s_up = nc.dram_tensor("s_up", [NFB, 128, 16 * 512], BF16, kind="Internal").ap()
    s_dn = nc.dram_tensor("s_dn", [16, 128, FC * 128], BF16, kind="Internal").ap()
    p = Prog(nc)
    cx = Ctx(p)
    emit_cast_dense_weights(p, w_out, w_up, w_dn, s_out, s_up, s_dn, EC, 'L')
    emit_dense(cx, T, EC, xT, mixT, s_out, s_up, s_dn, g_ffn, outT, 'L', gain_next=g_next, hnT=hnT)
    p.emit(final_wait_keys=['Lout'] + (['Lhout'] if with_hn else []))
    return nc


M0_NCOL = 1736
QO, KO, XO, BO, CO, VO, DTO, ZO = 0, 256, 384, 896, 1024, 1152, 1216, 1224
NEG = -30000.0


class Ring:
    def __init__(self, p, n, shape, dt, name):
        self.tiles = [p.sbuf(shape, dt, name=f"{name}{i}") for i in range(n)]
        self.name = name
        self.i = 0

    def next(self):
        k = self.i % len(self.tiles)
        self.i += 1
        return self.tiles[k], (self.name, k)


def host_consts():
    bf = ml_dtypes.bfloat16
    i = np.arange(128)
    U = (i[:, None] <= i[None, :]).astype(np.float32)
    c = {}
    c['ident_f'] = np.eye(128, dtype=np.float32)
    c['ident_b'] = np.eye(128, dtype=np.float32).astype(bf)
    c['U'] = U
    c['negU'] = -U
    c['ones_f'] = np.ones((128, 128), np.float32)
    nm = np.where(i[None, :] < i[:, None], NEG, 0.0).astype(np.float32)
    c['negmask4'] = np.tile(nm, (1, 4)).astype(bf)
    prev_swa = (i[:, None] > i[None, :]).astype(np.float32)
    prev_dil = (i[:, None] >= i[None, :]).astype(np.float32)
    own = (i[:, None] <= i[None, :]).astype(np.float32)
    c['swamask'] = np.tile(np.concatenate([prev_swa, own], 1), (1, 2)).astype(bf)
    c['dilmask'] = np.tile(np.concatenate([prev_dil, own], 1), (1, 2)).astype(bf)
    bo = np.zeros((128, 128), np.float32)
    bo[:64, :64] = 1
    bo[64:, 64:] = 1
    c['blockones'] = bo.astype(bf)
    op = np.zeros((128, 256), np.float32)
    op[:, 0:64] = 1
    op[:, 128 + 64:256] = 1
    c['onespad'] = op.astype(bf)
    return c


CONST_SPECS = dict(ident_f=([128, 128], F32), ident_b=([128, 128], BF16), U=([128, 128], F32),
                   negU=([128, 128], F32), ones_f=([128, 128], F32), negmask4=([128, 512], BF16),
                   swamask=([128, 512], BF16), dilmask=([128, 512], BF16), blockones=([128, 128], BF16),
                   onespad=([128, 256], BF16))


def load_consts(cx, nc, names):
    p = cx.p
    cx.c = {}
    for n in names:
        shp, dt = CONST_SPECS[n]
        src = nc.dram_tensor("c_" + n, shp, dt, kind="ExternalInput").ap()
        t = p.sbuf(shp, dt, name="k_" + n)
        p.dma('sp', t[:], src, writes=[('const', n)], key='consts')
        cx.c[n] = t


def emit_mixer0(cx, S, d, tag='m0', out_tile=None, per_tile=None):
    p = cx.p
    C = cx.c
    TT = 512
    nt = S // TT
    CT = lambda n: ('const', n)
    W = p.sbuf([128, DC, M0_NCOL], BF16, name='m0W')
    for c4 in range(4):
        p.dma('pool', W[:, c4 * 4:(c4 + 1) * 4, :],
              d['w_in'][c4 * 512:(c4 + 1) * 512, :].rearrange("(c p) n -> p c n", p=128),
              writes=[('m0W', c4)], key='m0w')
    Wtok = [('m0W', c4) for c4 in range(4)]
    gmix = p.sbuf([128, DC], F32, name='m0gmix')
    p.dma('sp', gmix[:], d['g_mix'].rearrange("(c p) -> p c", p=128), writes=['gains'], key='m0s',
          allow_slow_non_contiguous=True)
    small = {}
    for n, shp in (('qk_gain', [128, 2]), ('sink2', [128, 2]), ('conv_w', [128, 6, 4]), ('conv_b', [128, 6]),
                   ('dtb', [128, 8]), ('alog', [128, 8]), ('dskip', [128, 8]), ('gate', [128, 512])):
        t = p.sbuf(shp, F32, name='m0_' + n)
        p.dma('sp', t[:], d[n], writes=[('m0s', n)], key='m0s')
        small[n] = t
    esink = p.sbuf([128, 2], F32, name='m0esink')
    p.op('act', lambda e: e.activation(out=esink[:], in_=small['sink2'][:], func=AF.Exp),
         reads=[('m0s', 'sink2')], writes=['esink'])
    a_bc = p.sbuf([128, 8], F32, name='m0a')
    p.op('act', lambda e: e.activation(out=a_bc[:], in_=small['alog'][:], func=AF.Exp),
         reads=[('m0s', 'alog')], writes=['a_bc0'])
    na_bc = p.sbuf([128, 8], F32, name='m0na')
    p.op('dve', lambda e: e.tensor_scalar(out=na_bc[:], in0=a_bc[:], scalar1=-1.0, scalar2=None, op0=ALU.mult),
         reads=['a_bc0'], writes=['a_bc'])

    xring = Ring(p, 4, [128, TT], F32, 'm0xr')
    hT = p.sbuf([128, DC, TT], BF16, name='m0h')
    sq_ring = [p.sbuf([128, TT], BF16, name=f'm0sq{i}') for i in range(2)]
    qT = p.sbuf([128, 2, TT], BF16, name='m0qT')
    kT = p.sbuf([128, 2, TT], BF16, name='m0kT')
    qsb = Ring(p, 1, [128, TT], F32, 'm0qsb')
    qsq = Ring(p, 2, [128, TT], BF16, 'm0qsq')
    qln = Ring(p, 1, [128, TT], F32, 'm0qln')
    qrs = Ring(p, 1, [128, TT], F32, 'm0qrs')
    rstd = qrs.tiles[0]
    tmpl = qln.tiles[0]
    cinr = Ring(p, 2, [128, TT + 3], F32, 'm0cin')
    chist = p.sbuf([128, 6, 3], F32, name='m0chist')
    cacc = Ring(p, 2, [128, TT], F32, 'm0cacc')
    ctmp = Ring(p, 1, [128, TT], F32, 'm0ctmp')
    xsT = p.sbuf([128, 4, TT], F32, name='m0xsT')
    BT = p.sbuf([128, TT], BF16, name='m0BT')
    CTt = p.sbuf([128, TT], BF16, name='m0CT')
    Vpad = [p.sbuf([128, 8, 128], BF16, name=f'm0V{i}') for i in range(2)]
    praw = Ring(p, 2, [128, 512], BF16, 'm0praw')
    pT = [Ring(p, 2, [128, 512], BF16, f'm0pT{i}') for i in range(2)]
    tden = Ring(p, 1, [128, TT], F32, 'm0tden')
    tden2 = Ring(p, 1, [128, TT], F32, 'm0tden2')
    mixst = [p.sbuf([128, 6, TT], BF16, name='m0mix0')] * 2
    H = p.sbuf([128, 512], F32, name='m0H')
    Hbf = p.sbuf([128, 512], BF16, name='m0Hbf')
    t8 = Ring(p, 4, [128, 8], F32, 'm0t8')
    e8 = Ring(p, 2, [128, 8], F32, 'm0e8')
    dt8 = Ring(p, 2, [128, 8], F32, 'm0dt8')
    dA8 = Ring(p, 2, [128, 8], F32, 'm0dA8')
    dAU = Ring(p, 1, [128, 8, 128], F32, 'm0dAU')
    cs16 = Ring(p, 2, [128, 16], F32, 'm0cs16')
    d8 = Ring(p, 2, [128, 8], F32, 'm0d8')
    ed8 = Ring(p, 2, [128, 8], F32, 'm0ed8')
    w28 = Ring(p, 2, [128, 8], F32, 'm0w28')
    ecum8 = Ring(p, 2, [128, 8], F32, 'm0ecum8')
    dect8 = Ring(p, 2, [128, 8], F32, 'm0dect8')
    Esb = Ring(p, 2, [128, 512], F32, 'm0E')
    MT = Ring(p, 2, [128, 8, 128], BF16, 'm0MT')
    xtok = Ring(p, 1, [128, 512], F32, 'm0xtok')
    xdt = Ring(p, 2, [128, 8, 64], BF16, 'm0xdt')
    xdec = Ring(p, 2, [128, 512], BF16, 'm0xdec')
    xD = Ring(p, 1, [128, 512], F32, 'm0xD')
    Btok = Ring(p, 2, [128, 128], BF16, 'm0Btok')
    y1 = Ring(p, 2, [128, 512], F32, 'm0y1')
    sz = Ring(p, 4, [128, 512], F32, 'm0sz')
    junk = Ring(p, 1, [128, 512], BF16, 'm0junk')
    ss1 = Ring(p, 2, [128, 1], F32, 'm0ss1')
    ln1 = Ring(p, 2, [128, 1], F32, 'm0ln1')
    rs1 = Ring(p, 2, [128, 1], F32, 'm0rs1')
    ybf = Ring(p, 2, [128, 512], BF16, 'm0ybf')
    psbf, psbfy = cx.psbf2

    p.op('pool', lambda e: e.memset(chist[:], 0.0), writes=[('cinh', c6) for c6 in range(6)])
    p.op('pool', lambda e: e.memset(kT[:], 0.0), writes=[('kT', 0), ('kT', 1)])
    for i in range(2):
        p.op('pool', lambda e, i=i: e.memset(Vpad[i][:], 0.0), writes=[('Vpad', i, s) for s in range(8)])
    p.op('pool', lambda e: e.memset(H[:], 0.0), writes=['H'])
    p.op('pool', lambda e: e.memset(Hbf[:], 0.0), writes=['Hbf'])

    for t in range(nt):
        ts = slice(t * TT, (t + 1) * TT)
        sl = t % 2
        emit_rmsnorm_stream(cx, lambda c: d['xT'][c * 128:(c + 1) * 128, ts], DC, TT, gmix, hT, lambda c: ('m0h', c),
                            xring, sq_ring, rstd, tmpl, D_MODEL, 'sp', 'm0x')
        hreads = [('m0h', c) for c in range(DC)]
        mst = mixst[sl]
        mtok = lambda c: ('mixst', 0, c)

        def proj_fm(col0):
            b = cx.bank()
            for dc in range(DC):
                p.op('pe', lambda e, b=b, dc=dc: e.matmul(cx.ps[b][:, :TT], lhsT=W[:, dc, col0:col0 + 128],
                                                          rhs=hT[:, dc, :], start=(dc == 0), stop=(dc == DC - 1)),
                     reads=[('m0h', dc), Wtok[dc // 4]], writes=[('ps', b)])
            return b

        for qi in range(3):
            b = proj_fm(QO + qi * 128)
            qs, qst = qsb.next()
            sq, sqt = qsq.next()
            p.op('act', lambda e, b=b, qs=qs: e.activation(out=qs[:], in_=cx.ps[b][:, :TT], func=AF.Copy),
                 reads=[('ps', b)], writes=[qst])
            p.op('act', lambda e, b=b, sq=sq: e.activation(out=sq[:], in_=cx.ps[b][:, :TT], func=AF.Square),
                 reads=[('ps', b)], writes=[sqt])
            b2 = cx.bank()
            p.op('pe', lambda e, b2=b2, sq=sq: e.matmul(cx.ps[b2][:, :TT], lhsT=C['blockones'][:], rhs=sq[:],
                                                         start=True, stop=True),
                 reads=[sqt, CT('blockones')], writes=[('ps', b2)])
            ln, lnt = qln.next()
            rs, rst = qrs.next()
            p.op('act', lambda e, b2=b2, ln=ln: e.activation(out=ln[:], in_=cx.ps[b2][:, :TT], func=AF.Ln,
                                                              bias=cx.eps[:, 0:1], scale=1.0 / 64),
                 reads=[('ps', b2), 'eps_t'], writes=[lnt])
            p.op('act', lambda e, ln=ln, rs=rs: e.activation(out=rs[:], in_=ln[:], func=AF.Exp, scale=-0.5),
                 reads=[lnt], writes=[rst])
            if qi < 2:
                dst, dtok, gcol = qT[:, qi, :], ('qT', qi), 0
            else:
                dst, dtok, gcol = kT[:, sl, :], ('kT', sl), 1
            p.op('dve', lambda e, qs=qs, rs=rs, dst=dst, gcol=gcol: e.scalar_tensor_tensor(
                out=dst, in0=qs[:], scalar=small['qk_gain'][:, gcol:gcol + 1], in1=rs[:], op0=ALU.mult, op1=ALU.mult),
                reads=[qst, rst, ('m0s', 'qk_gain')], writes=[dtok])

        for c6 in range(6):
            b = proj_fm(XO + c6 * 128)
            cin, cint = cinr.next()
            p.op('act', lambda e, b=b, cin=cin: e.activation(out=cin[:, 3:3 + TT], in_=cx.ps[b][:, :TT], func=AF.Copy),
                 reads=[('ps', b)], writes=[(cint, 'd')])
            p.op('pool', lambda e, cin=cin, c6=c6: e.tensor_copy(out=cin[:, 0:3], in_=chist[:, c6, :]),
                 reads=[('cinh', c6)], writes=[(cint, 'h')])
            acc, acct = cacc.next()
            cw = small['conv_w']
            p.op('act', lambda e, acc=acc, cin=cin, c6=c6: e.activation(out=acc[:], in_=cin[:, 0:TT], func=AF.Copy, scale=cw[:, c6, 0:1]),
                 reads=[(cint, 'd'), (cint, 'h'), ('m0s', 'conv_w')], writes=[acct])
            for j in range(1, 4):
                p.op('dve', lambda e, acc=acc, cin=cin, c6=c6, j=j: e.scalar_tensor_tensor(
                    out=acc[:], in0=cin[:, j:j + TT], scalar=cw[:, c6, j:j + 1], in1=acc[:], op0=ALU.mult, op1=ALU.add),
                    reads=[(cint, 'd'), (cint, 'h'), acct], writes=[acct])
            p.op('pool', lambda e, cin=cin, c6=c6: e.tensor_copy(out=chist[:, c6, :], in_=cin[:, TT:TT + 3]),
                 reads=[(cint, 'd')], writes=[('cinh', c6)])
            if c6 < 4:
                dst, dtok = xsT[:, c6, :], ('xsT', c6)
            elif c6 == 4:
                dst, dtok = BT[:], 'BT'
            else:
                dst, dtok = CTt[:], 'CT'
            p.op('act', lambda e, acc=acc, dst=dst, c6=c6: e.activation(out=dst, in_=acc[:], func=AF.Silu,
                                                                         bias=small['conv_b'][:, c6:c6 + 1]),
                 reads=[acct, ('m0s', 'conv_b')], writes=[dtok])

        pv = []
        szl = []
        for ci in range(4):
            cs = slice(ci * 128, (ci + 1) * 128)
            bv = cx.bank()
            for dc in range(DC):
                p.op('pe', lambda e, bv=bv, dc=dc, cs=cs: e.matmul(cx.ps[bv][:, 0:72], lhsT=hT[:, dc, cs],
                                                                   rhs=W[:, dc, VO:VO + 72], start=(dc == 0), stop=(dc == DC - 1)),
                     reads=[('m0h', dc), Wtok[dc // 4]], writes=[('ps', bv)])
            G = t * 4 + ci
            vs = G % 8
            p.op('act', lambda e, bv=bv, vs=vs: e.activation(out=Vpad[0][:, vs, 0:64], in_=cx.ps[bv][:, 0:64], func=AF.Copy),
                 reads=[('ps', bv)], writes=[('Vpad', 0, vs)])
            p.op('act', lambda e, bv=bv, vs=vs: e.activation(out=Vpad[1][:, vs, 64:128], in_=cx.ps[bv][:, 0:64], func=AF.Copy),
                 reads=[('ps', bv)], writes=[('Vpad', 1, vs)])
            tt8, tt8t = t8.next()
            p.op('dve', lambda e, bv=bv, tt8=tt8: e.tensor_tensor(out=tt8[:], in0=cx.ps[bv][:, 64:72], in1=small['dtb'][:], op=ALU.add),
                 reads=[('ps', bv), ('m0s', 'dtb')], writes=[tt8t])
            pv.append((tt8, tt8t))
            bz = cx.bank()
            for dc in range(DC):
                p.op('pe', lambda e, bz=bz, dc=dc, cs=cs: e.matmul(cx.ps[bz][:, :], lhsT=hT[:, dc, cs], rhs=W[:, dc, ZO:ZO + 512],
                                                                   start=(dc == 0), stop=(dc == DC - 1)),
                     reads=[('m0h', dc), Wtok[dc // 4]], writes=[('ps', bz)])
            sz_, szt = sz.next()
            p.op('act', lambda e, bz=bz, sz_=sz_: e.activation(out=sz_[:], in_=cx.ps[bz][:, :], func=AF.Silu),
                 reads=[('ps', bz)], writes=[szt])
            szl.append((sz_, szt))

        for pr in range(2):
            nb = cx.bank()
            db = cx.bank()
            for cp in range(2):
                pts = []
                for hp in range(2):
                    rows = slice(64 * hp, 64 * hp + 64)
                    b = cx.bank()
                    for cj in range(2):
                        ci = cp * 2 + cj
                        qs_ = qT[rows, pr, ci * 128:(ci + 1) * 128]
                        if ci == 0:
                            kprev, kpt = kT[rows, 1 - sl, 384:512], ('kT', 1 - sl)
                        else:
                            kprev, kpt = kT[rows, sl, (ci - 1) * 128:ci * 128], ('kT', sl)
                        kown = kT[rows, sl, ci * 128:(ci + 1) * 128]
                        p.op('pe', lambda e, b=b, cj=cj, kprev=kprev, qs_=qs_: e.matmul(
                            cx.ps[b][:, (cj * 2) * 128:(cj * 2 + 1) * 128], lhsT=kprev, rhs=qs_, start=True, stop=True),
                            reads=[kpt, ('qT', pr)], writes=[('ps', b)])
                        p.op('pe', lambda e, b=b, cj=cj, kown=kown, qs_=qs_: e.matmul(
                            cx.ps[b][:, (cj * 2 + 1) * 128:(cj * 2 + 2) * 128], lhsT=kown, rhs=qs_, start=True, stop=True),
                            reads=[('kT', sl), ('qT', pr)], writes=[('ps', b)])
                    pr_, prt = praw.next()
                    p.op('act', lambda e, b=b, pr_=pr_: e.activation(out=pr_[:], in_=cx.ps[b][:, :], func=AF.Exp, scale=0.125),
                         reads=[('ps', b)], writes=[prt])
                    pt_, ptt = pT[hp].next()
                    p.op('pool', lambda e, pr_=pr_, pt_=pt_: e.tensor_tensor(out=pt_[:], in0=pr_[:], in1=C['swamask'][:], op=ALU.mult),
                         reads=[prt, CT('swamask')], writes=[ptt])
                    pts.append((pt_, ptt))
                for cj in range(2):
                    ci = cp * 2 + cj
                    G = t * 4 + ci
                    terms = []
                    for hp in range(2):
                        for kb in range(2):
                            if G == 0 and kb == 0:
                                continue
                            terms.append((hp, kb))
                    for which, bank, in ((0, nb), (1, db)):
                        for n_, (hp, kb) in enumerate(terms):
                            vslot = (G - 1 + kb) % 8
                            if which == 0:
                                lhsT, lt = Vpad[hp][:, vslot, :], ('Vpad', hp, vslot)
                            else:
                                lhsT, lt = C['onespad'][:, hp * 128:(hp + 1) * 128], CT('onespad')
                            rhs = pts[hp][0][:, (cj * 2 + kb) * 128:(cj * 2 + kb + 1) * 128]
                            p.op('pe', lambda e, bank=bank, ci=ci, lhsT=lhsT, rhs=rhs, n_=n_, nn=len(terms): e.matmul(
                                cx.ps[bank][:, ci * 128:(ci + 1) * 128], lhsT=lhsT, rhs=rhs, start=(n_ == 0), stop=(n_ == nn - 1)),
                                reads=[lt, pts[hp][1]], writes=[('ps', bank)])
            td, tdt = tden.next()
            td2, td2t = tden2.next()
            p.op('dve', lambda e, db=db, td=td, pr=pr: e.tensor_scalar(out=td[:], in0=cx.ps[db][:, :TT], scalar1=esink[:, pr:pr + 1],
                                                                        scalar2=None, op0=ALU.add),
                 reads=[('ps', db), 'esink'], writes=[tdt])
            p.op('dve', lambda e, td=td, td2=td2: e.reciprocal(out=td2[:], in_=td[:]), reads=[tdt], writes=[td2t])
            p.op('dve', lambda e, nb=nb, td2=td2, pr=pr: e.tensor_tensor(out=mst[:, pr, :], in0=cx.ps[nb][:, :TT], in1=td2[:], op=ALU.mult),
                 reads=[('ps', nb), td2t], writes=[mtok(pr)])

        for ci in range(4):
            cs = slice(ci * 128, (ci + 1) * 128)
            tt8, tt8t = pv[ci]
            ee, eet = e8.next()
            dt_, dtt = dt8.next()
            dA, dAt = dA8.next()
            p.op('act', lambda e, tt8=tt8, ee=ee: e.activation(out=ee[:], in_=tt8[:], func=AF.Exp), reads=[tt8t], writes=[eet])
            p.op('act', lambda e, ee=ee, dt_=dt_: e.activation(out=dt_[:], in_=ee[:], func=AF.Ln, bias=1.0), reads=[eet], writes=[dtt])
            p.op('dve', lambda e, dt_=dt_, dA=dA: e.tensor_tensor(out=dA[:], in0=dt_[:], in1=na_bc[:], op=ALU.mult),
                 reads=[dtt, 'a_bc'], writes=[dAt])
            dau, daut = dAU.next()
            p.op('pool', lambda e, dau=dau, dA=dA: e.tensor_tensor(out=dau[:], in0=C['U'][:].unsqueeze(1).to_broadcast([128, 8, 128]),
                                                                    in1=dA[:].unsqueeze(2).to_broadcast([128, 8, 128]), op=ALU.mult),
                 reads=[dAt, CT('U')], writes=[daut])
            bs = cx.bank()
            p.op('pe', lambda e, bs=bs, dA=dA: e.matmul(cx.ps[bs][:, 0:8], lhsT=C['U'][:], rhs=dA[:], start=True, stop=True),
                 reads=[dAt, CT('U')], writes=[('ps', bs)])
            p.op('pe', lambda e, bs=bs, dA=dA: e.matmul(cx.ps[bs][:, 8:16], lhsT=C['ones_f'][:], rhs=dA[:], start=True, stop=True),
                 reads=[dAt, CT('ones_f')], writes=[('ps', bs)])
            cs_, cst = cs16.next()
            p.op('act', lambda e, bs=bs, cs_=cs_: e.activation(out=cs_[:], in_=cx.ps[bs][:, 0:16], func=AF.Copy),
                 reads=[('ps', bs)], writes=[cst])
            dd, ddt = d8.next()
            p.op('dve', lambda e, cs_=cs_, dd=dd: e.tensor_tensor(out=dd[:], in0=cs_[:, 8:16], in1=cs_[:, 0:8], op=ALU.subtract),
                 reads=[cst], writes=[ddt])
            ed, edt = ed8.next()
            p.op('act', lambda e, dd=dd, ed=ed: e.activation(out=ed[:], in_=dd[:], func=AF.Exp), reads=[ddt], writes=[edt])
            w2, w2t = w28.next()
            p.op('dve', lambda e, ed=ed, dt_=dt_, w2=w2: e.tensor_tensor(out=w2[:], in0=ed[:], in1=dt_[:], op=ALU.mult),
                 reads=[edt, dtt], writes=[w2t])
            ec, ect = ecum8.next()
            p.op('act', lambda e, cs_=cs_, ec=ec: e.activation(out=ec[:], in_=cs_[:, 0:8], func=AF.Exp), reads=[cst], writes=[ect])
            dct, dctt = dect8.next()
            p.op('act', lambda e, cs_=cs_, dct=dct: e.activation(out=dct[:], in_=cs_[:, 8:16], func=AF.Exp), reads=[cst], writes=[dctt])
            bcb = cx.bank()
            p.op('pe', lambda e, bcb=bcb, cs=cs: e.matmul(cx.ps[bcb][:, 0:128], lhsT=BT[:, cs], rhs=CTt[:, cs], start=True, stop=True),
                 reads=['BT', 'CT'], writes=[('ps', bcb)])
            mt_, mtt = MT.next()
            for half in range(2):
                bsg = cx.bank()
                hs = slice(half * 4, half * 4 + 4)
                p.op('pe', lambda e, bsg=bsg, dau=dau, hs=hs: e.matmul(cx.ps[bsg][:, :], lhsT=C['ones_f'][:],
                                                                       rhs=dau[:, hs, :], start=True, stop=False),
                     reads=[daut, CT('ones_f')], writes=[('ps', bsg)])
                p.op('pe', lambda e, bsg=bsg, dA=dA, hs=hs: e.matmul(cx.ps[bsg][:, :], lhsT=C['negU'][:],
                                                                      rhs=dA[:, hs].unsqueeze(2).to_broadcast([128, 4, 128]),
                                                                      start=False, stop=False),
                     reads=[dAt, CT('negU')], writes=[('ps', bsg)])
                p.op('pe', lambda e, bsg=bsg: e.matmul(cx.ps[bsg][:, :], lhsT=C['ident_b'][:], rhs=C['negmask4'][:],
                                                       start=False, stop=True),
                     reads=[CT('ident_b'), CT('negmask4')], writes=[('ps', bsg)])
                E_, Et = Esb.next()
                p.op('act', lambda e, bsg=bsg, E_=E_: e.activation(out=E_[:], in_=cx.ps[bsg][:, :], func=AF.Exp),
                     reads=[('ps', bsg)], writes=[Et])
                p.op('dve', lambda e, E_=E_, mt_=mt_, hs=hs, bcb=bcb: e.tensor_tensor(
                    out=mt_[:, hs, :], in0=E_[:].rearrange("p (h t) -> p h t", h=4),
                    in1=cx.ps[bcb][:, 0:128].unsqueeze(1).to_broadcast([128, 4, 128]), op=ALU.mult),
                    reads=[Et, ('ps', bcb)], writes=[(mtt, half)])
            bx = cx.bank()
            for c in range(4):
                p.op('pe', lambda e, bx=bx, c=c, cs=cs: e.transpose(out=cx.ps[bx][:, c * 128:(c + 1) * 128], in_=xsT[:, c, cs],
                                                                    identity=C['ident_f'][:]),
                     reads=[('xsT', c), CT('ident_f')], writes=[('ps', bx)])
            xk, xkt = xtok.next()
            p.op('act', lambda e, bx=bx, xk=xk: e.activation(out=xk[:], in_=cx.ps[bx][:, :], func=AF.Copy),
                 reads=[('ps', bx)], writes=[xkt])
            xd, xdt_t = xdt.next()
            xk3 = xk[:].rearrange("p (h j) -> p h j", h=8)
            p.op('dve', lambda e, xk3=xk3, xd=xd, dt_=dt_: e.tensor_tensor(out=xd[:], in0=xk3,
                                                                           in1=dt_[:].unsqueeze(2).to_broadcast([128, 8, 64]), op=ALU.mult),
                 reads=[xkt, dtt], writes=[xdt_t])
            xc, xct = xdec.next()
            p.op('dve', lambda e, xk3=xk3, xc=xc, w2=w2: e.tensor_tensor(out=xc[:].rearrange("p (h j) -> p h j", h=8), in0=xk3,
                                                                         in1=w2[:].unsqueeze(2).to_broadcast([128, 8, 64]), op=ALU.mult),
                 reads=[xkt, w2t], writes=[xct])
            xD_, xDt = xD.next()
            p.op('pool', lambda e, xk3=xk3, xD_=xD_: e.tensor_tensor(out=xD_[:].rearrange("p (h j) -> p h j", h=8), in0=xk3,
                                                                     in1=small['dskip'][:].unsqueeze(2).to_broadcast([128, 8, 64]), op=ALU.mult),
                 reads=[xkt, ('m0s', 'dskip')], writes=[xDt])
            p.op('pe', lambda e, cs=cs: e.transpose(out=psbf[:, 0:128], in_=BT[:, cs], identity=C['ident_b'][:]),
                 reads=['BT', CT('ident_b')], writes=['psbf'])
            bt_, btt = Btok.next()
            p.op('act', lambda e, bt_=bt_: e.activation(out=bt_[:], in_=psbf[:, 0:128], func=AF.Copy), reads=['psbf'], writes=[btt])
            bst = cx.bank()
            p.op('pe', lambda e, bst=bst, bt_=bt_, xc=xc: e.matmul(cx.ps[bst][:, :], lhsT=bt_[:], rhs=xc[:], start=True, stop=True),
                 reads=[btt, xct], writes=[('ps', bst)])
            byo = cx.bank()
            p.op('pe', lambda e, byo=byo, cs=cs: e.matmul(cx.ps[byo][:, :], lhsT=CTt[:, cs], rhs=Hbf[:], start=True, stop=True),
                 reads=['CT', 'Hbf'], writes=[('ps', byo)])
            byd = cx.bank()
            for h in range(8):
                p.op('pe', lambda e, byd=byd, h=h, mt_=mt_, xd=xd: e.matmul(cx.ps[byd][:, h * 64:(h + 1) * 64], lhsT=mt_[:, h, :],
                                                                            rhs=xd[:, h, :], start=True, stop=True),
                     reads=[(mtt, h // 4), xdt_t], writes=[('ps', byd)])
            ya, yat = y1.next()
            p.op('dve', lambda e, byo=byo, ya=ya, ec=ec: e.tensor_tensor(out=ya[:].rearrange("p (h j) -> p h j", h=8),
                                                                         in0=cx.ps[byo][:, :].rearrange("p (h j) -> p h j", h=8),
                                                                         in1=ec[:].unsqueeze(2).to_broadcast([128, 8, 64]), op=ALU.mult),
                 reads=[('ps', byo), ect], writes=[yat])
            p.op('dve', lambda e, byd=byd, ya=ya: e.tensor_tensor(out=ya[:], in0=ya[:], in1=cx.ps[byd][:, :], op=ALU.add),
                 reads=[('ps', byd), yat], writes=[yat])
            p.op('pool', lambda e, ya=ya, xD_=xD_: e.tensor_tensor(out=ya[:], in0=ya[:], in1=xD_[:], op=ALU.add),
                 reads=[yat, xDt], writes=[yat])
            yc, yct = ya, yat
            p.op('dve', lambda e, dct=dct: e.tensor_tensor(out=H[:].rearrange("p (h j) -> p h j", h=8),
                                                           in0=H[:].rearrange("p (h j) -> p h j", h=8),
                                                           in1=dct[:].unsqueeze(2).to_broadcast([128, 8, 64]), op=ALU.mult),
                 reads=['H', dctt], writes=['H'])
            p.op('dve', lambda e, bst=bst: e.tensor_tensor(out=H[:], in0=H[:], in1=cx.ps[bst][:, :], op=ALU.add),
                 reads=['H', ('ps', bst)], writes=['H'])
            p.op('pool', lambda e: e.tensor_copy(out=Hbf[:], in_=H[:]), reads=['H'], writes=['Hbf'])
            sz_, szt = szl[ci]
            gy_, gyt = yc, yct
            p.op('pool', lambda e, gy_=gy_, sz_=sz_: e.tensor_tensor(out=gy_[:], in0=gy_[:], in1=sz_[:], op=ALU.mult),
                 reads=[yct, szt], writes=[gyt])
            ss_, sst = ss1.next()
            jk, jkt = junk.next()
            p.op('pool', lambda e, ss_=ss_: e.memset(ss_[:], 0.0), writes=[sst])
            p.op('act', lambda e, gy_=gy_, jk=jk, ss_=ss_: e.activation(out=jk[:], in_=gy_[:], func=AF.Square, accum_out=ss_[:, 0:1]),
                 reads=[gyt, sst], writes=[jkt, sst])
            l1, l1t = ln1.next()
            r1, r1t = rs1.next()
            p.op('act', lambda e, ss_=ss_, l1=l1: e.activation(out=l1[:], in_=ss_[:], func=AF.Ln, bias=cx.eps[:, 0:1], scale=1.0 / 512),
                 reads=[sst, 'eps_t'], writes=[l1t])
            p.op('act', lambda e, l1=l1, r1=r1: e.activation(out=r1[:], in_=l1[:], func=AF.Exp, scale=-0.5), reads=[l1t], writes=[r1t])
            yf, yft = ybf.next()
            p.op('dve', lambda e, gy_=gy_, r1=r1, yf=yf: e.scalar_tensor_tensor(out=yf[:], in0=gy_[:], scalar=r1[:, 0:1], in1=small['gate'][:],
                                                                                op0=ALU.mult, op1=ALU.mult),
                 reads=[gyt, r1t, ('m0s', 'gate')], writes=[yft])
            for c in range(4):
                p.op('pe', lambda e, c=c, yf=yf: e.transpose(out=psbfy[:, c * 128:(c + 1) * 128], in_=yf[:, c * 128:(c + 1) * 128],
                                                              identity=C['ident_b'][:]),
                     reads=[yft, CT('ident_b')], writes=['psbf_y'])
            p.op('act', lambda e, cs=cs: e.activation(out=mst[:, 2:6, cs], in_=psbfy[:, 0:512].rearrange("p (c t) -> p c t", c=4), func=AF.Copy),
                 reads=['psbf_y'], writes=[mtok(2 + ci)])
        if out_tile is None:
            p.dma('pool', d['mixT'][:, ts].rearrange("(c p) t -> p c t", p=128), mst[:],
                  reads=[mtok(c) for c in range(6)], writes=['mixT_out'], key='m0out')
        else:
            out_tile(t, mst, [mtok(c) for c in range(6)])
        if per_tile is not None:
            per_tile(t)


AB_Q, AB_K, AB_V, AB_Z, AB_X, AB_DT = 0, 1024, 1152, 1280, 3328, 6400


def m0_host_inputs(inp, b, g, S):
    w = inp['ab_w_in'][0]
    kv = g // 2
    cols = np.concatenate([
        np.arange(AB_Q + g * 256, AB_Q + (g + 1) * 256),
        np.arange(AB_K + kv * 64, AB_K + (kv + 1) * 64), np.arange(AB_K + kv * 64, AB_K + (kv + 1) * 64),
        np.arange(AB_X + g * 512, AB_X + (g + 1) * 512),
        np.arange(AB_X + 2048 + g * 128, AB_X + 2048 + (g + 1) * 128),
        np.arange(AB_X + 2560 + g * 128, AB_X + 2560 + (g + 1) * 128),
        np.arange(AB_V + kv * 64, AB_V + (kv + 1) * 64),
        np.arange(AB_DT + g * 8, AB_DT + (g + 1) * 8),
        np.arange(AB_Z + g * 512, AB_Z + (g + 1) * 512)])
    assert len(cols) == M0_NCOL
    chan = np.concatenate([np.arange(g * 512, (g + 1) * 512), 2048 + np.arange(g * 128, (g + 1) * 128),
                           2560 + np.arange(g * 128, (g + 1) * 128)])
    cw = inp['ab_conv_w'][0][:, chan]
    cb = inp['ab_conv_b'][0][chan]
    sk = inp['ab_sinks'][0][4 * g:4 * g + 4]
    rep = lambda v: np.ascontiguousarray(np.broadcast_to(np.asarray(v, np.float32)[None, :], (128, len(v))))
    d = dict(
        w_in=np.ascontiguousarray(w[:, cols]),
        g_mix=np.ascontiguousarray(inp['norm_mix'][0]),
        qk_gain=np.ascontiguousarray(np.stack([np.tile(inp['ab_q_norm'][0], 2), np.tile(inp['ab_k_norm'][0], 2)], 1)),
        sink2=np.ascontiguousarray(np.stack([np.repeat(sk[0:2], 64), np.repeat(sk[2:4], 64)], 1)),
        conv_w=np.ascontiguousarray(cw.reshape(4, 6, 128).transpose(2, 1, 0)),
        conv_b=np.ascontiguousarray(cb.reshape(6, 128).T),
        dtb=rep(inp['ab_dt_bias'][0][8 * g:8 * g + 8]),
        alog=rep(inp['ab_a_log'][0][8 * g:8 * g + 8]),
        dskip=rep(inp['ab_d_skip'][0][8 * g:8 * g + 8]),
        gate=rep(inp['ab_gate_norm'][0][512 * g:512 * (g + 1)]),
    )
    return {k: np.asarray(v, np.float32) for k, v in d.items()}


M0_SMALL = dict(qk_gain=[128, 2], sink2=[128, 2], conv_w=[128, 6, 4], conv_b=[128, 6], dtb=[128, 8],
                alog=[128, 8], dskip=[128, 8], gate=[128, 512])
M0_CONSTS = ['ident_f', 'ident_b', 'U', 'negU', 'ones_f', 'negmask4', 'swamask', 'blockones', 'onespad']


def const_inputs(names):
    hc = host_consts()
    return {"c_" + n: hc[n] for n in names}


def build_mixer0_program(S):
    nc = bass.Bass("TRN2", target_bir_lowering=False)
    d = {}
    d['xT'] = nc.dram_tensor("xT", [D_MODEL, S], F32, kind="ExternalInput").ap()
    d['w_in'] = nc.dram_tensor("w_in", [D_MODEL, M0_NCOL], F32, kind="ExternalInput").ap()
    d['g_mix'] = nc.dram_tensor("g_mix", [D_MODEL], F32, kind="ExternalInput").ap()
    for n, shp in M0_SMALL.items():
        d[n] = nc.dram_tensor(n, shp, F32, kind="ExternalInput").ap()
    d['mixT'] = nc.dram_tensor("mixT", [768, S], BF16, kind="ExternalOutput").ap()
    p = Prog(nc)
    cx = Ctx(p)
    load_consts(cx, nc, M0_CONSTS)
    emit_mixer0(cx, S, d)
    p.emit(final_wait_keys=['m0out'])
    return nc


def emit_mixer1(cx, S, d, tag='m1', h_src=None, out_pair=None):
    p = cx.p
    C = cx.c
    TT = 512
    ST = 2048
    nst = S // ST
    CT = lambda n: ('const', n)
    W = p.sbuf([128, DC, 1536], BF16, name='m1W')
    for c4 in range(4):
        p.dma('pool', W[:, c4 * 4:(c4 + 1) * 4, :],
              d['w_qkv'][c4 * 512:(c4 + 1) * 512, :].rearrange("(c p) n -> p c n", p=128),
              writes=[('m1W', c4)], key='m1w')
    Wtok = [('m1W', c4) for c4 in range(4)]
    qkg = p.sbuf([128, 2], F32, name='m1qkg')
    p.dma('sp', qkg[:], d['qk_gain'], writes=[('m1s', 'qk_gain')], key='m1s')
    hT = p.sbuf([128, DC, TT], BF16, name='m1h')
    qT = p.sbuf([128, 4, ST], BF16, name='m1qT')
    kT = p.sbuf([128, 4, 2, ST], BF16, name='m1kT')
    vT = p.sbuf([128, 4, 2, ST], BF16, name='m1vT')
    accN = p.sbuf([128, ST], F32, name='m1accN')
    accD = p.sbuf([128, ST], F32, name='m1accD')
    oT = p.sbuf([128, ST], BF16, name='m1oT')
    qsb = Ring(p, 1, [128, TT], F32, 'm1qsb')
    qsq = Ring(p, 2, [128, TT], BF16, 'm1qsq')
    qln = Ring(p, 1, [128, TT], F32, 'm1qln')
    qrs = Ring(p, 1, [128, TT], F32, 'm1qrs')
    praw = Ring(p, 3, [128, 512], BF16, 'm1praw')
    pT = [Ring(p, 3, [128, 512], BF16, f'm1pT{i}') for i in range(2)]
    VE = Ring(p, 16, [128, 128], BF16, 'm1VE')
    VO = Ring(p, 16, [128, 128], BF16, 'm1VO')
    vcache = {}
    psbf2 = cx.psbf2
    for r_ in (VE, VO):
        for i, t_ in enumerate(r_.tiles):
            p.op('pool', lambda e, t_=t_: e.memset(t_[:], 0.0), writes=[(r_.name, i)])

    for M in range(nst):
        sl = M % 2
        for tt in range(4):
            t = M * 4 + tt
            ts = slice(t * TT, (t + 1) * TT)
            lc = slice(tt * TT, (tt + 1) * TT)
            for c4 in range(4):
                p.dma('sp', hT[:, c4 * 4:(c4 + 1) * 4, :],
                      (d['hT'][c4 * 512:(c4 + 1) * 512, ts] if h_src is None else h_src(t, c4)).rearrange("(c p) t -> p c t", p=128),
                      writes=[('m1h', c4 * 4 + j) for j in range(4)], key=f'm1h{c4}')
            for fc in range(12):
                b = cx.bank()
                for dc in range(DC):
                    p.op('pe', lambda e, b=b, dc=dc, fc=fc: e.matmul(cx.ps[b][:, :TT], lhsT=W[:, dc, fc * 128:(fc + 1) * 128],
                                                                     rhs=hT[:, dc, :], start=(dc == 0), stop=(dc == DC - 1)),
                         reads=[('m1h', dc), Wtok[dc // 4]], writes=[('ps', b)])
                kind, c = fc // 4, fc % 4
                if kind == 2:
                    vdst = vT[:, c, sl, lc]
                    p.op('act', lambda e, b=b, vdst=vdst: e.activation(out=vdst, in_=cx.ps[b][:, :TT], func=AF.Copy),
                         reads=[('ps', b)], writes=[('vT', c, sl, tt)])
                    continue
                qs, qst = qsb.next()
                sq, sqt = qsq.next()
                p.op('act', lambda e, b=b, qs=qs: e.activation(out=qs[:], in_=cx.ps[b][:, :TT], func=AF.Copy),
                     reads=[('ps', b)], writes=[qst])
                p.op('act', lambda e, b=b, sq=sq: e.activation(out=sq[:], in_=cx.ps[b][:, :TT], func=AF.Square),
                     reads=[('ps', b)], writes=[sqt])
                b2 = cx.bank()
                p.op('pe', lambda e, b2=b2, sq=sq: e.matmul(cx.ps[b2][:, :TT], lhsT=C['blockones'][:], rhs=sq[:], start=True, stop=True),
                     reads=[sqt, CT('blockones')], writes=[('ps', b2)])
                ln, lnt = qln.next()
                rs, rst = qrs.next()
                p.op('act', lambda e, b2=b2, ln=ln: e.activation(out=ln[:], in_=cx.ps[b2][:, :TT], func=AF.Ln,
                                                                  bias=cx.eps[:, 0:1], scale=1.0 / 64),
                     reads=[('ps', b2), 'eps_t'], writes=[lnt])
                p.op('act', lambda e, ln=ln, rs=rs: e.activation(out=rs[:], in_=ln[:], func=AF.Exp, scale=-0.5),
                     reads=[lnt], writes=[rst])
                if kind == 0:
                    dst, dtok = qT[:, c, lc], ('qT1', c, tt)
                else:
                    dst, dtok = kT[:, c, sl, lc], ('kT1', c, sl, tt)
                p.op('dve', lambda e, qs=qs, rs=rs, dst=dst, kind=kind: e.scalar_tensor_tensor(
                    out=dst, in0=qs[:], scalar=qkg[:, kind:kind + 1], in1=rs[:], op0=ALU.mult, op1=ALU.mult),
                    reads=[qst, rst, ('m1s', 'qk_gain')], writes=[dtok])

        def colslice(start, r):
            return slice(start, start + 127 * r + 1, r)

        def toks(name, c, slot, start, r):
            lo = start // TT
            hi = (start + 127 * r) // TT
            return [(name, c, slot, j) for j in range(lo, hi + 1)]

        for pr in range(4):
            for r in (1, 4, 16):
                nblk = ST // (128 * r)
                if r == 1:
                    quads = [[(0, m) for m in range(q4 * 4, q4 * 4 + 4)] for q4 in range(4)]
                elif r == 4:
                    quads = [[(c, m) for c in range(4)] for m in range(4)]
                else:
                    quads = [[(c, 0) for c in range(q4 * 4, q4 * 4 + 4)] for q4 in range(4)]
                for qi, quad in enumerate(quads):
                    nb = cx.bank()
                    db = cx.bank()
                    for half in range(2):
                        units = quad[half * 2:half * 2 + 2]
                        info = []
                        for (c, m) in units:
                            own_start = r * 128 * m + c
                            if m > 0:
                                prev = (sl, r * 128 * (m - 1) + c)
                            elif M > 0:
                                prev = (1 - sl, ST - 128 * r + c)
                            else:
                                prev = None
                            info.append((own_start, prev))
                        pts = []
                        for hp in range(2):
                            rows = slice(64 * hp, 64 * hp + 64)
                            b = cx.bank()
                            for uj, (own_start, prev) in enumerate(info):
                                qcols = colslice(own_start, r)
                                qs_ = qT[rows, pr, qcols]
                                qtk = [('qT1', pr, j) for j in range(own_start // TT, (own_start + 127 * r) // TT + 1)]
                                if prev is not None:
                                    kp = kT[rows, pr, prev[0], colslice(prev[1], r)]
                                    p.op('pe', lambda e, b=b, uj=uj, kp=kp, qs_=qs_: e.matmul(
                                        cx.ps[b][:, (uj * 2) * 128:(uj * 2 + 1) * 128], lhsT=kp, rhs=qs_, start=True, stop=True),
                                        reads=toks('kT1', pr, prev[0], prev[1], r) + qtk, writes=[('ps', b)])
                                ko = kT[rows, pr, sl, qcols]
                                p.op('pe', lambda e, b=b, uj=uj, ko=ko, qs_=qs_: e.matmul(
                                    cx.ps[b][:, (uj * 2 + 1) * 128:(uj * 2 + 2) * 128], lhsT=ko, rhs=qs_, start=True, stop=True),
                                    reads=toks('kT1', pr, sl, own_start, r) + qtk, writes=[('ps', b)])
                            pr_, prt = praw.next()
                            p.op('act', lambda e, b=b, pr_=pr_: e.activation(out=pr_[:], in_=cx.ps[b][:, :], func=AF.Exp, scale=0.125),
                                 reads=[('ps', b)], writes=[prt])
                            pt_, ptt = pT[hp].next()
                            p.op('pool', lambda e, pr_=pr_, pt_=pt_: e.tensor_tensor(out=pt_[:], in0=pr_[:], in1=C['dilmask'][:], op=ALU.mult),
                                 reads=[prt, CT('dilmask')], writes=[ptt])
                            pts.append((pt_, ptt))
                        for uj, (own_start, prev) in enumerate(info):
                            u = half * 2 + uj
                            vtiles = {}
                            for kb, src in ((0, prev), (1, (sl, own_start))):
                                if src is None:
                                    continue
                                ck = (M, pr, r, src[0], src[1])
                                if ck in vcache and VE.i - vcache[ck][2] <= 12:
                                    vtiles[kb] = vcache[ck][:2]
                                    continue
                                boff = 0
                                psbf = psbf2[kb]
                                vsrc = vT[:, pr, src[0], colslice(src[1], r)]
                                p.op('pe', lambda e, vsrc=vsrc, boff=boff, psbf=psbf: e.transpose(out=psbf[:, boff:boff + 128],
                                                                                        in_=vsrc, identity=C['ident_b'][:]),
                                     reads=toks('vT', pr, src[0], src[1], r) + [CT('ident_b')], writes=[('psbf', kb)])
                                ve, vet = VE.next()
                                vo, vot = VO.next()
                                p.op('act', lambda e, ve=ve, boff=boff, psbf=psbf: e.activation(out=ve[:, 0:64], in_=psbf[:, boff:boff + 64], func=AF.Copy),
                                     reads=[('psbf', kb)], writes=[vet])
                                p.op('act', lambda e, vo=vo, boff=boff, psbf=psbf: e.activation(out=vo[:, 64:128], in_=psbf[:, boff + 64:boff + 128], func=AF.Copy),
                                     reads=[('psbf', kb)], writes=[vot])
                                vtiles[kb] = ((ve, vet), (vo, vot))
                                if kb == 1:
                                    vcache[ck] = ((ve, vet), (vo, vot), VE.i)
                            terms = [(hp, kb) for hp in range(2) for kb in range(2) if kb in vtiles]
                            for which, bank in ((0, nb), (1, db)):
                                for n_, (hp, kb) in enumerate(terms):
                                    if which == 0:
                                        vt_, vtt_ = vtiles[kb][hp]
                                        lhsT, lt = vt_[:], vtt_
                                    else:
                                        lhsT, lt = C['onespad'][:, hp * 128:(hp + 1) * 128], CT('onespad')
                                    rhs = pts[hp][0][:, (uj * 2 + kb) * 128:(uj * 2 + kb + 1) * 128]
                                    p.op('pe', lambda e, bank=bank, u=u, lhsT=lhsT, rhs=rhs, n_=n_, nn=len(terms): e.matmul(
                                        cx.ps[bank][:, u * 128:(u + 1) * 128], lhsT=lhsT, rhs=rhs, start=(n_ == 0), stop=(n_ == nn - 1)),
                                        reads=[lt, pts[hp][1]], writes=[('ps', bank)])
                    for acc, bank, an in ((accN, nb, 'accN'), (accD, db, 'accD')):
                        if r == 1:
                            va = acc[:, qi * 512:(qi + 1) * 512]
                            pv = cx.ps[bank][:, :]
                        elif r == 4:
                            va = acc[:, qi * 512:(qi + 1) * 512].rearrange("p (i c) -> p c i", c=4)
                            pv = cx.ps[bank][:, :].rearrange("p (c i) -> p c i", c=4)
                        else:
                            va = acc[:, :].rearrange("p (i c) -> p c i", c=16)[:, qi * 4:(qi + 1) * 4, :]
                            pv = cx.ps[bank][:, :].rearrange("p (c i) -> p c i", c=4)
                        if r == 1:
                            p.op('dve', lambda e, va=va, pv=pv: e.tensor_copy(out=va, in_=pv),
                                 reads=[('ps', bank)], writes=[an])
                        else:
                            p.op('dve', lambda e, va=va, pv=pv: e.tensor_tensor(out=va, in0=va, in1=pv, op=ALU.add),
                                 reads=[('ps', bank), an], writes=[an])
            p.op('dve', lambda e: e.reciprocal(out=accD[:], in_=accD[:]), reads=['accD'], writes=['accD'])
            p.op('dve', lambda e: e.tensor_tensor(out=oT[:], in0=accN[:], in1=accD[:], op=ALU.mult),
                 reads=['accN', 'accD'], writes=['oT1'])
            if out_pair is None:
                p.dma('pool', d['oT'][pr * 128:(pr + 1) * 128, M * ST:(M + 1) * ST], oT[:], reads=['oT1'], writes=['oT_out'], key='m1out')
            else:
                out_pair(M, pr, oT, ['oT1'])


def m1_host_inputs(inp, g):
    w = inp['c_w_qkv'][0]
    cols = np.concatenate([np.arange(k * 2048 + g * 512, k * 2048 + (g + 1) * 512) for k in range(3)])
    return dict(w_qkv=np.ascontiguousarray(w[:, cols]),
                qk_gain=np.ascontiguousarray(np.stack([np.tile(inp['c_q_norm'][0], 2), np.tile(inp['c_k_norm'][0], 2)], 1)).astype(np.float32))


M1_CONSTS = ['ident_b', 'dilmask', 'blockones', 'onespad']


def build_mixer1_program(S):
    nc = bass.Bass("TRN2", target_bir_lowering=False)
    d = {}
    d['hT'] = nc.dram_tensor("hT", [D_MODEL, S], BF16, kind="ExternalInput").ap()
    d['w_qkv'] = nc.dram_tensor("w_qkv", [D_MODEL, 1536], F32, kind="ExternalInput").ap()
    d['qk_gain'] = nc.dram_tensor("qk_gain", [128, 2], F32, kind="ExternalInput").ap()
    d['oT'] = nc.dram_tensor("oT", [512, S], BF16, kind="ExternalOutput").ap()
    p = Prog(nc)
    cx = Ctx(p)
    load_consts(cx, nc, M1_CONSTS)
    emit_mixer1(cx, S, d)
    p.emit(final_wait_keys=['m1out'])
    return nc


I32 = mybir.dt.int32
GROUPS = [[0, 1, 2, 3], [4, 5, 6, 7]]
ALL_CONSTS = ['ident_f', 'ident_b', 'U', 'negU', 'ones_f', 'negmask4', 'swamask', 'dilmask', 'blockones', 'onespad']


def build_fused_program(S, debug=False):
    TSEG = S // 4
    nt_all = S // 512
    ntseg = TSEG // 512
    nst = S // 2048
    nc = bass.Bass("TRN2", target_bir_lowering=False)
    ext = lambda n, shp, dt=F32: nc.dram_tensor(n, list(shp), dt, kind="ExternalInput").ap()
    itn = lambda n, shp, dt=BF16: nc.dram_tensor(n, list(shp), dt, kind="Internal").ap()
    d = {}
    d['xT'] = ext("xT", [D_MODEL, S])
    xTs = ext("xTs", [D_MODEL, TSEG])
    d['w_in'] = ext("w_in", [D_MODEL, M0_NCOL])
    d['g_mix'] = ext("g_mix", [D_MODEL])
    for n, shp in M0_SMALL.items():
        d[n] = ext(n, shp)
    rank = ext("rank", [1, 1], I32)
    w_out0 = ext("w_out0", [3072, D_MODEL])
    w_up0 = ext("w_up0", [D_MODEL, D_FF])
    w_dn0 = ext("w_dn0", [D_FF, D_MODEL])
    g_ffn0 = ext("g_ffn0", [D_MODEL])
    g_mix1 = ext("g_mix1", [D_MODEL])
    d1 = {}
    d1['w_qkv'] = ext("w_qkv", [D_MODEL, 1536])
    d1['qk_gain'] = ext("qk_gain1", [128, 2])
    w_out1 = ext("w_out1", [2048, D_MODEL])
    w_up1 = ext("w_up1", [D_MODEL, D_FF])
    w_dn1 = ext("w_dn1", [D_FF, D_MODEL])
    g_ffn1 = ext("g_ffn1", [D_MODEL])
    outT = nc.dram_tensor("outT", [D_MODEL, TSEG], F32, kind="ExternalOutput").ap()
    sc = []
    for l, EC in ((0, 24), (1, 16)):
        sc.append((itn(f"s_out{l}", [8, 128, EC * 256]), itn(f"s_up{l}", [NFB, 128, 16 * 512]),
                   itn(f"s_dn{l}", [16, 128, FC * 128])))
    mix0_loc = itn("mix0_loc", [nt_all, 768, 512])
    G0 = itn("G0", [nt_all, 4 * 768, 512])
    x1T = itn("x1T", [D_MODEL, TSEG], F32)
    h1_loc = itn("h1_loc", [ntseg, 2, 1024, 512])
    G1 = itn("G1", [ntseg, 2, 4 * 1024, 512])
    o_loc = itn("o_loc", [nst, 4, 128, 2048])
    G2 = itn("G2", [nst, 4, 4 * 128, 2048])

    if debug:
        dbg_G0 = nc.dram_tensor("dbg_G0", [4 * 768, 512], BF16, kind="ExternalOutput").ap()
        dbg_x1 = nc.dram_tensor("dbg_x1", [D_MODEL, TSEG], F32, kind="ExternalOutput").ap()
        dbg_G1 = nc.dram_tensor("dbg_G1", [4 * 1024, 512], BF16, kind="ExternalOutput").ap()
        dbg_G2 = nc.dram_tensor("dbg_G2", [4 * 128, 2048], BF16, kind="ExternalOutput").ap()
        dbg_mix = nc.dram_tensor("dbg_mix", [768, 512], BF16, kind="ExternalOutput").ap()
    p = Prog(nc)
    cx = Ctx(p)
    load_consts(cx, nc, ALL_CONSTS)

    def rankval(e, scale):
        k = (id(e), 'rank')
        if k not in p.regs:
            r = e.alloc_register(f"rk{p.phase}")
            e.reg_load(r, rank[0:1, 0:1])
            p.regs[k] = e.snap(e.snap(r, min_val=0, max_val=3) * scale, min_val=0, max_val=3 * scale)
        return p.regs[k]

    p.begin_phase()
    casts = (cast_dense_weight_ops(p, w_out0, w_up0, w_dn0, *sc[0], 24, 'L0') +
             cast_dense_weight_ops(p, w_out1, w_up1, w_dn1, *sc[1], 16, 'L1'))
    per = -(-len(casts) // max(1, nt_all - 2))

    def per_tile(t):
        for _ in range(per):
            if casts:
                casts.pop(0)()

    def out_tile0(t, mst, reads):
        p.dma('act', mix0_loc[t].rearrange("(c p) t -> p c t", p=128), mst[:], reads=reads,
              writes=[('mix0_loc', t)], key='m0out')
        p.cc("AllGather", mix0_loc[t], G0[t], GROUPS, reads=[('mix0_loc', t)], writes=[('G0', t)], key='cc0')

    emit_mixer0(cx, S, d, out_tile=out_tile0, per_tile=per_tile)
    while casts:
        casts.pop(0)()
    p.end_phase()

    p.begin_phase()
    if debug:
        p.dma('sp', dbg_G0, G0[0], writes=['dbg0'], key='dbg')
        p.dma('sp', dbg_mix, mix0_loc[0], writes=['dbg0m'], key='dbg')

    stg0 = itn("stg0", [2, 4 * 768, 512])

    def mix_load0(t, mt, tok):
        sslot = t % 2

        def fn(e):
            src = G0[t:][bass.ds(rankval(e, ntseg), 1)]
            return e.dma_start(out=stg0[sslot].rearrange("(a b) t -> a b t", a=128),
                               in_=src.rearrange("o (a b) t -> a (o b) t", a=128))
        p.op('sp', fn, writes=[('stg0', sslot)], dma_key='L0stg')
        toks_ = []
        for r in range(4):
            p.dma('sp', mt[:, r * 2:(r + 1) * 2, :],
                  stg0[sslot, r * 768:r * 768 + 256, :].rearrange("(c p) t -> p c t", p=128),
                  reads=[('stg0', sslot)], writes=[(tok, 'a', r)], key='L0min')
            p.dma('sp', mt[:, 8 + r * 4:8 + (r + 1) * 4, :],
                  stg0[sslot, r * 768 + 256:(r + 1) * 768, :].rearrange("(c p) t -> p c t", p=128),
                  reads=[('stg0', sslot)], writes=[(tok, 'y', r)], key='L0min')
            toks_ += [(tok, 'a', r), (tok, 'y', r)]
        return toks_

    def hn_store0(t, h2, reads):
        for hf in range(2):
            p.dma('act', h1_loc[t, hf].rearrange("(c p) t -> p c t", p=128), h2[:, hf * 8:(hf + 1) * 8, :], reads=reads,
                  writes=[('h1_loc', t, hf)], key='L0hout')
            p.cc("AllGather", h1_loc[t, hf], G1[t, hf], GROUPS, reads=[('h1_loc', t, hf)], writes=[('G1', t, hf)], key='cc1')

    emit_dense(cx, TSEG, 24, xTs, None, *sc[0], g_ffn0, x1T, 'L0', gain_next=g_mix1, mix_load=mix_load0,
               hn_store=hn_store0, aq='act')
    p.end_phase()

    p.begin_phase()
    if debug:
        p.dma('sp', dbg_x1, x1T, writes=['dbg1'], key='dbg')
        p.dma('sp', dbg_G1, G1[0, 0], writes=['dbg2'], key='dbg')

    def h_src(t, c4):
        r, lt = divmod(t, ntseg)
        return G1[lt, c4 // 2, r * 1024 + (c4 % 2) * 512: r * 1024 + (c4 % 2) * 512 + 512, :]

    def out_pair1(M, pr, oT, reads):
        p.dma('act', o_loc[M, pr], oT[:], reads=reads, writes=[('o_loc', M, pr)], key='m1out')
        p.cc("AllGather", o_loc[M, pr], G2[M, pr], GROUPS, reads=[('o_loc', M, pr)], writes=[('G2', M, pr)], key='cc2')

    emit_mixer1(cx, S, d1, h_src=h_src, out_pair=out_pair1)
    p.end_phase()

    p.begin_phase()
    if debug:
        p.dma('sp', dbg_G2, G2[0, 0], writes=['dbg3'], key='dbg')
    stseg = TSEG // 2048

    stg2 = itn("stg2", [2, 4, 512, 2048])

    def mix_load1(t, mt, tok):
        j = t // 4
        sslot = j % 2
        if t % 4 == 0:
            def fn(e):
                src = G2[j:][bass.ds(rankval(e, stseg), 1)]
                return e.dma_start(out=stg2[sslot].rearrange("q (a b) t -> a q (b t)", a=128),
                                   in_=src.rearrange("o q (a b) t -> a (o q) (b t)", a=128))
            p.op('sp', fn, writes=[('stg2', sslot)], dma_key='L1stg')
        for r in range(4):
            p.dma('sp', mt[:, r * 4:(r + 1) * 4, :],
                  stg2[sslot, :, r * 128:(r + 1) * 128, (t % 4) * 512:(t % 4 + 1) * 512].rearrange("q p t -> p q t"),
                  reads=[('stg2', sslot)], writes=[(tok, 'q', r)], key='L1min')
        return [(tok, 'q', r) for r in range(4)]

    emit_dense(cx, TSEG, 16, x1T, None, *sc[1], g_ffn1, outT, 'L1', mix_load=mix_load1, aq='act')
    p.end_phase()
    p.close()
    return nc


BATCH = 2
SEQ = 16384
NCORES = 8
_PROGS = {}


def _prog(name, fn, *a):
    key = (name,) + tuple(a)
    if key not in _PROGS:
        _PROGS[key] = fn(*a)
    return _PROGS[key]


def fused_in_maps(inp, S):
    TSEG = S // 4
    x = inp['x']
    xT = [np.ascontiguousarray(x[b].T) for b in range(x.shape[0])]
    cst = const_inputs(ALL_CONSTS)
    maps = []
    for c in range(NCORES):
        b, g = divmod(c, 4)
        m = m0_host_inputs(inp, b, g, S)
        m['xT'] = xT[b]
        m['xTs'] = np.ascontiguousarray(xT[b][:, g * TSEG:(g + 1) * TSEG])
        m['rank'] = np.array([[g]], np.int32)
        m1 = m1_host_inputs(inp, g)
        m['w_qkv'] = m1['w_qkv']
        m['qk_gain1'] = m1['qk_gain']
        m.update(w_out0=inp['ab_w_out'][0], w_up0=inp['w_up'][0], w_dn0=inp['w_down'][0], g_ffn0=inp['norm_ffn'][0],
                 g_mix1=inp['norm_mix'][1], w_out1=inp['c_w_o'][0], w_up1=inp['w_up'][1], w_dn1=inp['w_down'][1],
                 g_ffn1=inp['norm_ffn'][1])
        m.update(cst)
        maps.append(m)
    return maps


def kernel(**inp):
    inp = {k: np.asarray(v) for k, v in inp.items()}
    S = inp['x'].shape[1]
    TSEG = S // 4
    nc = _prog('fused', build_fused_program, S)
    res = run_bass_kernel_spmd(nc, fused_in_maps(inp, S), core_ids=list(range(NCORES))).results
    out = np.empty((inp['x'].shape[0], S, D_MODEL), np.float32)
    for c in range(NCORES):
        b, g = divmod(c, 4)
        out[b, g * TSEG:(g + 1) * TSEG, :] = np.asarray(res[c]['outT']).T
    return out
```
